# Optimizing a Trainium2 kernel written in Bass

```python
import math
import jax, jax.numpy as jnp
from jax import lax
import numpy as np

D_MODEL = 1024
BATCH = 4
SEQ = 4096
DEPTH = 4
DEC_BATCH = 128
DEC_SEQ = 1
PAST_LEN = 8192
PAGE_SIZE = 128

N_A_LAYERS = DEPTH // 2
N_B_LAYERS = DEPTH - N_A_LAYERS
HG_HEADS = 8
HG_DK = D_MODEL // HG_HEADS
HG_DV = D_MODEL // HG_HEADS
HG_CHUNK = 64
N_Q_HEADS = 16
N_KV_HEADS = 4
GROUP = N_Q_HEADS // N_KV_HEADS
HEAD_DIM = D_MODEL // N_Q_HEADS
WINDOW = 128
D_FF = 4 * D_MODEL
ROPE_THETA = 10000.0
EPS = 1e-6
ATTN_SCALE = 1.0 / math.sqrt(HEAD_DIM)

kernel_name = "yoco_hgrn2_swa_sink_decoder_step"

F32 = jnp.float32


def _rms_norm(x, g):
    xf = x.astype(F32)
    y = xf * lax.rsqrt(jnp.mean(xf * xf, axis=-1, keepdims=True) + EPS)
    return (y * g.astype(F32)).astype(x.dtype)


def _ada(c, w, b):
    return (jax.nn.silu(c) @ w + b)[:, None, :]


def _rope(x, pos):
    half = HEAD_DIM // 2
    inv = ROPE_THETA ** (-jnp.arange(half, dtype=F32) / half)
    ang = pos.astype(F32)[:, None] * inv[None, :]
    cos = jnp.cos(ang)[None, :, None, :]
    sin = jnp.sin(ang)[None, :, None, :]
    xf = x.astype(F32)
    x1, x2 = xf[..., :half], xf[..., half:]
    return jnp.concatenate([x1 * cos - x2 * sin, x2 * cos + x1 * sin], axis=-1).astype(x.dtype)


def _hgrn2_scan(q, k, v, logf, s0):
    b, L = q.shape[:2]
    C = HG_CHUNK if L % HG_CHUNK == 0 else L
    n = L // C

    def to_chunks(t):
        return t.astype(F32).reshape(b, n, C, HG_HEADS, t.shape[-1]).transpose(1, 0, 3, 2, 4)

    qc, kc, vc, gc = to_chunks(q), to_chunks(k), to_chunks(v), to_chunks(logf)
    causal = jnp.tril(jnp.ones((C, C), dtype=bool))[:, :, None]

    def step(S, inp):
        qi, ki, vi, gi = inp
        cum = jnp.cumsum(gi, axis=-2)
        diff = cum[..., :, None, :] - cum[..., None, :, :]
        decay = jnp.exp(jnp.where(causal, diff, -jnp.inf))
        att = jnp.einsum('bhtd,bhsd,bhtsd->bhts', qi, ki, decay)
        o = (jnp.einsum('bhts,bhsv->bhtv', att, vi)
             + jnp.einsum('bhtd,bhdv->bhtv', qi * jnp.exp(cum), S))
        last = cum[..., -1:, :]
        S_new = (jnp.exp(last[..., 0, :])[..., None] * S
                 + jnp.einsum('bhsd,bhsv->bhdv', ki * jnp.exp(last - cum), vi))
        return S_new, o

    S_fin, o = lax.scan(step, s0.astype(F32), (qc, kc, vc, gc))
    o = o.transpose(1, 0, 3, 2, 4).reshape(b, L, HG_HEADS, HG_DV)
    return o, S_fin


def _hgrn2(h, s0, w_in, w_out, lb, gn_g):
    b, L, _ = h.shape
    q, f, i, g = jnp.split(h @ w_in, 4, axis=-1)
    q = jax.nn.silu(q)
    fg = lb + (1.0 - lb) * jax.nn.sigmoid(f.astype(F32))
    heads = lambda t: t.reshape(b, L, HG_HEADS, -1)
    o, S = _hgrn2_scan(heads(q), heads(1.0 - fg), heads(i), heads(jnp.log(fg)), s0)
    o = _rms_norm(o, gn_g) * jax.nn.silu(heads(g).astype(F32))
    return o.reshape(b, L, D_MODEL).astype(h.dtype) @ w_out, S


def _sink_softmax(s, sinks):
    sk = jnp.broadcast_to(sinks.astype(F32).reshape(N_KV_HEADS, GROUP, 1, 1), s.shape[:-1] + (1,))
    p = jax.nn.softmax(jnp.concatenate([s, sk], axis=-1), axis=-1)
    return p[..., :-1]


def _swa_prompt(q, k, v, sinks):
    b, L = q.shape[:2]
    nb = L // WINDOW
    qb = q.reshape(b, nb, WINDOW, N_KV_HEADS, GROUP, HEAD_DIM).astype(F32)

    def band(t):
        tp = jnp.concatenate([jnp.zeros_like(t[:, :WINDOW]), t], axis=1)
        tb = tp.reshape(b, nb + 1, WINDOW, N_KV_HEADS, HEAD_DIM)
        return jnp.concatenate([tb[:, :-1], tb[:, 1:]], axis=2).astype(F32)

    kb, vb = band(k), band(v)
    s = jnp.einsum('bnqkgd,bnjkd->bnkgqj', qb, kb) * ATTN_SCALE
    qi = jnp.arange(WINDOW)[:, None] + WINDOW
    kj = jnp.arange(2 * WINDOW)[None, :]
    rel = qi - kj
    valid = (rel >= 0) & (rel < WINDOW)
    not_first = (jnp.arange(nb) > 0)[:, None, None]
    valid = valid[None] & (not_first | (kj >= WINDOW)[None])
    s = jnp.where(valid[None, :, None, None], s, -jnp.inf)
    p = _sink_softmax(s, sinks)
    o = jnp.einsum('bnkgqj,bnjkd->bnqkgd', p, vb)
    return o.reshape(b, L, N_Q_HEADS * HEAD_DIM)


def _swa_sample(q, k_all, v_all, sinks):
    b, T = q.shape[:2]
    qpos = PAST_LEN + jnp.arange(T)
    kpos = PAST_LEN - WINDOW + jnp.arange(WINDOW + T)
    rel = qpos[:, None] - kpos[None, :]
    valid = (rel >= 0) & (rel < WINDOW)
    qg = q.reshape(b, T, N_KV_HEADS, GROUP, HEAD_DIM).astype(F32)
    s = jnp.einsum('bqkgd,bjkd->bkgqj', qg, k_all.astype(F32)) * ATTN_SCALE
    s = jnp.where(valid, s, -jnp.inf)
    p = _sink_softmax(s, sinks)
    o = jnp.einsum('bkgqj,bjkd->bqkgd', p, v_all.astype(F32))
    return o.reshape(b, T, N_Q_HEADS * HEAD_DIM)


def _trunk(x, c, pos, hg_s0, cache_k, cache_v, P):
    (w_ada, b_ada, norm1_g, norm2_g, hg_w_in, hg_w_out, hg_lb, hg_gn_g,
     kv_w_ada, kv_b_ada, kv_norm_g, w_kv, k_norm_g,
     w_q, q_norm_g, sinks, w_o, w_up, w_down) = P
    b, L, _ = x.shape
    hg_states = []
    k_all = v_all = k_state = v_state = None
    for l in range(DEPTH):
        sh1, sc1, g1, sh2, sc2, g2 = jnp.split(_ada(c, w_ada[l], b_ada[l]), 6, axis=-1)
        if l == N_A_LAYERS:
            sh, sc = jnp.split(_ada(c, kv_w_ada, kv_b_ada), 2, axis=-1)
            hk = _rms_norm(x, kv_norm_g) * (1 + sc) + sh
            k_new, v_new = jnp.split(hk @ w_kv, 2, axis=-1)
            k_new = _rope(_rms_norm(k_new.reshape(b, L, N_KV_HEADS, HEAD_DIM), k_norm_g), pos)
            v_new = v_new.reshape(b, L, N_KV_HEADS, HEAD_DIM)
            if cache_k is None:
                k_all, v_all = k_new, v_new
            else:
                k_all = jnp.concatenate([cache_k.astype(k_new.dtype), k_new], axis=1)
                v_all = jnp.concatenate([cache_v.astype(v_new.dtype), v_new], axis=1)
            k_state, v_state = k_all[:, -WINDOW:], v_all[:, -WINDOW:]
        h = _rms_norm(x, norm1_g[l]) * (1 + sc1) + sh1
        if l < N_A_LAYERS:
            mix, S = _hgrn2(h, hg_s0[l], hg_w_in[l], hg_w_out[l], hg_lb[l], hg_gn_g[l])
            hg_states.append(S.astype(x.dtype))
        else:
            j = l - N_A_LAYERS
            q = _rope(_rms_norm((h @ w_q[j]).reshape(b, L, N_Q_HEADS, HEAD_DIM), q_norm_g[j]), pos)
            if cache_k is None:
                att = _swa_prompt(q, k_all, v_all, sinks[j])
            else:
                att = _swa_sample(q, k_all, v_all, sinks[j])
            mix = att.astype(x.dtype) @ w_o[j]
        x = x + g1 * mix
        h2 = _rms_norm(x, norm2_g[l]) * (1 + sc2) + sh2
        x = x + g2 * (jnp.square(jax.nn.relu(h2 @ w_up[l])) @ w_down[l])
    return x, jnp.stack(hg_states), k_state, v_state


def setup_inputs(seed: int = 0) -> dict:
    key = jax.random.key(seed)
    ks = jax.random.split(key, 32)
    n = lambda k, shape, s: jax.random.normal(k, shape, dtype=F32) * s
    D = D_MODEL
    return {
        "x_prompt": n(ks[0], (BATCH, SEQ, D), 1.0),
        "x_sample": n(ks[1], (DEC_BATCH, DEC_SEQ, D), 1.0),
        "c_prompt": n(ks[2], (BATCH, D), 1.0),
        "c_sample": n(ks[3], (DEC_BATCH, D), 1.0),
        "state_hgrn": n(ks[4], (N_A_LAYERS, DEC_BATCH, HG_HEADS, HG_DK, HG_DV), 0.3),
        "cache_k": n(ks[5], (DEC_BATCH, WINDOW, N_KV_HEADS, HEAD_DIM), 1.0),
        "cache_v": n(ks[6], (DEC_BATCH, WINDOW, N_KV_HEADS, HEAD_DIM), 1.0),
        "w_ada": n(ks[7], (DEPTH, D, 6 * D), 0.5 * D ** -0.5),
        "b_ada": n(ks[8], (DEPTH, 6 * D), 0.02),
        "norm1_g": 1.0 + n(ks[9], (DEPTH, D), 0.02),
        "norm2_g": 1.0 + n(ks[10], (DEPTH, D), 0.02),
        "hg_w_in": n(ks[11], (N_A_LAYERS, D, 4 * D), D ** -0.5),
        "hg_w_out": n(ks[12], (N_A_LAYERS, D, D), D ** -0.5),
        "hg_lower_bounds": n(ks[13], (N_A_LAYERS, D), 0.1),
        "hg_gn_g": 1.0 + n(ks[14], (N_A_LAYERS, HG_DV), 0.02),
        "kv_w_ada": n(ks[15], (D, 2 * D), 0.5 * D ** -0.5),
        "kv_b_ada": n(ks[16], (2 * D,), 0.02),
        "kv_norm_g": 1.0 + n(ks[17], (D,), 0.02),
        "w_kv": n(ks[18], (D, 2 * N_KV_HEADS * HEAD_DIM), D ** -0.5),
        "k_norm_g": 1.0 + n(ks[19], (HEAD_DIM,), 0.02),
        "w_q": n(ks[20], (N_B_LAYERS, D, N_Q_HEADS * HEAD_DIM), D ** -0.5),
        "q_norm_g": 1.0 + n(ks[21], (N_B_LAYERS, HEAD_DIM), 0.02),
        "sinks": n(ks[22], (N_B_LAYERS, N_Q_HEADS), 0.5),
        "w_o": n(ks[23], (N_B_LAYERS, N_Q_HEADS * HEAD_DIM, D), (N_Q_HEADS * HEAD_DIM) ** -0.5),
        "w_up": n(ks[24], (DEPTH, D, D_FF), D ** -0.5),
        "w_down": n(ks[25], (DEPTH, D_FF, D), D_FF ** -0.5),
    }


def reference(x_prompt, x_sample, c_prompt, c_sample, state_hgrn, cache_k, cache_v,
              w_ada, b_ada, norm1_g, norm2_g, hg_w_in, hg_w_out, hg_lower_bounds, hg_gn_g,
              kv_w_ada, kv_b_ada, kv_norm_g, w_kv, k_norm_g,
              w_q, q_norm_g, sinks, w_o, w_up, w_down):
    sm = jax.nn.softmax(hg_lower_bounds.astype(F32), axis=0)
    hg_lb = jnp.cumsum(sm, axis=0) - sm[0]
    P = (w_ada, b_ada, norm1_g, norm2_g, hg_w_in, hg_w_out, hg_lb, hg_gn_g,
         kv_w_ada, kv_b_ada, kv_norm_g, w_kv, k_norm_g,
         w_q, q_norm_g, sinks, w_o, w_up, w_down)
    bp, Lp, _ = x_prompt.shape
    s0_prompt = jnp.zeros((N_A_LAYERS, bp, HG_HEADS, HG_DK, HG_DV), dtype=x_prompt.dtype)
    y_prompt, hg_p, k_p, v_p = _trunk(x_prompt, c_prompt, jnp.arange(Lp), s0_prompt,
                                      None, None, P)
    Ls = x_sample.shape[1]
    y_sample, hg_s, k_s, v_s = _trunk(x_sample, c_sample, PAST_LEN + jnp.arange(Ls), state_hgrn,
                                      cache_k, cache_v, P)
    return (y_prompt, y_sample, hg_p, k_p, v_p, hg_s, k_s, v_s)
```

```python
import math
from contextlib import ExitStack

import numpy as np
import concourse.bass as bass
import concourse.mybir as mybir
from concourse.bass_utils import run_bass_kernel_spmd

F32 = mybir.dt.float32
BF16 = mybir.dt.bfloat16
ALU = mybir.AluOpType
AF = mybir.ActivationFunctionType

D = 1024
SEQ = 4096
NB = 4
NSAMP = 128
NS = 16
WINDOW = 128
PAST = 8192
EPS = 1e-6
SCALE = 1.0 / 8.0

SAME_ENGINE_SYNC = True


class Op:
    __slots__ = ("id", "eng", "fn", "deps", "is_dma", "sem_key", "n_dma", "waits", "inc_amt",
                 "needs_inc", "lidx", "count", "dma_val", "vc", "final")


class Sched:
    ENGS = ("pe", "act", "dve", "pool", "sp")

    def __init__(self, nc):
        self.nc = nc
        self.ops = []
        self.by_eng = {e: [] for e in self.ENGS}
        self.last_w = {}
        self.readers = {}
        self.dma_count = {}
        self.out_dmas = []
        self.bulk = set()

    def _track(self, op, reads, writes):
        pr = [k for k in reads if isinstance(k, tuple) and k and k[0] == "ps"]
        if pr:
            reads = [k for k in reads if k not in pr]
            writes = list(writes) + pr
        deps = set()
        for k in reads:
            w = self.last_w.get(k)
            if w is not None:
                deps.add(w)
        for k in writes:
            w = self.last_w.get(k)
            if w is not None:
                deps.add(w)
            for r in self.readers.get(k, ()):
                deps.add(r)
        for k in reads:
            self.readers.setdefault(k, []).append(op.id)
        for k in writes:
            self.last_w[k] = op.id
            self.readers[k] = []
        deps.discard(op.id)
        op.deps = deps

    def op(self, eng, name, *args, reads=(), writes=(), **kw):
        def fn(e, name=name, args=args, kw=kw):
            return getattr(e, name)(*args, **kw)
        return self.add(eng, fn, reads, writes)

    def dmas(self, queue, pairs, reads=(), writes=(), sem_key=None, final=False, bulk=False):
        pairs = list(pairs)

        def fn(e, pairs=pairs):
            return [e.dma_start(out=o, in_=i) for (o, i) in pairs]
        return self.dma(queue, fn, reads, writes, sem_key=sem_key, n=len(pairs), final=final, bulk=bulk)

    def add(self, eng, fn, reads=(), writes=()):
        op = Op()
        op.id = len(self.ops)
        op.eng = eng
        op.fn = fn
        op.is_dma = False
        op.sem_key = None
        op.n_dma = 0
        op.needs_inc = False
        op.final = False
        self._track(op, reads, writes)
        self.ops.append(op)
        self.by_eng[eng].append(op)
        return op

    def dma(self, queue, fn, reads=(), writes=(), sem_key=None, n=1, final=False, inc=16, bulk=False):
        op = Op()
        op.id = len(self.ops)
        op.eng = queue
        op.fn = fn
        op.is_dma = True
        op.sem_key = sem_key
        op.n_dma = n
        op.inc_amt = inc
        op.needs_inc = True
        op.final = final
        if bulk:
            self.bulk.add(sem_key)
        self.dma_count[sem_key] = self.dma_count.get(sem_key, 0) + inc * n
        op.dma_val = self.dma_count[sem_key]
        self._track(op, reads, writes)
        self.ops.append(op)
        self.by_eng[queue].append(op)
        if final:
            self.out_dmas.append(op)
        return op

    def finalize(self):
        for op in self.ops:
            if op.is_dma and op.sem_key in self.bulk:
                op.dma_val = self.dma_count[op.sem_key]
        lcount = {e: 0 for e in self.ENGS}
        for op in self.ops:
            if not op.is_dma:
                lcount[op.eng] += 1
                op.lidx = lcount[op.eng]
        evc = {e: {} for e in self.ENGS}
        for op in self.ops:
            E = op.eng
            my = evc[E]
            waits = []
            for d in sorted(op.deps):
                dop = self.ops[d]
                if dop.is_dma:
                    key = ("D", dop.sem_key)
                    val = dop.dma_val
                else:
                    if dop.eng == E and (E == "pe" or not SAME_ENGINE_SYNC) and not op.is_dma:
                        continue
                    key = ("E", dop.eng)
                    val = dop.lidx
                if my.get(key, 0) >= val:
                    continue
                waits.append(d)
                dop.needs_inc = True
                for k, v in dop.vc.items():
                    if my.get(k, 0) < v:
                        my[k] = v
            op.waits = waits
            vc = dict(my)
            if op.is_dma:
                k = ("D", op.sem_key)
                vc[k] = max(vc.get(k, 0), op.dma_val)
            else:
                vc[("E", E)] = op.lidx
            op.vc = vc
        self.final_waits = {}
        for op in self.out_dmas:
            self.final_waits[op.sem_key] = max(self.final_waits.get(op.sem_key, 0), op.dma_val)
        cnt = {e: 0 for e in self.ENGS}
        for op in self.ops:
            if not op.is_dma:
                if op.needs_inc:
                    cnt[op.eng] += 1
                op.count = cnt[op.eng]
        self.max_counts = cnt

    def emit(self):
        nc = self.nc
        self.finalize()
        with ExitStack() as st:
            esem = {e: st.enter_context(nc.semaphore("es_" + e)) for e in ("pe", "act", "dve", "pool")}
            dsem = {}
            for i, k in enumerate(self.dma_count):
                dsem[k] = st.enter_context(nc.semaphore("ds_%d" % i))
            block = st.enter_context(nc.Block())
            ops = self.ops

            def run(eng_name, eng):
                for op in self.by_eng[eng_name]:
                    wmap = {}
                    for d in op.waits:
                        dop = ops[d]
                        if dop.is_dma:
                            s = dsem[dop.sem_key]
                            v = dop.dma_val
                        else:
                            s = esem[dop.eng]
                            v = dop.count
                        key = id(s)
                        if key not in wmap or wmap[key][1] < v:
                            wmap[key] = (s, v)
                    for s, v in wmap.values():
                        eng.wait_ge(s, v)
                    r = op.fn(eng)
                    if op.is_dma:
                        assert len(r) == op.n_dma, (len(r), op.n_dma)
                        for ins in r:
                            ins.then_inc(dsem[op.sem_key], op.inc_amt)
                    elif op.needs_inc:
                        r.then_inc(esem[eng_name], 1)
                if eng_name == "sp":
                    for k, v in self.final_waits.items():
                        eng.wait_ge(dsem[k], v)

            @block.tensor
            def _(e):
                run("pe", e)

            @block.scalar
            def _(e):
                run("act", e)

            @block.vector
            def _(e):
                run("dve", e)

            @block.gpsimd
            def _(e):
                run("pool", e)

            @block.sync
            def _(e):
                run("sp", e)


C_ID, C_MR, C_AM, C_M4, C_OB, C_PM, C_JM, C_FL, C_RM, C_N = 0, 128, 640, 1152, 1664, 1792, 1920, 1921, 1922, 1923


def host_consts(flag):
    c = np.zeros((128, C_N), np.float32)
    c[:, C_ID:C_ID + 128] = np.eye(128, dtype=np.float32)
    mr = np.ones((512,), np.float32)
    mr[::64] = 0.0
    c[:, C_MR:C_MR + 512] = mr[None, :]
    s = np.arange(64)[:, None]
    t = np.arange(64)[None, :]
    am = ((s <= t) & ((s // 32) == (t // 32))).astype(np.float32)
    c[0:64, C_AM:C_AM + 512] = np.tile(am, (1, 8))
    c[0:32, C_RM] = 1.0
    j = np.arange(128)[:, None]
    q = np.arange(128)[None, :]
    mprev = (j > q).astype(np.float32)
    mcur = (j <= q).astype(np.float32)
    c[:, C_M4:C_M4 + 512] = np.concatenate([mprev, mprev, mcur, mcur], axis=1)
    ob = np.zeros((128, 128), np.float32)
    ob[0:64, 0:64] = 1.0
    ob[64:128, 64:128] = 1.0
    c[:, C_OB:C_OB + 128] = ob
    pm = np.zeros((128, 128), np.float32)
    for p in range(128):
        d = p % 64
        partner = p + 32 if d < 32 else p - 32
        pm[p, partner] = 1.0
    c[:, C_PM:C_PM + 128] = pm
    c[:, C_JM] = 1.0
    c[0, C_JM] = 0.0
    c[:, C_FL] = flag
    return c


def rope_tables(pos):
    half = 32
    inv = (np.float32(10000.0) ** (-np.arange(half, dtype=np.float32) / np.float32(half))).astype(np.float32)
    ang = pos.astype(np.float32)[None, :] * inv[:, None]
    cos = np.cos(ang).astype(np.float32)
    sin = np.sin(ang).astype(np.float32)
    ct = np.concatenate([cos, cos, cos, cos], axis=0)
    st = np.concatenate([-sin, sin, -sin, sin], axis=0)
    return np.ascontiguousarray(ct), np.ascontiguousarray(st)


def build(T, do_sample=True, dbg=False):
    NT = T // 512
    NBLK = T // 128
    nc = bass.Bass("TRN2", target_bir_lowering=False)

    def din(name, shape, dt=F32):
        return nc.dram_tensor(name, list(shape), dt, kind="ExternalInput").ap()

    def dout(name, shape, dt=F32):
        return nc.dram_tensor(name, list(shape), dt, kind="ExternalOutput").ap()

    def dint(name, shape, dt=F32):
        return nc.dram_tensor(name, list(shape), dt, kind="Internal").ap()

    xp = din("xp", [T, D])
    c17 = din("c17", [17, D])
    xs = din("xs", [NS, D])
    st_in = din("st_in", [2, NS, 8, 128, 128])
    ck = din("ck", [NS, 128, 256])
    cv = din("cv", [NS, 128, 256])
    w_ada = din("w_ada", [4, D, 6 * D])
    b_ada = din("b_ada", [4, 6 * D])
    norm1_g = din("norm1_g", [4, D])
    norm2_g = din("norm2_g", [4, D])
    hg_w_in = din("hg_w_in", [2, 8, D, 512])
    hg_w_out = din("hg_w_out", [2, D, D])
    hg_lb = din("hg_lower_bounds", [2, D])
    hg_gn = din("hg_gn_g", [2, 128])
    kv_w_ada = din("kv_w_ada", [D, 2 * D])
    kv_b_ada = din("kv_b_ada", [2 * D])
    kv_norm_g = din("kv_norm_g", [D])
    w_kv = din("w_kv", [D, 512])
    k_norm_g = din("k_norm_g", [64])
    w_q = din("w_q", [2, D, D])
    q_norm_g = din("q_norm_g", [2, 64])
    sinks = din("sinks", [2, 16])
    w_o = din("w_o", [2, D, D])
    w_up = din("w_up", [4, D, 4 * D])
    w_down = din("w_down", [4, 4 * D, D])
    consts = din("consts", [128, C_N])
    costab = din("costab", [128, T])
    sintab = din("sintab", [128, T])
    cs16 = din("cs16", [128, 32])

    yp = dout("yp", [T, D])
    ys = dout("ys", [NS, D])
    sp_state = dout("sp_state", [2, 8, 128, 128])
    kp = dout("kp", [128, 256])
    vp = dout("vp", [128, 256])
    ss = dout("ss", [2, NS, 8, 128, 128])
    ks = dout("ks", [NS, 128, 256])
    vs = dout("vs", [NS, 128, 256])

    DBG = {}
    if dbg:
        for nm in ("mix", "o", "qq", "kk", "e3", "att", "cum", "rs"):
            DBG[nm] = dout("dbg_" + nm, [128, 8, 512] if nm == "mix" else [128, 512])
    xs_scr = [dint("xs_scr%d" % i, [128, 1024]) for i in range(2)]
    xg_scr = [dint("xg_scr%d" % i, [256, 1024]) for i in range(2)]
    kvs_scr = dint("kvs_scr", [(T // 512) * 8, 64, 2048], BF16)
    kv_scr = dint("kv_scr", [128, 512], BF16)
    kvg_scr = dint("kvg_scr", [256, 512], BF16)
    PAIRS = [[0, 1], [2, 3], [4, 5], [6, 7]]

    with ExitStack() as st:
        st.enter_context(nc.allow_low_precision("bf16 matmul operands, fp32 accumulation"))
        st.enter_context(nc.allow_non_contiguous_dma("small strided parameter loads"))

        def sb(name, shape, dt=F32):
            return st.enter_context(nc.sbuf_tensor(name, list(shape), dt))

        S = Sched(nc)
        banks = [st.enter_context(nc.psum_tensor("bank%d" % i, [128, 512], F32)) for i in range(8)]
        bctr = [0]
        psn = [6]

        def PS():
            i = bctr[0] % psn[0]
            bctr[0] = (i + 1) % psn[0]
            return banks[i], ("ps", i)

        lctr = [0]

        def PL():
            i = 6 + lctr[0] % 2
            lctr[0] += 1
            return banks[i], ("ps", i)

        NF = 9
        fts = [sb("ft%d" % i, [128, 512]) for i in range(NF)]
        fctr = [0]

        def FT():
            i = fctr[0]
            fctr[0] = (i + 1) % NF
            return fts[i], ("ft", i)

        NBT = 8
        bts = [sb("bt%d" % i, [128, 512], BF16) for i in range(NBT)]
        bbctr = [0]

        def BT():
            i = bbctr[0]
            bbctr[0] = (i + 1) % NBT
            return bts[i], ("bt", i)

        xT = sb("xT", [128, 8, T])
        cst = sb("cst", [128, C_N])
        NSLOT = 4
        arena = sb("arena", [128, NSLOT * 4096], BF16)

        def slot(i, n=1):
            return arena[:, i * 4096:(i + n) * 4096], [("W", i + r) for r in range(n)]

        R32 = sb("R32", [128, 16384], BF16)
        hTall = R32[:, :].rearrange("p (k t) -> p k t", k=8) if T == 2048 else None
        if T != 2048:
            hTall = R32[:, 0:8 * T].rearrange("p (k t) -> p k t", k=8)
        MIXK = [("mix", c_) for c_ in range(8)]
        if T >= 1024:
            mixT = hTall[:, :, 512:1024]
            MIXW = [("hT", 1)]
        else:
            mixT = sb("mixT", [128, 8, 512], BF16)[:]
            MIXW = []
        tabc = sb("tabc", [128, 512])
        tabs = sb("tabs", [128, 512])
        identb = sb("identb", [128, 128], BF16)
        onesD = sb("onesD", [128, 128])
        ones128 = sb("ones128", [128, 128])
        onesb = sb("onesb", [128, 128], BF16)
        onesDb = sb("onesDb", [128, 128], BF16)
        ones128b = sb("ones128b", [128, 128], BF16)
        onesblkb = sb("onesblkb", [128, 128], BF16)
        permb = sb("permb", [128, 128], BF16)
        epsc = sb("epsc", [128, 1])
        epsl = sb("epsl", [128, 1])
        vrows = sb("vrows", [128, 128])
        vcols = sb("vcols", [128, 128])
        cT = sb("cT", [128, 8, 17])
        cTb = sb("cTb", [128, 8, 17], BF16)
        modT = sb("modT", [128, 48, 17])
        kvmodT = sb("kvmodT", [128, 16, 17])
        aT = sb("aT", [128, 3, 8, 17])
        lbAB = sb("lbAB", [128, 2, 2, 8])
        modP = sb("modP", [128, 4, 48])
        aP = sb("aP", [128, 4, 2, 8])
        kvmodP = sb("kvmodP", [128, 16])
        aKVP = sb("aKVP", [128, 8])
        lbt = sb("lbt", [128, 2, 8])
        esink = sb("esink", [128, 2, 8])
        mask4f = sb("mask4f", [128, 512])
        ones17 = sb("ones17", [1, 17])
        rd = sb("rd", [128, 128])
        KVW = max(2 * (T + 128) + (NBLK + 1) * 256, 8704)
        KVR = sb("KVR", [128, KVW], BF16)
        KT = KVR[:, 0:2 * (T + 128)].rearrange("p (m t) -> p m t", m=2)
        Vtok = KVR[:, 2 * (T + 128):2 * (T + 128) + (NBLK + 1) * 256].rearrange("p (b c) -> p b c", c=256)
        kdtok = KVR[0:64, 0:1024].rearrange("p (c d) -> p c d", c=8)
        vtok = KVR[0:64, 1024:2048].rearrange("p (c d) -> p c d", c=8)
        attsb = KVR[0:64, 2048:2560]
        Sbf = KVR[:, 2560:3584].rearrange("p (h v) -> p h v", h=8)
        Sst = KVR[:, 3584:5632].bitcast(F32).rearrange("p (h v) -> p h v", h=8)
        klast = sb("klast", [128, 2, 128])
        kt1 = sb("kt1", [64, 2048], BF16)
        KTS = [KVR[0:64, 0:2048], kt1[:]]
        p1x = sb("p1x", [128, 512])
        P1F = [KVR[:, 5632:6656].bitcast(F32), KVR[:, 6656:7680].bitcast(F32), KVR[:, 7680:8704].bitcast(F32), p1x[:]]
        p1b = sb("p1b", [128, 4, 512], BF16)

        ident = cst[:, C_ID:C_ID + 128]
        maskreset = cst[:, C_MR:C_MR + 512]
        attmask = cst[0:64, C_AM:C_AM + 512]
        mask4 = cst[:, C_M4:C_M4 + 512]
        onesblk = cst[:, C_OB:C_OB + 128]
        perm = cst[:, C_PM:C_PM + 128]
        jmask = cst[:, C_JM:C_JM + 1]
        flag = cst[:, C_FL:C_FL + 1]
        rowmask = cst[0:64, C_RM:C_RM + 1]
        cbs = sb("cbs", [128, 8, 2])

        V_N1, V_N2, V_KVN, V_LB, V_GN, V_QN, V_KN = 0, 32, 64, 72, 88, 90, 92
        SK = [("S", h) for h in range(8)]

        S.dmas("sp", [(cst[:], consts[:, :])], writes=["cst"], sem_key="ld", bulk=True)
        S.op("dve", "memset", vrows[:], 0.0, writes=["vrows"])
        prs = [(vrows[V_N1:V_N1 + 32, :], norm1_g.rearrange("l (k p) -> (l k) p", p=128)),
               (vrows[V_N2:V_N2 + 32, :], norm2_g.rearrange("l (k p) -> (l k) p", p=128)),
               (vrows[V_KVN:V_KVN + 8, :], kv_norm_g.rearrange("(k p) -> k p", p=128)),
               (vrows[V_LB:V_LB + 16, :], hg_lb.rearrange("l (k p) -> (l k) p", p=128)),
               (vrows[V_GN:V_GN + 2, :], hg_gn[:, :])]
        for l in range(2):
            for a in range(2):
                prs.append((vrows[V_QN + l:V_QN + l + 1, a * 64:(a + 1) * 64], q_norm_g[l:l + 1, :]))
        for a in range(2):
            prs.append((vrows[V_KN:V_KN + 1, a * 64:(a + 1) * 64], k_norm_g.rearrange("(o d) -> o d", o=1)))
        S.dmas("sp", prs, writes=["vrows"], sem_key="ld2", bulk=True)
        prs = []
        for l in range(2):
            sv = sinks[l].rearrange("(m a i) -> a m i", m=2, a=2, i=4)
            for a in range(2):
                prs.append((esink[a * 64:(a + 1) * 64, l, :].rearrange("p (m i) -> p m i", m=2), sv[a:a + 1].to_broadcast([64, 2, 4])))
        S.dmas("sp", prs, writes=["esink"], sem_key="ld", bulk=True)
        S.op("act", "activation", esink[:], esink[:], AF.Exp, reads=["esink"], writes=["esink"])

        S.op("dve", "memset", cbs[:], 0.0, writes=["cbs"])
        S.op("dve", "memset", onesD[:], 1.0 / D, writes=["onesD"])
        S.op("dve", "memset", ones128[:], 1.0 / 128, writes=["ones128"])
        S.op("dve", "memset", onesb[:], 1.0, writes=["onesb"])
        S.op("dve", "memset", onesDb[:], 1.0 / D, writes=["onesDb"])
        S.op("dve", "memset", ones128b[:], 1.0 / 128, writes=["ones128b"])
        S.op("dve", "tensor_copy", onesblkb[:], cst[:, C_OB:C_OB + 128], reads=["cst"], writes=["onesblkb"])
        S.op("dve", "tensor_copy", permb[:], cst[:, C_PM:C_PM + 128], reads=["cst"], writes=["permb"])
        S.op("dve", "memset", epsc[:], EPS, writes=["epsc"])
        S.op("dve", "memset", epsl[:], 1e-7, writes=["epsl"])
        S.op("dve", "memset", ones17[:], 1.0, writes=["ones17"])
        S.op("dve", "tensor_copy", identb[:], ident, reads=["cst"], writes=["identb"])
        S.op("dve", "tensor_scalar", mask4f[:, 0:256], mask4[:, 0:256], flag, None, ALU.mult, reads=["cst"], writes=["mask4f"])
        S.op("dve", "tensor_copy", mask4f[:, 256:512], mask4[:, 256:512], reads=["cst"], writes=["mask4f"])

        pb, pk = PS()
        S.op("pe", "transpose", pb[:, 0:128], vrows[:], ident, reads=["vrows", "cst"], writes=[pk])
        S.op("dve", "tensor_copy", vcols[:], pb[:, 0:128], reads=[pk], writes=["vcols"])
        S.op("dve", "tensor_tensor", lbt[:, 1, :], vcols[:, V_LB:V_LB + 8], vcols[:, V_LB + 8:V_LB + 16], ALU.subtract, reads=["vcols"], writes=["lbt"])
        S.op("act", "activation", lbt[:, 1, :], lbt[:, 1, :], AF.Exp, reads=["lbt"], writes=["lbt"])
        S.op("dve", "tensor_scalar", lbt[:, 1, :], lbt[:, 1, :], 1.0, None, ALU.add, reads=["lbt"], writes=["lbt"])
        S.op("dve", "reciprocal", lbt[:, 1, :], lbt[:, 1, :], reads=["lbt"], writes=["lbt"])
        S.op("dve", "tensor_scalar", lbt[:, 0, :], lbt[:, 1, :], -1.0, 1.0, ALU.mult, ALU.add, reads=["lbt"], writes=["lbt"])
        S.op("dve", "tensor_tensor", lbt[:, 0, :], lbt[:, 0, :], lbt[:, 0, :], ALU.subtract, reads=["lbt"], writes=["lbt"])
        for l in range(2):
            S.op("dve", "tensor_scalar", lbAB[:, l, 0, :], lbt[:, l, :], -0.5, 0.5, ALU.mult, ALU.add, reads=["lbt"], writes=["lbAB"])
            S.op("dve", "tensor_tensor", lbAB[:, l, 1, :], lbAB[:, l, 0, :], lbt[:, l, :], ALU.add, reads=["lbt", "lbAB"], writes=["lbAB"])

        c0, c0k = FT()
        c1, c1k = FT()
        S.dmas("sp", [(c0[0:17, :], c17[:, 0:512]), (c1[0:17, :], c17[:, 512:1024])], writes=[c0k, c1k], sem_key="ld", bulk=True)
        S.op("act", "activation", c0[0:17, :], c0[0:17, :], AF.Silu, reads=[c0k], writes=[c0k])
        S.op("act", "activation", c1[0:17, :], c1[0:17, :], AF.Silu, reads=[c1k], writes=[c1k])
        pb, pk = PS()
        for k in range(8):
            src = (c0 if k < 4 else c1)[0:17, (k % 4) * 128:(k % 4 + 1) * 128]
            S.op("pe", "transpose", pb[:, k * 17:(k + 1) * 17], src, ident[0:17, 0:17], reads=[c0k, c1k, "cst"], writes=[pk])
        S.op("dve", "tensor_copy", cT[:].rearrange("p k s -> p (k s)"), pb[:, 0:136], reads=[pk], writes=["cT"])
        S.op("dve", "tensor_copy", cTb[:], cT[:], reads=["cT"], writes=["cTb"])

        def load_x():
          for b in range(NBLK):
            t = b // 4
            xa, xak = FT()
            xb_, xbk = FT()
            S.dmas("sp", [(xa[:], xp[b * 128:(b + 1) * 128, 0:512]), (xb_[:], xp[b * 128:(b + 1) * 128, 512:1024])],
                   writes=[xak, xbk], sem_key=xak)
            for g in range(2):
                src, srck = (xa, xak) if g == 0 else (xb_, xbk)
                pb, pk = PS()
                for kk in range(4):
                    S.op("pe", "transpose", pb[:, kk * 128:(kk + 1) * 128], src[:, kk * 128:(kk + 1) * 128], ident, reads=[srck, "cst"], writes=[pk])
                dst = xT[:, g * 4:(g + 1) * 4, b * 128:(b + 1) * 128]
                wr = [("xT", g * 4 + kk, t) for kk in range(4)] + ["SMPREGION"]
                if g == 0:
                    S.op("dve", "tensor_copy", dst, pb[:].rearrange("p (k t) -> p k t", k=4), reads=[pk], writes=wr)
                else:
                    S.op("act", "activation", dst, pb[:].rearrange("p (k t) -> p k t", k=4), AF.Copy, reads=[pk], writes=wr)

        actr = [0]

        def ada(wsrc, bsrc, ncolt, dst, dkey):
            for j in range(ncolt):
                wsl, wk = slot(actr[0] % 4)
                actr[0] += 1
                wv = wsl.rearrange("p (k c) -> p k c", k=8)
                S.dmas("pool", [(wv, wsrc.rearrange("(k p) c -> p k c", p=128)[:, :, j * 512:(j + 1) * 512])], writes=wk, sem_key=wk[0])
                br, bk = FT()
                S.dmas("sp", [(br[0:1, :], bsrc[j * 512:(j + 1) * 512].rearrange("(o c) -> o c", o=1))], writes=[bk], sem_key=bk)
                pb, pk = PS()
                for fc in range(4):
                    for k in range(8):
                        S.op("pe", "matmul", pb[:, fc * 17:(fc + 1) * 17], wv[:, k, fc * 128:(fc + 1) * 128], cTb[:, k, :], start=(k == 0), stop=False,
                             reads=wk + ["cTb"], writes=[pk])
                    S.op("pe", "matmul", pb[:, fc * 17:(fc + 1) * 17], br[0:1, fc * 128:(fc + 1) * 128], ones17[:], start=False, stop=True,
                         reads=[bk, "ones17"], writes=[pk])
                S.op("dve", "tensor_copy", dst[:, j * 4:(j + 1) * 4, :].rearrange("p c s -> p (c s)"), pb[:, 0:68], reads=[pk], writes=[dkey])

        def mod_derive(l):
            for k in range(8):
                S.op("dve", "tensor_scalar", aT[:, 0, k, :], modT[:, 8 + k, :], 1.0, vcols[:, V_N1 + l * 8 + k:V_N1 + l * 8 + k + 1], ALU.add, ALU.mult,
                     reads=["modT", "vcols"], writes=["aT0"])
                S.op("dve", "tensor_scalar", aT[:, 1, k, :], modT[:, 32 + k, :], 1.0, vcols[:, V_N2 + l * 8 + k:V_N2 + l * 8 + k + 1], ALU.add, ALU.mult,
                     reads=["modT", "vcols"], writes=["aT1"])

        def norm_mod(t, acol, shcol, akey, skey, hslot=None):
            ts = slice(t * 512, (t + 1) * 512)
            if hslot is None:
                hslot = t
            hsl = slice(hslot * 512, (hslot + 1) * 512)
            hw = [("hT", hslot)] + (MIXK if (hslot == 1 and T >= 1024) else [])
            pb, pk = PS()
            for k in range(8):
                sq, sk = BT()
                S.op("act", "activation", sq[:], xT[:, k, ts], AF.Square, reads=[("xT", k, t)], writes=[sk])
                S.op("pe", "matmul", pb[:], onesDb[:], sq[:], start=(k == 0), stop=(k == 7), reads=[sk, "onesDb"], writes=[pk])
            rs, rk = FT()
            S.op("act", "activation", rs[:], pb[:], AF.Ln, bias=epsc[:, 0:1], scale=1.0, reads=[pk, "epsc"], writes=[rk])
            S.op("act", "activation", rs[:], rs[:], AF.Exp, scale=-0.5, reads=[rk], writes=[rk])
            for k in range(8):
                tm, tk = FT()
                S.op("dve", "scalar_tensor_tensor", tm[:], xT[:, k, ts], acol(k), rs[:], ALU.mult, ALU.mult, reads=[("xT", k, t), rk, akey], writes=[tk])
                S.op("act", "activation", hTall[:, k, hsl], tm[:], AF.Identity, bias=shcol(k), scale=1.0, reads=[tk, skey], writes=hw)

        def resid_update(t, fc, pb, pk, gcol, gkey):
            ts = slice(t * 512, (t + 1) * 512)
            S.op("dve", "scalar_tensor_tensor", xT[:, fc, ts], pb[:], gcol, xT[:, fc, ts], ALU.mult, ALU.add,
                 reads=[pk, ("xT", fc, t), gkey], writes=[("xT", fc, t)])

        dbgb = {}

        def dump(nm, ap, key, rows=128):
            if not dbg:
                return
            if nm not in dbgb:
                dbgb[nm] = sb("dbgb_" + nm, [128, 512])
            f, fk = dbgb[nm], ("dbgb", nm)
            S.op("dve", "tensor_copy", f[0:rows, :], ap, reads=[key], writes=[fk])
            S.dmas("sp", [(DBG[nm][0:rows, :], f[0:rows, :])], reads=[fk], sem_key=("dbg", nm), final=True)

        def load_w(pairs, keys):
            S.dmas("pool", pairs, writes=keys, sem_key=keys[0])

        def hgrn_stage(l, t, h, wh, wkeys, state_only, par):
            ts = slice(t * 512, (t + 1) * 512)
            A = lbAB[:, l, 0, h:h + 1]
            B = lbAB[:, l, 1, h:h + 1]
            hk = [("hT", 0)]
            X = {}
            kts = KTS[par]
            kdtok_ = kts[:, 0:1024].rearrange("p (c d) -> p c d", c=8)
            vtok_ = kts[:, 1024:2048].rearrange("p (c d) -> p c d", c=8)
            ktk = ("kts", par)
            tf, tfk = P1F[2 * par], ("p1f", 2 * par)
            tq, tqk = P1F[2 * par + 1], ("p1f", 2 * par + 1)
            iTb, ibk = p1b[:, 2 * par, :], ("p1b", 2 * par)
            tg, tgk = p1b[:, 2 * par + 1, :], ("p1b", 2 * par + 1)

            def proj(j):
                pb, pk = PS()
                for k in range(8):
                    S.op("pe", "matmul", pb[:], wh[:, k, j, :], hTall[:, k, 0:512], start=(k == 0), stop=(k == 7), reads=wkeys + hk, writes=[pk])
                return pb, pk

            def P1a():
                X["pf"] = proj(1)
                if state_only:
                    X["pi"] = proj(2)
                else:
                    S.dmas("sp", [(kts, kvs_scr[t * 8 + h])], reads=[("kvs", t, h)], writes=[ktk], sem_key=("kvsl", par))
                    X["pq"] = proj(0)
                    X["pg"] = proj(3)

            def P1b():
                pf, pfk = X["pf"]
                S.op("act", "activation", tf, pf[:], AF.Tanh, scale=0.5, reads=[pfk], writes=[tfk])
                if state_only:
                    pi, pik = X["pi"]
                    S.op("act", "activation", iTb, pi[:], AF.Copy, reads=[pik], writes=[ibk])
                if not state_only:
                    pq, pqk = X["pq"]
                    pg, pgk = X["pg"]
                    S.op("act", "activation", tq, pq[:], AF.Silu, reads=[pqk], writes=[tqk])
                    S.op("act", "activation", tg, pg[:], AF.Silu, reads=[pgk], writes=[tgk])

            def P2a():
                S.op("dve", "tensor_scalar", tf, tf, A, B, ALU.mult, ALU.add, reads=[tfk, "lbAB"], writes=[tfk])
                tl, tlk = FT()
                S.op("act", "activation", tl[:], tf, AF.Ln, bias=epsl[:, 0:1], scale=1.0, reads=[tfk, "epsl"], writes=[tlk])
                tk_, tkk = FT()
                S.op("act", "activation", tk_[:], tf, AF.Identity, bias=1.0, scale=-1.0, reads=[tfk], writes=[tkk])
                cum, cumk = FT()
                S.op("dve", "tensor_tensor_scan", cum[:], maskreset, tl[:], 0.0, ALU.mult, ALU.add, reads=["cst", tlk], writes=[cumk])
                cv3 = cum[:].rearrange("p (c j) -> p c j", j=64)
                e3, e3k = FT()
                S.op("act", "activation", e3[:], cum[:], AF.Exp, reads=[cumk], writes=[e3k])
                X.update(e3=(e3, e3k))
                if state_only:
                    d2, d2k = FT()
                    S.op("pool", "tensor_tensor", d2[:].rearrange("p (c j) -> p c j", j=64), cv3, cv3[:, :, 63:64].to_broadcast([128, 8, 64]), ALU.subtract,
                         reads=[cumk], writes=[d2k])
                    S.op("act", "activation", d2[:], d2[:], AF.Exp, scale=-1.0, reads=[d2k], writes=[d2k])
                    kd, kdk = BT()
                    S.op("dve", "tensor_tensor", kd[:], tk_[:], d2[:], ALU.mult, reads=[tkk, d2k], writes=[kdk])
                    X.update(kd=(kd, kdk))
                if not state_only:
                    S.op("pool", "tensor_copy", cbs[:, :, 1:2], cv3[:, :, 31:32], reads=[cumk], writes=["cbs"])
                    d1, d1k = FT()
                    S.op("dve", "tensor_tensor", d1[:].rearrange("p (c b j) -> p c b j", b=2, j=32), cum[:].rearrange("p (c b j) -> p c b j", b=2, j=32),
                         cbs[:].unsqueeze(3).to_broadcast([128, 8, 2, 32]), ALU.subtract, reads=[cumk, "cbs"], writes=[d1k])
                    e1, e1k = FT()
                    S.op("act", "activation", e1[:], d1[:], AF.Exp, reads=[d1k], writes=[e1k])
                    S.op("act", "activation", d1[:], d1[:], AF.Exp, scale=-1.0, reads=[d1k], writes=[d1k])
                    qq, qqk = BT()
                    S.op("dve", "tensor_tensor", qq[:], tq, e1[:], ALU.mult, reads=[tqk, e1k], writes=[qqk])
                    kk_, kkk = BT()
                    S.op("dve", "scalar_tensor_tensor", kk_[:], d1[:], 1e30, tk_[:], ALU.min, ALU.mult, reads=[tkk, d1k], writes=[kkk])
                    S.op("pool", "tensor_tensor", tl[:].rearrange("p (c j) -> p c j", j=64), cv3, cv3[:, :, 31:32].to_broadcast([128, 8, 64]), ALU.subtract,
                         reads=[cumk], writes=[tlk])
                    S.op("act", "activation", tl[:], tl[:], AF.Exp, scale=-1.0, reads=[tlk], writes=[tlk])
                    k32, k32k = BT()
                    S.op("dve", "scalar_tensor_tensor", k32[:], tl[:], 1.0, tk_[:], ALU.min, ALU.mult, reads=[tkk, tlk], writes=[k32k])
                    qe, qek = BT()
                    S.op("pool", "tensor_tensor", qe[:], tq, e3[:], ALU.mult, reads=[tqk, e3k], writes=[qek])
                    X.update(qq=(qq, qqk), kk=(kk_, kkk), k32=(k32, k32k), qe=(qe, qek), d1=(d1, d1k))

            def P2b():
                e3, e3k = X["e3"]
                if state_only:
                    kd, kdk = X["kd"]
                    pkd, pkdk = PS()
                    pv, pvk = PS()
                    pkd_b = pkd[:].bitcast(BF16)
                    pv_b = pv[:].bitcast(BF16)
                    for c in range(8):
                        S.op("pe", "transpose", pkd_b[0:64, c * 128:(c + 1) * 128], kd[:, c * 64:(c + 1) * 64], identb[:], reads=[kdk, "identb"], writes=[pkdk])
                    for c in range(8):
                        S.op("pe", "transpose", pv_b[0:64, c * 128:(c + 1) * 128], iTb[:, c * 64:(c + 1) * 64], identb[:], reads=[ibk, "identb"], writes=[pvk])
                    S.op("act", "activation", kts[:, 0:1024], pkd_b[0:64, :], AF.Copy, reads=[pkdk], writes=[ktk])
                    S.op("dve", "tensor_copy", kts[:, 1024:2048], pv_b[0:64, :], reads=[pvk, ktk], writes=[ktk])
                    S.dmas("sp", [(kvs_scr[t * 8 + h], kts)], reads=[ktk], writes=[("kvs", t, h)], sem_key=("kvsw", par))
                if not state_only:
                    qq, qqk = X["qq"]
                    kk_, kkk = X["kk"]
                    k32, k32k = X["k32"]
                    qe, qek = X["qe"]
                    patt, pattk = PS()
                    patt2, patt2k = PS()
                    for c in range(8):
                        cs = slice(c * 64, (c + 1) * 64)
                        S.op("pe", "matmul", patt[0:64, cs], kk_[:, cs], qq[:, cs], start=True, stop=True, reads=[kkk, qqk], writes=[pattk])
                    for c in range(8):
                        cs = slice(c * 64, (c + 1) * 64)
                        S.op("pe", "matmul", patt2[0:64, cs], k32[:, cs], qq[:, cs], start=True, stop=True, reads=[k32k, qqk], writes=[patt2k])
                    S.op("dve", "tensor_tensor", attsb, patt[0:64, :], attmask, ALU.mult, reads=[pattk, "cst"], writes=["attsb"])
                    au = attsb.rearrange("p (c b j) -> p c b j", b=2, j=32)[:, :, 1, :]
                    pu = patt2[0:64, :].rearrange("p (c b j) -> p c b j", b=2, j=32)[:, :, 1, :]
                    S.op("dve", "scalar_tensor_tensor", au, pu, rowmask, au, ALU.mult, ALU.add, reads=[patt2k, "cst", "attsb"], writes=["attsb"])
                    po, pok = PL()
                skey = ("S", h)
                sbkey = ("Sbf", h)
                psSs = []
                for c in range(8):
                    bS = banks[4 + c // 4]
                    psSs.append((bS[:, (c % 4) * 128:(c % 4 + 1) * 128], ("ps", 4 + c // 4)))
                    S.op("pe", "matmul", psSs[c][0], kdtok_[:, c, :], vtok_[:, c, :], start=True, stop=True, reads=[ktk], writes=[psSs[c][1]])
                if not state_only:
                    for c in range(8):
                        cs = slice(c * 64, (c + 1) * 64)
                        S.op("pe", "matmul", po[:, cs], vtok_[:, c, :], attsb[:, cs], start=(c == 0), stop=False, skip_group_check=True,
                             reads=[ktk, "attsb"], writes=[pok])
                for c in range(8):
                    cs = slice(c * 64, (c + 1) * 64)
                    if not state_only:
                        S.op("pe", "matmul", po[:, cs], Sbf[:, h, :], qe[:, cs], start=False, stop=True, skip_group_check=True, reads=[sbkey, qek], writes=[pok])
                    S.op("dve", "scalar_tensor_tensor", Sst[:, h, :], Sst[:, h, :], e3[:, c * 64 + 63:c * 64 + 64], psSs[c][0], ALU.mult, ALU.add,
                         reads=[psSs[c][1], e3k, skey], writes=[skey])
                    if not state_only:
                        S.op("act", "activation", Sbf[:, h, :], Sst[:, h, :], AF.Copy, reads=[skey], writes=[sbkey])
                if state_only:
                    return
                d1, d1k = X["d1"]
                osq, osk = BT()
                S.op("act", "activation", osq[:], po[:], AF.Square, reads=[pok], writes=[osk])
                pn, pnk = PS()
                S.op("pe", "matmul", pn[:], ones128b[:], osq[:], start=True, stop=True, reads=[osk, "ones128b"], writes=[pnk])
                rs, rk = d1, d1k
                S.op("act", "activation", rs[:], pn[:], AF.Ln, bias=epsc[:, 0:1], scale=1.0, reads=[pnk, "epsc"], writes=[rk])
                S.op("act", "activation", rs[:], rs[:], AF.Exp, scale=-0.5, reads=[rk], writes=[rk])
                S.op("dve", "tensor_tensor", rs[:], po[:], rs[:], ALU.mult, reads=[pok, rk], writes=[rk])
                S.op("dve", "scalar_tensor_tensor", mixT[:, h, :], rs[:], vcols[:, V_GN + l:V_GN + l + 1], tg, ALU.mult, ALU.mult,
                     reads=[rk, tgk, "vcols"], writes=[("mix", h)] + MIXW)
            return P1a, P1b, P2a, P2b

        hctr = [0]

        def hgrn_pass(l, state_only, wout=None, wokeys=None):
            psn[0] = 4
            bctr[0] = 0
            _hgrn_pass(l, state_only, wout, wokeys)
            psn[0] = 6

        def _hgrn_pass(l, state_only, wout=None, wokeys=None):
            loaded = {}

            def issue(t, h):
                if t >= NT or (t, h) in loaded:
                    return
                wsl, wkeys = slot(2 + hctr[0] % 2)
                hctr[0] += 1
                wh = wsl.rearrange("p (k j c) -> p k j c", k=8, j=4)
                load_w([(wsl.rearrange("p (k c) -> p k c", k=8), hg_w_in[l, h].rearrange("(k p) c -> p k c", p=128))], wkeys)
                loaded[(t, h)] = (wh, wkeys)
            issue(0, 0)
            for t in range(NT):
                norm_mod(t, lambda k: aT[:, 0, k, 0:1], lambda k: modT[:, k, 0:1], "aT0", "modT", hslot=0)
                stages = {}
                wh, wkeys = loaded[(t, 0)]
                stages[0] = hgrn_stage(l, t, 0, wh, wkeys, state_only, 0)
                issue(t, 1)
                stages[0][0]()
                stages[0][1]()
                for h in range(8):
                    if h + 1 < 8:
                        wh, wkeys = loaded[(t, h + 1)]
                        stages[h + 1] = hgrn_stage(l, t, h + 1, wh, wkeys, state_only, (h + 1) % 2)
                        stages[h + 1][0]()
                    stages[h][2]()
                    if h + 1 < 8:
                        stages[h + 1][1]()
                        if h + 2 < 8:
                            issue(t, h + 2)
                        else:
                            issue(t + 1, 0)
                    stages[h][3]()
                if not state_only:
                    if dbg and l == 0 and t == 0:
                        pass
                    for fc in range(8):
                        pb, pk = PS()
                        for k in range(8):
                            S.op("pe", "matmul", pb[:], wout[:, k, fc * 128:(fc + 1) * 128], mixT[:, k, :], start=(k == 0), stop=(k == 7),
                                 reads=wokeys + [("mix", k)], writes=[pk])
                        resid_update(t, fc, pb, pk, modT[:, 16 + fc, 0:1], "modT")

        def exchange_state(l):
            S.dmas("sp", [(xs_scr[l][:, :], Sst.rearrange("p h v -> p (h v)"))], reads=SK, writes=[("xs", l)], sem_key=("xs", l))
            S.dma("pool", lambda e, l=l: [e.collective_compute("AllGather", ALU.bypass, replica_groups=PAIRS, ins=[xs_scr[l][:, :]], outs=[xg_scr[l][:, :]])],
                  reads=[("xs", l)], writes=[("xg", l)], sem_key=("cc", l), inc=1)
            S.dmas("sp", [(Sst.rearrange("p h v -> p (h v)"), xg_scr[l][0:128, :])], reads=[("xg", l)], writes=SK, sem_key=("xgl", l))
            for h in range(8):
                S.op("dve", "tensor_scalar", Sst[:, h, :], Sst[:, h, :], flag, None, ALU.mult, reads=[("S", h), "cst"], writes=[("S", h)])
                S.op("act", "activation", Sbf[:, h, :], Sst[:, h, :], AF.Copy, reads=[("S", h)], writes=[("Sbf", h)])

        def hgrn_layer(l):
            S.op("dve", "memset", Sst, 0.0, writes=SK)
            hgrn_pass(l, True)
            exchange_state(l)
            wsl, wokeys = slot(0, 2)
            wout = wsl.rearrange("p (k c) -> p k c", k=8)
            load_w([(wout, hg_w_out[l].rearrange("(k p) c -> p k c", p=128))], wokeys)
            hgrn_pass(l, False, wout, wokeys)
            S.dmas("sp", [(sp_state[l].rearrange("h k v -> k h v"), Sst)], reads=SK, sem_key=("spst", l), final=True)

        mctr = [0]

        def mlp(l):
            for t in range(NT):
                norm_mod(t, lambda k: aT[:, 1, k, 0:1], lambda k: modT[:, 24 + k, 0:1], "aT1", "modT")
            mloaded = {}

            def missue(g):
                if g >= 8 or g in mloaded:
                    return
                wsl, wk = slot(2 * (mctr[0] % 2), 2)
                mctr[0] += 1
                wup = wsl[:, 0:4096].rearrange("p (k c) -> p k c", k=8)
                wdn = wsl[:, 4096:8192].rearrange("p (k c) -> p k c", k=4)
                load_w([(wup, w_up[l].rearrange("(k p) c -> p k c", p=128)[:, :, g * 512:(g + 1) * 512]),
                        (wdn, w_down[l][g * 512:(g + 1) * 512, :].rearrange("(k p) c -> p k c", p=128))], wk)
                mloaded[g] = (wup, wdn, wk)
            missue(0)
            for hg in range(8):
                missue(hg + 1)
                wup, wdn, wk = mloaded[hg]
                for t in range(NT):
                    ts = slice(t * 512, (t + 1) * 512)
                    hids = []
                    for hc in range(4):
                        pb, pk = PS()
                        for k in range(8):
                            S.op("pe", "matmul", pb[:], wup[:, k, hc * 128:(hc + 1) * 128], hTall[:, k, ts], start=(k == 0), stop=(k == 7),
                                 reads=wk + [("hT", t)], writes=[pk])
                        r, rk = FT()
                        S.op("act", "activation", r[:], pb[:], AF.Relu, reads=[pk], writes=[rk])
                        hd, hdk = BT()
                        S.op("pool", "tensor_tensor", hd[:], r[:], r[:], ALU.mult, reads=[rk], writes=[hdk])
                        hids.append((hd, hdk))
                    for fc in range(8):
                        pb, pk = PS()
                        for hc in range(4):
                            hd, hdk = hids[hc]
                            S.op("pe", "matmul", pb[:], wdn[:, hc, fc * 128:(fc + 1) * 128], hd[:], start=(hc == 0), stop=(hc == 3),
                                 reads=wk + [hdk], writes=[pk])
                        resid_update(t, fc, pb, pk, modT[:, 40 + fc, 0:1], "modT")

        def load_tabs(t):
            S.dmas("sp", [(tabc[:], costab[:, t * 512:(t + 1) * 512]), (tabs[:], sintab[:, t * 512:(t + 1) * 512])], writes=["tab"], sem_key="tab")

        def qk_norm_rope(pb, pk, gcol, out_ap, out_keys, f32_out=None, f32_key=None):
            sq, sk = BT()
            S.op("act", "activation", sq[:], pb[:], AF.Square, reads=[pk], writes=[sk])
            pn, pnk = PS()
            S.op("pe", "matmul", pn[:], onesblkb[:], sq[:], start=True, stop=True, reads=[sk, "onesblkb"], writes=[pnk])
            rs, rk = FT()
            S.op("act", "activation", rs[:], pn[:], AF.Ln, bias=epsc[:, 0:1], scale=1.0 / 64, reads=[pnk, "epsc"], writes=[rk])
            S.op("act", "activation", rs[:], rs[:], AF.Exp, scale=-0.5, reads=[rk], writes=[rk])
            qg, qgk = BT()
            S.op("dve", "tensor_scalar", qg[:], pb[:], gcol, None, ALU.mult, reads=[pk, "vcols"], writes=[qgk])
            pp, ppk = PS()
            S.op("pe", "matmul", pp[:], permb[:], qg[:], start=True, stop=True, reads=[qgk, "permb"], writes=[ppk])
            ta, tak = FT()
            S.op("pool", "tensor_tensor", ta[:], qg[:], tabc[:], ALU.mult, reads=[qgk, "tab"], writes=[tak])
            tb, tbk = FT()
            S.op("dve", "tensor_tensor", tb[:], pp[:], tabs[:], ALU.mult, reads=[ppk, "tab"], writes=[tbk])
            S.op("pool", "tensor_tensor", ta[:], ta[:], tb[:], ALU.add, reads=[tak, tbk], writes=[tak])
            if f32_out is not None:
                S.op("dve", "tensor_tensor", f32_out, ta[:], rs[:], ALU.mult, reads=[tak, rk], writes=[f32_key])
                S.op("act", "activation", out_ap, f32_out, AF.Copy, reads=[f32_key], writes=out_keys)
            else:
                S.op("dve", "tensor_tensor", out_ap, ta[:], rs[:], ALU.mult, reads=[tak, rk], writes=out_keys)

        def kv_phase():
            if do_sample:
                S.op("dve", "tensor_copy", kvmodT[:, :, 0], kvmodP[:], reads=["kvmodP"], writes=["kvmodT"])
                S.op("dve", "tensor_copy", aT[:, 2, :, 0], aKVP[:], reads=["aKVP"], writes=["aT2"])
            else:
                ada(kv_w_ada, kv_b_ada, 4, kvmodT, "kvmodT")
                for k in range(8):
                    S.op("dve", "tensor_scalar", aT[:, 2, k, :], kvmodT[:, 8 + k, :], 1.0, vcols[:, V_KVN + k:V_KVN + k + 1], ALU.add, ALU.mult,
                         reads=["kvmodT", "vcols"], writes=["aT2"])
            wsl, wkk = slot(0)
            wkv = wsl.rearrange("p (k c) -> p k c", k=8)
            load_w([(wkv, w_kv.rearrange("(k p) c -> p k c", p=128))], wkk)
            gk = vcols[:, V_KN:V_KN + 1]
            ALIAS = SK + [("Sbf", h) for h in range(8)] + ["kdtok", "vtok", "attsb", ("kts", 0)] + [("p1f", i_) for i_ in range(3)]
            for t in range(NT):
                ts = slice(t * 512, (t + 1) * 512)
                norm_mod(t, lambda k: aT[:, 2, k, 0:1], lambda k: kvmodT[:, k, 0:1], "aT2", "kvmodT", hslot=0)
                load_tabs(t)
                for m in range(2):
                    pb, pk = PS()
                    for k in range(8):
                        S.op("pe", "matmul", pb[:], wkv[:, k, m * 128:(m + 1) * 128], hTall[:, k, 0:512], start=(k == 0), stop=(k == 7),
                             reads=wkk + [("hT", 0)], writes=[pk])
                    dst = KT[:, m, 128 + t * 512:128 + (t + 1) * 512]
                    if t == NT - 1:
                        kf, kfk = FT()
                        qk_norm_rope(pb, pk, gk, dst, [("KT", m, t)] + ALIAS, f32_out=kf[:], f32_key=kfk)
                        S.op("pool", "tensor_copy", klast[:, m, :], kf[:, 384:512], reads=[kfk], writes=[("klast", m)])
                    else:
                        qk_norm_rope(pb, pk, gk, dst, [("KT", m, t)] + ALIAS)
                for b in range(4):
                    blk = t * 4 + b
                    pb, pk = PS()
                    for k in range(8):
                        S.op("pe", "matmul", pb[:, 0:256], hTall[:, k, b * 128:(b + 1) * 128], wkv[:, k, 256:512],
                             start=(k == 0), stop=(k == 7), reads=wkk + [("hT", 0)], writes=[pk])
                    S.op("act", "activation", Vtok[:, blk + 1, :], pb[:, 0:256], AF.Copy, reads=[pk], writes=[("V", blk + 1)] + ALIAS)
                    if blk == NBLK - 1:
                        vf, vfk = FT()
                        S.op("dve", "tensor_copy", vf[:, 0:256], pb[:, 0:256], reads=[pk], writes=[vfk])
                        S.dmas("sp", [(vp[:, :], vf[:, 0:256])], reads=[vfk], sem_key="vp", final=True)
            ko, kok = FT()
            pb, pk = PS()
            for m in range(2):
                S.op("pe", "transpose", pb[:, m * 128:(m + 1) * 128], klast[:, m, :], ident, reads=[("klast", m), "cst"], writes=[pk])
            S.op("dve", "tensor_copy", ko[:, 0:256], pb[:, 0:256], reads=[pk], writes=[kok])
            S.dmas("sp", [(kp[:, :], ko[:, 0:256])], reads=[kok], sem_key="kp", final=True)
            tlast = NT - 1
            S.dmas("sp", [(kv_scr[:, 0:256].rearrange("p (m t) -> p m t", m=2), KT[:, :, T:T + 128]), (kv_scr[:, 256:512], Vtok[:, NBLK, :])],
                   reads=[("KT", 0, tlast), ("KT", 1, tlast), ("V", NBLK)], writes=["kvscr"], sem_key="kvscr")
            S.dma("pool", lambda e: [e.collective_compute("AllGather", ALU.bypass, replica_groups=PAIRS, ins=[kv_scr[:, :]], outs=[kvg_scr[:, :]])],
                  reads=["kvscr"], writes=["kvg"], sem_key="cckv", inc=1)
            S.dmas("sp", [(KT[:, :, 0:128], kvg_scr[0:128, 0:256].rearrange("p (m t) -> p m t", m=2)), (Vtok[:, 0, :], kvg_scr[0:128, 256:512])],
                   reads=["kvg"], writes=[("KTpre",), ("V", 0)], sem_key="kvgl")

        def attn_layer(l):
            j = l - 2
            wsl, wqk = slot(0, 2)
            wq = wsl.rearrange("p (k c) -> p k c", k=8)
            load_w([(wq, w_q[j].rearrange("(k p) c -> p k c", p=128))], wqk)
            wsl2, wok = slot(2, 2)
            wo = wsl2.rearrange("p (c f) -> p c f", c=8)
            load_w([(wo, w_o[j].rearrange("(c p) f -> p c f", p=128))], wok)
            gq = vcols[:, V_QN + j:V_QN + j + 1]
            for t in range(NT):
                ts = slice(t * 512, (t + 1) * 512)
                norm_mod(t, lambda k: aT[:, 0, k, 0:1], lambda k: modT[:, k, 0:1], "aT0", "modT", hslot=0)
                load_tabs(t)
                for c in range(8):
                    pb, pk = PS()
                    for k in range(8):
                        S.op("pe", "matmul", pb[:], wq[:, k, c * 128:(c + 1) * 128], hTall[:, k, 0:512], start=(k == 0), stop=(k == 7),
                             reads=wqk + [("hT", 0)], writes=[pk])
                    qT, qTk = BT()
                    qk_norm_rope(pb, pk, gq, qT[:], [qTk])
                    m = c // 4
                    st_ = {}

                    def S1(qb):
                        n = t * 4 + qb
                        qs = slice(qb * 128, (qb + 1) * 128)
                        kreads = [("KTpre",)] if n == 0 else [("KT", m, (n - 1) // 4)]
                        kreads.append(("KT", m, n // 4))
                        pbanks = [PS(), PS()]
                        for a in range(2):
                            rows = slice(a * 64, (a + 1) * 64)
                            pss, pssk = pbanks[a]
                            for kb in range(2):
                                ko_ = (n + kb) * 128
                                S.op("pe", "matmul", pss[:, kb * 128:(kb + 1) * 128], KT[rows, m, ko_:ko_ + 128], qT[rows, qs],
                                     start=True, stop=True, reads=kreads + [qTk], writes=[pssk])
                        st_[qb] = {"pbanks": pbanks}

                    def S2(qb):
                        n = t * 4 + qb
                        pbanks = st_[qb]["pbanks"]
                        pe_, pek = FT()
                        pev = pe_[:].rearrange("p (kb a q) -> p kb a q", kb=2, a=2)
                        for a in range(2):
                            pss, pssk = pbanks[a]
                            S.op("act", "activation", pev[:, :, a, :], pss[:, 0:256].rearrange("p (kb q) -> p kb q", kb=2), AF.Exp, scale=SCALE,
                                 reads=[pssk], writes=[pek])
                        pT, pTk = BT()
                        mk = mask4f[:] if n == 0 else mask4
                        S.op("pool", "tensor_tensor", pT[:], pe_[:], mk, ALU.mult, reads=[pek, "cst", "mask4f"], writes=[pTk])
                        st_[qb]["pT"] = (pT, pTk)

                    def S3(qb):
                        n = t * 4 + qb
                        pT, pTk = st_[qb]["pT"]
                        po, pok = PS()
                        S.op("pe", "matmul", po[:, 0:256], Vtok[:, n, m * 128:(m + 1) * 128], pT[:, 0:256], start=True, stop=False, reads=[("V", n), pTk], writes=[pok])
                        S.op("pe", "matmul", po[:, 0:256], Vtok[:, n + 1, m * 128:(m + 1) * 128], pT[:, 256:512], start=False, stop=True, reads=[("V", n + 1), pTk], writes=[pok])
                        S.op("pe", "matmul", po[:, 256:512], onesb[:], pT[:, 0:256], start=True, stop=False, reads=["onesb", pTk], writes=[pok])
                        S.op("pe", "matmul", po[:, 256:512], onesb[:], pT[:, 256:512], start=False, stop=True, reads=["onesb", pTk], writes=[pok])
                        st_[qb]["po"] = (po, pok)

                    def S4(qb):
                        qs = slice(qb * 128, (qb + 1) * 128)
                        po, pok = st_[qb]["po"]
                        for a in range(2):
                            rows = slice(a * 64, (a + 1) * 64)
                            rk_ = ("rd", a)
                            S.op("act", "activation", rd[rows, :], po[rows, 256 + a * 128:256 + (a + 1) * 128], AF.Ln, bias=esink[rows, j, c:c + 1], scale=1.0,
                                 reads=[pok, "esink"], writes=[rk_])
                            S.op("act", "activation", rd[rows, :], rd[rows, :], AF.Exp, scale=-1.0, reads=[rk_], writes=[rk_])
                            S.op("dve", "tensor_tensor", mixT[rows, c, qs], po[rows, a * 128:(a + 1) * 128], rd[rows, :], ALU.mult, reads=[pok, rk_], writes=[("mix", c)] + MIXW)
                    S1(0)
                    S2(0)
                    for qb in range(4):
                        if qb + 1 < 4:
                            S1(qb + 1)
                        S3(qb)
                        if qb + 1 < 4:
                            S2(qb + 1)
                        S4(qb)
                for fc in range(8):
                    pb, pk = PS()
                    for c in range(8):
                        S.op("pe", "matmul", pb[:], wo[:, c, fc * 128:(fc + 1) * 128], mixT[:, c, :], start=(c == 0), stop=(c == 7),
                             reads=wok + [("mix", c)], writes=[pk])
                    resid_update(t, fc, pb, pk, modT[:, 16 + fc, 0:1], "modT")


        SMPW = 9216 + (4096 if T >= 2048 else 0)
        if T >= 2048:
            smp = xT[:].rearrange("p k t -> p (k t)")[:, 0:SMPW]
        else:
            smp = sb("smp", [128, SMPW])[:]
        soff = [0]

        def salloc(words, shape=None, dt=F32):
            a = smp[:, soff[0]:soff[0] + words]
            soff[0] += words
            assert soff[0] <= SMPW
            if dt == BF16:
                a = a.bitcast(BF16)
            return a

        def sop(eng, name, *args, reads=(), writes=(), **kw):
            return S.op(eng, name, *args, reads=list(reads) + ["SMPREGION"], writes=writes, **kw)

        def sdmas(queue, pairs, reads=(), writes=(), **kw):
            return S.dmas(queue, pairs, reads=list(reads) + ["SMPREGION"], writes=writes, **kw)

        xsT = salloc(128).rearrange("p (k s) -> p k s", k=8)
        hsT = salloc(64, dt=BF16).rearrange("p (k s) -> p k s", k=8)
        mixsT = salloc(64, dt=BF16).rearrange("p (k s) -> p k s", k=8)
        qsT = salloc(128).rearrange("p (k s) -> p k s", k=8)
        qsTb = salloc(64, dt=BF16).rearrange("p (k s) -> p k s", k=8)
        qpad = [salloc(64, dt=BF16).rearrange("p (k s) -> p k s", k=8) for _ in range(2)]
        ksT = salloc(32).rearrange("p (m s) -> p m s", m=2)
        vsT = salloc(32).rearrange("p (m s) -> p m s", m=2)
        NSF = 16
        sfs = [salloc(128) for _ in range(NSF)]
        sfc = [0]

        def SF():
            i = sfc[0]
            sfc[0] = (i + 1) % NSF
            return sfs[i], ("sf", i)
        NSB = 8
        sbs = [salloc(64, dt=BF16) for _ in range(NSB)]
        sbc = [0]

        def SB():
            i = sbc[0]
            sbc[0] = (i + 1) % NSB
            return sbs[i], ("sb", i)
        sin_b = [salloc(512).rearrange("p (s v) -> p s v", s=4) for _ in range(2)]
        sout_b = [salloc(512).rearrange("p (s v) -> p s v", s=4) for _ in range(2)]
        km = salloc(1024, dt=BF16)[0:16, :].rearrange("p (s d) -> p s d", s=16)
        ktv = salloc(128, dt=BF16)[0:16, :].rearrange("p (j d) -> p j d", j=2)
        ckT = salloc(512, dt=BF16).rearrange("p (m s j) -> p m s j", m=2, s=4)
        cvt = salloc(512, dt=BF16).rearrange("p (s d) -> p s d", s=4)
        ckin = [salloc(256) for _ in range(2)]
        kvnew = salloc(512)[0:16, :]
        pTs = [salloc(64, dt=BF16) for _ in range(2)]
        tab16 = salloc(32)
        wstg = [salloc(2048).rearrange("p (k c) -> p k c", k=2) for _ in range(2)] if T >= 2048 else None

        def s_norm_mod(ai, shT, akey, skey):
            sq, sk = SF()
            sop("act", "activation", sq, xsT.rearrange("p k s -> p (k s)"), AF.Square, reads=["xsT"], writes=[sk])
            pb, pk = PS()
            for k in range(8):
                sop("pe", "matmul", pb[:, 0:16], onesD[:], sq[:, k * 16:(k + 1) * 16], start=(k == 0), stop=(k == 7), reads=[sk, "onesD"], writes=[pk])
            rs, rk = SF()
            sop("act", "activation", rs[:, 0:16], pb[:, 0:16], AF.Ln, bias=epsc[:, 0:1], scale=1.0, reads=[pk, "epsc"], writes=[rk])
            sop("act", "activation", rs[:, 0:16], rs[:, 0:16], AF.Exp, scale=-0.5, reads=[rk], writes=[rk])
            t1, t1k = SF()
            t13 = t1.rearrange("p (k s) -> p k s", k=8)
            sop("dve", "tensor_tensor", t13, xsT, rs[:, 0:16].unsqueeze(1).to_broadcast([128, 8, 16]), ALU.mult, reads=["xsT", rk], writes=[t1k])
            sop("dve", "tensor_tensor", t13, t13, aT[:, ai, :, 1:17], ALU.mult, reads=[t1k, akey], writes=[t1k])
            sop("dve", "tensor_tensor", hsT, t13, shT, ALU.add, reads=[t1k, skey], writes=["hsT"])

        def s_resid(pb, pk, gT, gkey):
            t1, t1k = SF()
            t13 = t1.rearrange("p (k s) -> p k s", k=8)
            sop("dve", "tensor_tensor", t13, pb[:, 0:128].rearrange("p (k s) -> p k s", k=8), gT, ALU.mult, reads=[pk, gkey], writes=[t1k])
            sop("dve", "tensor_tensor", xsT, xsT, t13, ALU.add, reads=["xsT", t1k], writes=["xsT"])

        def s_proj_out(w3, wkeys, srcT, srckey, gT, gkey):
            pb, pk = PS()
            for fc in range(8):
                for k in range(8):
                    sop("pe", "matmul", pb[:, fc * 16:(fc + 1) * 16], w3[:, k, fc * 128:(fc + 1) * 128], srcT[:, k, :], start=(k == 0), stop=(k == 7),
                        reads=wkeys + [srckey], writes=[pk])
            s_resid(pb, pk, gT, gkey)

        def s_rope(pb, pk, gcol, out_f32, okey, n=16):
            sq, sk = SF()
            sop("act", "activation", sq[:, 0:n], pb, AF.Square, reads=[pk], writes=[sk])
            pn, pnk = PS()
            sop("pe", "matmul", pn[:, 0:n], onesblk, sq[:, 0:n], start=True, stop=True, reads=[sk, "cst"], writes=[pnk])
            rs, rk = SF()
            sop("act", "activation", rs[:, 0:n], pn[:, 0:n], AF.Ln, bias=epsc[:, 0:1], scale=1.0 / 64, reads=[pnk, "epsc"], writes=[rk])
            sop("act", "activation", rs[:, 0:n], rs[:, 0:n], AF.Exp, scale=-0.5, reads=[rk], writes=[rk])
            qg, qgk = SF()
            sop("dve", "tensor_scalar", qg[:, 0:n], pb, gcol, None, ALU.mult, reads=[pk, "vcols"], writes=[qgk])
            pp, ppk = PS()
            sop("pe", "matmul", pp[:, 0:n], perm, qg[:, 0:n], start=True, stop=True, reads=[qgk, "cst"], writes=[ppk])
            ta, tak = SF()
            sop("dve", "tensor_tensor", ta[:, 0:n], qg[:, 0:n], tab16[:, 0:16], ALU.mult, reads=[qgk, "tab16"], writes=[tak])
            tb, tbk = SF()
            sop("dve", "tensor_tensor", tb[:, 0:n], pp[:, 0:n], tab16[:, 16:32], ALU.mult, reads=[ppk, "tab16"], writes=[tbk])
            sop("dve", "tensor_tensor", ta[:, 0:n], ta[:, 0:n], tb[:, 0:n], ALU.add, reads=[tak, tbk], writes=[tak])
            sop("dve", "tensor_tensor", out_f32, ta[:, 0:n], rs[:, 0:n], ALU.mult, reads=[tak, rk], writes=[okey])

        def s_hgrn_head(l, h, wh, wkeys):
            A = lbAB[:, l, 0, h:h + 1]
            B = lbAB[:, l, 1, h:h + 1]
            pb, pk = PS()
            for j in range(4):
                for k in range(8):
                    sop("pe", "matmul", pb[:, j * 16:(j + 1) * 16], wh[:, k, j, :], hsT[:, k, :], start=(k == 0), stop=(k == 7), reads=wkeys + ["hsT"], writes=[pk])
            qs_, qsk = SF()
            sop("act", "activation", qs_[:, 0:16], pb[:, 0:16], AF.Silu, reads=[pk], writes=[qsk])
            gs_, gsk = SF()
            sop("act", "activation", gs_[:, 0:16], pb[:, 48:64], AF.Silu, reads=[pk], writes=[gsk])
            fg, fgk = SF()
            sop("act", "activation", fg[:, 0:16], pb[:, 16:32], AF.Tanh, scale=0.5, reads=[pk], writes=[fgk])
            kv2, kv2k = SF()
            sop("act", "activation", kv2[:, 16:32], pb[:, 32:48], AF.Copy, reads=[pk], writes=[kv2k])
            sop("dve", "tensor_scalar", fg[:, 0:16], fg[:, 0:16], A, B, ALU.mult, ALU.add, reads=[fgk, "lbAB"], writes=[fgk])
            sop("dve", "tensor_scalar", kv2[:, 0:16], fg[:, 0:16], -1.0, 1.0, ALU.mult, ALU.add, reads=[fgk, kv2k], writes=[kv2k])
            pt, ptk = PS()
            sop("pe", "transpose", pt[0:16, 0:128], kv2[:, 0:16], ident, reads=[kv2k, "cst"], writes=[ptk])
            sop("pe", "transpose", pt[0:16, 128:256], kv2[:, 16:32], ident, reads=[kv2k, "cst"], writes=[ptk])
            sop("dve", "tensor_copy", ktv.rearrange("p j d -> p (j d)"), pt[0:16, 0:256], reads=[ptk], writes=["ktv"])
            sop("dve", "tensor_tensor", km, ktv[:, 0, :].unsqueeze(1).to_broadcast([16, 16, 128]),
                identb[0:16, 0:16].unsqueeze(2).to_broadcast([16, 16, 128]), ALU.mult, reads=["ktv", "identb"], writes=["km"])
            po, pok = PL()
            for bi in range(4):
                si_, so_ = sin_b[bi % 2], sout_b[bi % 2]
                sik, sok = ("sin", bi % 2), ("sout", bi % 2)
                sdmas("sp", [(si_, st_in[l, bi * 4:(bi + 1) * 4, h].rearrange("s k v -> k s v"))], writes=[sik], sem_key=sik)
                for s4 in range(4):
                    s_ = bi * 4 + s4
                    pkv, pkvk = PS()
                    sop("pe", "matmul", pkv[:, 0:128], km[:, s_, :], ktv[:, 1, :], start=True, stop=True, reads=["km", "ktv"], writes=[pkvk])
                    sop("dve", "scalar_tensor_tensor", so_[:, s4, :], si_[:, s4, :], fg[:, s_:s_ + 1], pkv[:, 0:128], ALU.mult, ALU.add,
                        reads=[sik, fgk, pkvk], writes=[sok])
                    sop("pe", "matmul", po[:, s_:s_ + 1], so_[:, s4, :], qs_[:, s_:s_ + 1], start=True, stop=True, reads=[sok, qsk], writes=[pok])
                sdmas("sp", [(ss[l, bi * 4:(bi + 1) * 4, h].rearrange("s k v -> k s v"), so_)], reads=[sok], sem_key=sok, final=True)
            osq, osk = SF()
            sop("act", "activation", osq[:, 0:16], po[:, 0:16], AF.Square, reads=[pok], writes=[osk])
            pn, pnk = PS()
            sop("pe", "matmul", pn[:, 0:16], ones128[:], osq[:, 0:16], start=True, stop=True, reads=[osk, "ones128"], writes=[pnk])
            rs, rk = SF()
            sop("act", "activation", rs[:, 0:16], pn[:, 0:16], AF.Ln, bias=epsc[:, 0:1], scale=1.0, reads=[pnk, "epsc"], writes=[rk])
            sop("act", "activation", rs[:, 0:16], rs[:, 0:16], AF.Exp, scale=-0.5, reads=[rk], writes=[rk])
            sop("dve", "tensor_tensor", rs[:, 0:16], po[:, 0:16], rs[:, 0:16], ALU.mult, reads=[pok, rk], writes=[rk])
            sop("dve", "scalar_tensor_tensor", mixsT[:, h, :], rs[:, 0:16], vcols[:, V_GN + l:V_GN + l + 1], gs_[:, 0:16], ALU.mult, ALU.mult,
                reads=[rk, gsk, "vcols"], writes=["mixsT"])

        def s_hgrn_layer(l):
            s_norm_mod(0, modT[:, 0:8, 1:17], "aT0", "modT")
            hl = {}

            def hissue(h):
                if h >= 8 or h in hl:
                    return
                wsl, wkeys = slot(2 + hctr[0] % 2)
                hctr[0] += 1
                wh = wsl.rearrange("p (k j c) -> p k j c", k=8, j=4)
                load_w([(wsl.rearrange("p (k c) -> p k c", k=8), hg_w_in[l, h].rearrange("(k p) c -> p k c", p=128))], wkeys)
                hl[h] = (wh, wkeys)
            hissue(0)
            for h in range(8):
                hissue(h + 1)
                wh, wkeys = hl[h]
                s_hgrn_head(l, h, wh, wkeys)
            wsl, wokeys = slot(0, 2)
            wout = wsl.rearrange("p (k c) -> p k c", k=8)
            load_w([(wout, hg_w_out[l].rearrange("(k p) c -> p k c", p=128))], wokeys)
            s_proj_out(wout, wokeys, mixsT, "mixsT", modT[:, 16:24, 1:17], "modT")

        def s_mlp(l):
            s_norm_mod(1, modT[:, 24:32, 1:17], "aT1", "modT")
            mloaded = {}

            def missue(g):
                if g >= 8 or g in mloaded:
                    return
                wsl, wk = slot(2 * (mctr[0] % 2), 2)
                mctr[0] += 1
                wup = wsl[:, 0:4096].rearrange("p (k c) -> p k c", k=8)
                wdn = wsl[:, 4096:8192].rearrange("p (k c) -> p k c", k=4)
                sbase = wk[0][1]
                if wstg is None:
                    load_w([(wup, w_up[l].rearrange("(k p) c -> p k c", p=128)[:, :, g * 512:(g + 1) * 512]),
                            (wdn, w_down[l][g * 512:(g + 1) * 512, :].rearrange("(k p) c -> p k c", p=128))], wk)
                    dkeys = [[], []]
                else:
                    load_w([(wup, w_up[l].rearrange("(k p) c -> p k c", p=128)[:, :, g * 512:(g + 1) * 512])], wk)
                    dkeys = []
                    for hh in range(2):
                        sk_ = ("wstg", hh)
                        sdmas("sp", [(wstg[hh], w_down[l][g * 512 + hh * 256:g * 512 + (hh + 1) * 256, :].rearrange("(k p) c -> p k c", p=128))],
                              writes=[sk_], sem_key=sk_)
                        dk_ = ("wdn", sbase, hh)
                        if hh == 0:
                            sop("dve", "tensor_copy", wdn[:, 0:2, :], wstg[hh], reads=[sk_] + wk, writes=[dk_])
                        else:
                            sop("act", "activation", wdn[:, 2:4, :], wstg[hh], AF.Copy, reads=[sk_] + wk, writes=[dk_])
                        dkeys.append([dk_])
                mloaded[g] = (wup, wdn, wk, dkeys)
            missue(0)
            for hg in range(8):
                missue(hg + 1)
                wup, wdn, wk, dkeys = mloaded[hg]
                pb, pk = PS()
                for hc in range(4):
                    for k in range(8):
                        sop("pe", "matmul", pb[:, hc * 16:(hc + 1) * 16], wup[:, k, hc * 128:(hc + 1) * 128], hsT[:, k, :], start=(k == 0), stop=(k == 7),
                            reads=wk + ["hsT"], writes=[pk])
                r, rk = SF()
                sop("act", "activation", r[:, 0:64], pb[:, 0:64], AF.Relu, reads=[pk], writes=[rk])
                hd, hdk = SB()
                sop("dve", "tensor_tensor", hd[:, 0:64], r[:, 0:64], r[:, 0:64], ALU.mult, reads=[rk], writes=[hdk])
                pb2, pk2 = PS()
                for fc in range(8):
                    for hc in range(4):
                        sop("pe", "matmul", pb2[:, fc * 16:(fc + 1) * 16], wdn[:, hc, fc * 128:(fc + 1) * 128], hd[:, hc * 16:(hc + 1) * 16],
                            start=(hc == 0), stop=(hc == 3), reads=wk + dkeys[hc // 2] + [hdk], writes=[pk2])
                s_resid(pb2, pk2, modT[:, 40:48, 1:17], "modT")

        def s_kv_phase():
            ada(kv_w_ada, kv_b_ada, 4, kvmodT, "kvmodT")
            for k in range(8):
                S.op("dve", "tensor_scalar", aT[:, 2, k, :], kvmodT[:, 8 + k, :], 1.0, vcols[:, V_KVN + k:V_KVN + k + 1], ALU.add, ALU.mult,
                     reads=["kvmodT", "vcols"], writes=["aT2"])
            wsl, wkk = slot(0)
            wkv = wsl.rearrange("p (k c) -> p k c", k=8)
            load_w([(wkv, w_kv.rearrange("(k p) c -> p k c", p=128))], wkk)
            S.op("dve", "tensor_copy", kvmodP[:], kvmodT[:, :, 0], reads=["kvmodT"], writes=["kvmodP"])
            S.op("dve", "tensor_copy", aKVP[:], aT[:, 2, :, 0], reads=["aT2"], writes=["aKVP"])
            s_norm_mod(2, kvmodT[:, 0:8, 1:17], "aT2", "kvmodT")
            gk = vcols[:, V_KN:V_KN + 1]
            for m in range(2):
                pb, pk = PS()
                for k in range(8):
                    sop("pe", "matmul", pb[:, 0:16], wkv[:, k, m * 128:(m + 1) * 128], hsT[:, k, :], start=(k == 0), stop=(k == 7), reads=wkk + ["hsT"], writes=[pk])
                s_rope(pb[:, 0:16], pk, gk, ksT[:, m, :], "ksT")
                pb, pk = PS()
                for k in range(8):
                    sop("pe", "matmul", pb[:, 0:16], wkv[:, k, 256 + m * 128:256 + (m + 1) * 128], hsT[:, k, :], start=(k == 0), stop=(k == 7), reads=wkk + ["hsT"], writes=[pk])
                sop("act", "activation", vsT[:, m, :], pb[:, 0:16], AF.Copy, reads=[pk], writes=["vsT"])
            pt, ptk = PS()
            for m in range(2):
                sop("pe", "transpose", pt[0:16, m * 128:(m + 1) * 128], ksT[:, m, :], ident, reads=["ksT", "cst"], writes=[ptk])
                sop("pe", "transpose", pt[0:16, 256 + m * 128:256 + (m + 1) * 128], vsT[:, m, :], ident, reads=["vsT", "cst"], writes=[ptk])
            sop("dve", "tensor_copy", kvnew, pt[0:16, :], reads=[ptk], writes=["kvnew"])
            sdmas("sp", [(ks[:, 127, :], kvnew[:, 0:256]), (vs[:, 127, :], kvnew[:, 256:512])], reads=["kvnew"], sem_key="kvnew", final=True)
            S.dmas("sp", [(ks[:, 0:127, :], ck[:, 1:128, :]), (vs[:, 0:127, :], cv[:, 1:128, :])], sem_key="cachecp", final=True)

        def s_attn_layer(l):
            j = l - 2
            wsl, wqk = slot(0, 2)
            wq = wsl.rearrange("p (k c) -> p k c", k=8)
            load_w([(wq, w_q[j].rearrange("(k p) c -> p k c", p=128))], wqk)
            wsl2, wok = slot(2, 2)
            wo = wsl2.rearrange("p (c f) -> p c f", c=8)
            load_w([(wo, w_o[j].rearrange("(c p) f -> p c f", p=128))], wok)
            gq = vcols[:, V_QN + j:V_QN + j + 1]
            s_norm_mod(0, modT[:, 0:8, 1:17], "aT0", "modT")
            for c in range(8):
                pb, pk = PS()
                for k in range(8):
                    sop("pe", "matmul", pb[:, 0:16], wq[:, k, c * 128:(c + 1) * 128], hsT[:, k, :], start=(k == 0), stop=(k == 7), reads=wqk + ["hsT"], writes=[pk])
                s_rope(pb[:, 0:16], pk, gq, qsT[:, c, :], "qsT")
            sop("act", "activation", qsTb, qsT, AF.Copy, reads=["qsT"], writes=["qsTb"])
            for a in range(2):
                sop("dve", "memset", qpad[a], 0.0, writes=[("qpad", a)])
                rows = slice(a * 64, (a + 1) * 64)
                sop("dve", "tensor_copy", qpad[a][rows], qsT[rows], reads=["qsT"], writes=[("qpad", a)])
            pss, pssk = PL()
            po, pok = PL()
            for bi in range(4):
                sdmas("pool", [(cvt, cv[bi * 4:(bi + 1) * 4].rearrange("s j d -> j s d"))], writes=["cvt"], sem_key="cvt")
                for s4 in range(4):
                    s_ = bi * 4 + s4
                    ci, cik = ckin[s4 % 2], ("ckin", s4 % 2)
                    sdmas("sp", [(ci, ck[s_])], writes=[cik], sem_key=cik)
                    pt, ptk = PS()
                    for m in range(2):
                        sop("pe", "transpose", pt[:, m * 128:(m + 1) * 128], ci[:, m * 128:(m + 1) * 128], ident, reads=[cik, "cst"], writes=[ptk])
                    sop("act", "activation", ckT[:, :, s4, :], pt[:, 0:256].rearrange("p (m j) -> p m j", m=2), AF.Copy, reads=[ptk], writes=["ckT"])
                    for a in range(2):
                        for m in range(2):
                            c0_ = a * 128 + s_ * 8 + 4 * m
                            sop("pe", "matmul", pss[:, c0_:c0_ + 4], ckT[:, m, s4, :], qpad[a][:, 4 * m:4 * m + 4, s_],
                                start=True, stop=True, reads=["ckT", ("qpad", a)], writes=[pssk])
                for a in range(2):
                    cols = slice(bi * 32, (bi + 1) * 32)
                    pe_, pek = SF()
                    sop("act", "activation", pe_[:, 0:32], pss[:, a * 128 + bi * 32:a * 128 + (bi + 1) * 32], AF.Exp, scale=SCALE, reads=[pssk], writes=[pek])
                    sop("dve", "tensor_scalar", pTs[a][:, cols], pe_[:, 0:32], jmask, None, ALU.mult, reads=[pek, "cst"], writes=[("pTs", a)])
                for s4 in range(4):
                    s_ = bi * 4 + s4
                    for m in range(2):
                        for a in range(2):
                            cs_ = slice(s_ * 8 + 4 * m, s_ * 8 + 4 * m + 4)
                            sop("pe", "matmul", po[:, a * 128 + s_ * 8 + 4 * m:a * 128 + s_ * 8 + 4 * m + 4], cvt[:, s4, m * 128:(m + 1) * 128], pTs[a][:, cs_],
                                start=True, stop=True, reads=["cvt", ("pTs", a)], writes=[pok])
            for a in range(2):
                sop("pe", "matmul", po[:, 256 + a * 128:256 + (a + 1) * 128], onesb[:], pTs[a][:, 0:128], start=True, stop=True, reads=["onesb", ("pTs", a)], writes=[pok])
            pr, prk = SF()
            sop("dve", "tensor_tensor", pr.rearrange("p (m i s) -> p m i s", m=2, i=4), qsT.rearrange("p (m i) s -> p m i s", m=2),
                ksT.unsqueeze(2).to_broadcast([128, 2, 4, 16]), ALU.mult, reads=["qsT", "ksT"], writes=[prk])
            psn, psnk = PS()
            sop("pe", "matmul", psn[:, 0:128], onesblk, pr, start=True, stop=True, reads=[prk, "cst"], writes=[psnk])
            pn_, pnk_ = SF()
            sop("act", "activation", pn_, psn[:, 0:128], AF.Exp, scale=SCALE, reads=[psnk], writes=[pnk_])
            on_, onk = SF()
            sop("dve", "tensor_tensor", on_.rearrange("p (m i s) -> p m i s", m=2, i=4), pn_.rearrange("p (m i s) -> p m i s", m=2, i=4),
                vsT.unsqueeze(2).to_broadcast([128, 2, 4, 16]), ALU.mult, reads=[pnk_, "vsT"], writes=[onk])
            num, numk = SF()
            den, denk = SF()
            for a in range(2):
                rows = slice(a * 64, (a + 1) * 64)
                pov = po[rows, a * 128:(a + 1) * 128].rearrange("p (s c) -> p c s", c=8)
                dnv = po[rows, 256 + a * 128:256 + (a + 1) * 128].rearrange("p (s c) -> p c s", c=8)
                n3 = num[rows, :].rearrange("p (c s) -> p c s", c=8)
                d3 = den[rows, :].rearrange("p (c s) -> p c s", c=8)
                sop("dve", "tensor_tensor", n3, pov, on_[rows, :].rearrange("p (c s) -> p c s", c=8), ALU.add, reads=[pok, onk], writes=[numk])
                sop("dve", "tensor_tensor", d3, dnv, pn_[rows, :].rearrange("p (c s) -> p c s", c=8), ALU.add, reads=[pok, pnk_], writes=[denk])
                sop("dve", "tensor_tensor", d3, d3, esink[rows, j, :].unsqueeze(2).to_broadcast([64, 8, 16]), ALU.add, reads=[denk, "esink"], writes=[denk])
                sop("dve", "reciprocal", den[rows, :], den[rows, :], reads=[denk], writes=[denk])
                sop("dve", "tensor_tensor", mixsT[rows, :, :], n3, d3, ALU.mult, reads=[numk, denk], writes=["mixsT"])
            s_proj_out(wo, wok, mixsT, "mixsT", modT[:, 16:24, 1:17], "modT")

        def sample_phase():
            sop("dve", "memset", smp, 0.0, writes=["xsT", "hsT", "mixsT", "qsT", "qsTb", "ksT", "vsT", "km", "ktv", "ckT", "cvt", "kvnew",
                                                    ("pTs", 0), ("pTs", 1), "tab16", ("wstg", 0), ("wstg", 1)] + [("sf", i) for i in range(NSF)] + [("sb", i) for i in range(NSB)])
            sdmas("sp", [(tab16, cs16[:, :])], writes=["tab16"], sem_key="tab16")
            xa, xak = FT()
            xb_, xbk = FT()
            S.dmas("sp", [(xa[0:16, :], xs[:, 0:512]), (xb_[0:16, :], xs[:, 512:1024])], writes=[xak, xbk], sem_key=xak)
            pb, pk = PS()
            for k in range(8):
                src = (xa if k < 4 else xb_)[0:16, (k % 4) * 128:(k % 4 + 1) * 128]
                sop("pe", "transpose", pb[:, k * 16:(k + 1) * 16], src, ident[0:16, 0:16], reads=[xak, xbk, "cst"], writes=[pk])
            sop("dve", "tensor_copy", xsT.rearrange("p k s -> p (k s)"), pb[:, 0:128], reads=[pk], writes=["xsT"])
            for l in range(4):
                ada(w_ada[l], b_ada[l], 12, modT, "modT")
                mod_derive(l)
                S.op("dve", "tensor_copy", modP[:, l, :], modT[:, :, 0], reads=["modT"], writes=[("modP", l)])
                S.op("dve", "tensor_copy", aP[:, l, :, :], aT[:, 0:2, :, 0], reads=["aT0", "aT1"], writes=[("aP", l)])
                if l == 2:
                    s_kv_phase()
                if l < 2:
                    s_hgrn_layer(l)
                else:
                    s_attn_layer(l)
                s_mlp(l)
            pb, pk = PS()
            pb2, pk2 = PS()
            for k in range(8):
                dstp = pb if k < 4 else pb2
                dk_ = pk if k < 4 else pk2
                sop("pe", "transpose", dstp[0:16, (k % 4) * 128:(k % 4 + 1) * 128], xsT[:, k, :], ident, reads=["xsT", "cst"], writes=[dk_])
            ya, yak = FT()
            yb, ybk = FT()
            sop("dve", "tensor_copy", ya[0:16, :], pb[0:16, :], reads=[pk], writes=[yak])
            sop("dve", "tensor_copy", yb[0:16, :], pb2[0:16, :], reads=[pk2], writes=[ybk])
            S.dmas("sp", [(ys[:, 0:512], ya[0:16, :]), (ys[:, 512:1024], yb[0:16, :])], reads=[yak, ybk], sem_key=yak, final=True)

        def final_out():
            for b in range(NBLK):
                t = b // 4
                for g in range(2):
                    yo, yk = FT()
                    pb, pk = PS()
                    for kk in range(4):
                        k = g * 4 + kk
                        S.op("pe", "transpose", pb[:, kk * 128:(kk + 1) * 128], xT[:, k, b * 128:(b + 1) * 128], ident, reads=[("xT", k, t), "cst"], writes=[pk])
                    if g == 0:
                        S.op("dve", "tensor_copy", yo[:], pb[:], reads=[pk], writes=[yk])
                    else:
                        S.op("act", "activation", yo[:], pb[:], AF.Copy, reads=[pk], writes=[yk])
                    S.dmas("sp", [(yp[b * 128:(b + 1) * 128, g * 512:(g + 1) * 512], yo[:])], reads=[yk], sem_key=yk, final=True)

        import os
        STG = os.environ.get("KSTAGE", "full")
        if do_sample:
            sample_phase()
        load_x()
        for l in range(4):
            if STG == "io":
                break
            if do_sample:
                S.op("dve", "tensor_copy", modT[:, :, 0], modP[:, l, :], reads=[("modP", l)], writes=["modT"])
                S.op("dve", "tensor_copy", aT[:, 0:2, :, 0], aP[:, l, :, :], reads=[("aP", l)], writes=["aT0", "aT1"])
            else:
                ada(w_ada[l], b_ada[l], 12, modT, "modT")
                mod_derive(l)
            if STG == "ada":
                break
            if l == 2:
                kv_phase()
            if l < 2:
                if STG == "mlp0":
                    pass
                elif STG == "passA":
                    S.op("dve", "memset", Sst, 0.0, writes=SK)
                    hgrn_pass(l, True)
                    S.dmas("sp", [(sp_state[l].rearrange("h k v -> k h v"), Sst)], reads=SK, sem_key=("spst", l), final=True)
                    break
                else:
                    hgrn_layer(l)
            else:
                attn_layer(l)
            if STG == "hgrn0":
                break
            mlp(l)
            if STG in ("l0", "mlp0"):
                break
            if STG == "l1" and l == 1:
                break
            if STG == "l2" and l == 2:
                break
        final_out()

        S.emit()
        print("ops", len(S.ops), "sem counts", S.max_counts, "dma sems", len(S.dma_count), "sbuf left", nc.sbuf_bytes_remaining)
    return nc


_CACHE = {}


def make_in_maps(inputs, T, seq):
    f = lambda a: np.ascontiguousarray(np.asarray(a, dtype=np.float32))
    xp_all = f(inputs["x_prompt"])
    xs_all = f(inputs["x_sample"]).reshape(NSAMP, D)
    cp = f(inputs["c_prompt"])
    cs = f(inputs["c_sample"])
    st_all = f(inputs["state_hgrn"])
    ck_all = f(inputs["cache_k"]).reshape(NSAMP, 128, 256)
    cv_all = f(inputs["cache_v"]).reshape(NSAMP, 128, 256)
    shared = {k: f(inputs[k]) for k in ("w_ada", "b_ada", "norm1_g", "norm2_g", "hg_w_in", "hg_w_out", "hg_lower_bounds",
                                        "hg_gn_g", "kv_w_ada", "kv_b_ada", "kv_norm_g", "w_kv", "k_norm_g", "w_q",
                                        "q_norm_g", "sinks", "w_o", "w_up", "w_down")}
    shared["hg_w_in"] = np.ascontiguousarray(shared["hg_w_in"].reshape(2, D, 4, 8, 128).transpose(0, 3, 1, 2, 4).reshape(2, 8, D, 512))
    shared["w_q"] = np.ascontiguousarray(shared["w_q"].reshape(2, D, 2, 2, 4, 64).transpose(0, 1, 2, 4, 3, 5).reshape(2, D, D))
    shared["w_o"] = np.ascontiguousarray(shared["w_o"].reshape(2, 2, 2, 4, 64, D).transpose(0, 1, 3, 2, 4, 5).reshape(2, D, D))
    ct16, st16 = rope_tables(np.full((16,), PAST, np.int64))
    cs16 = np.ascontiguousarray(np.concatenate([ct16, st16], axis=1))
    maps = []
    for c in range(8):
        b, half = c // 2, c % 2
        m = dict(shared)
        m["xp"] = np.ascontiguousarray(xp_all[b, half * T:(half + 1) * T])
        sl = slice(c * NS, (c + 1) * NS)
        m["c17"] = np.ascontiguousarray(np.concatenate([cp[b:b + 1], cs[sl]], axis=0))
        m["xs"] = np.ascontiguousarray(xs_all[sl])
        m["st_in"] = np.ascontiguousarray(st_all[:, sl])
        m["ck"] = np.ascontiguousarray(ck_all[sl])
        m["cv"] = np.ascontiguousarray(cv_all[sl])
        m["consts"] = host_consts(float(half))
        ct, stb = rope_tables(np.arange(half * T, (half + 1) * T))
        m["costab"] = ct
        m["sintab"] = stb
        m["cs16"] = cs16
        maps.append(m)
    return maps


def run(inputs, T, dbg=False):
    if T not in _CACHE:
        _CACHE[T] = build(T, dbg=dbg)
    nc = _CACHE[T]
    maps = make_in_maps(inputs, T, 2 * T)
    res = run_bass_kernel_spmd(nc, maps, core_ids=list(range(8)))
    R = res.results
    global LAST_RESULTS
    LAST_RESULTS = R
    seq = 2 * T
    y_prompt = np.zeros((NB, seq, D), np.float32)
    hg_p = np.zeros((2, NB, 8, 128, 128), np.float32)
    k_p = np.zeros((NB, 128, 4, 64), np.float32)
    v_p = np.zeros((NB, 128, 4, 64), np.float32)
    y_sample = np.zeros((NSAMP, 1, D), np.float32)
    hg_s = np.zeros((2, NSAMP, 8, 128, 128), np.float32)
    k_s = np.zeros((NSAMP, 128, 4, 64), np.float32)
    v_s = np.zeros((NSAMP, 128, 4, 64), np.float32)
    for c in range(8):
        b, half = c // 2, c % 2
        r = R[c]
        y_prompt[b, half * T:(half + 1) * T] = r["yp"]
        if half == 1:
            hg_p[:, b] = r["sp_state"]
            k_p[b] = r["kp"].reshape(128, 4, 64)
            v_p[b] = r["vp"].reshape(128, 4, 64)
        sl = slice(c * NS, (c + 1) * NS)
        y_sample[sl, 0] = r["ys"]
        hg_s[:, sl] = r["ss"]
        k_s[sl] = r["ks"].reshape(NS, 128, 4, 64)
        v_s[sl] = r["vs"].reshape(NS, 128, 4, 64)
    return (y_prompt, y_sample, hg_p, k_p, v_p, hg_s, k_s, v_s)


def kernel(**inputs):
    return run(inputs, SEQ // 2)
```

```python
import math
from contextlib import ExitStack

import numpy as np
import concourse.bass as bass
import concourse.mybir as mybir
from concourse.bass_utils import run_bass_kernel_spmd

F32 = mybir.dt.float32
BF16 = mybir.dt.bfloat16
ALU = mybir.AluOpType
AF = mybir.ActivationFunctionType

D = 1024
SEQ = 4096
NB = 4
NSAMP = 128
NS = 16
WINDOW = 128
PAST = 8192
EPS = 1e-6
SCALE = 1.0 / 8.0

SAME_ENGINE_SYNC = True


class Op:
    __slots__ = ("id", "eng", "fn", "deps", "is_dma", "sem_key", "n_dma", "waits", "inc_amt",
                 "needs_inc", "lidx", "count", "dma_val", "vc", "final")


class Sched:
    ENGS = ("pe", "act", "dve", "pool", "sp")

    def __init__(self, nc):
        self.nc = nc
        self.ops = []
        self.by_eng = {e: [] for e in self.ENGS}
        self.last_w = {}
        self.readers = {}
        self.dma_count = {}
        self.out_dmas = []
        self.bulk = set()

    def _track(self, op, reads, writes):
        pr = [k for k in reads if isinstance(k, tuple) and k and k[0] == "ps"]
        if pr:
            reads = [k for k in reads if k not in pr]
            writes = list(writes) + pr
        deps = set()
        for k in reads:
            w = self.last_w.get(k)
            if w is not None:
                deps.add(w)
        for k in writes:
            w = self.last_w.get(k)
            if w is not None:
                deps.add(w)
            for r in self.readers.get(k, ()):
                deps.add(r)
        for k in reads:
            self.readers.setdefault(k, []).append(op.id)
        for k in writes:
            self.last_w[k] = op.id
            self.readers[k] = []
        deps.discard(op.id)
        op.deps = deps

    def op(self, eng, name, *args, reads=(), writes=(), **kw):
        def fn(e, name=name, args=args, kw=kw):
            return getattr(e, name)(*args, **kw)
        return self.add(eng, fn, reads, writes)

    def dmas(self, queue, pairs, reads=(), writes=(), sem_key=None, final=False, bulk=False):
        pairs = list(pairs)

        def fn(e, pairs=pairs):
            return [e.dma_start(out=o, in_=i) for (o, i) in pairs]
        return self.dma(queue, fn, reads, writes, sem_key=sem_key, n=len(pairs), final=final, bulk=bulk)

    def add(self, eng, fn, reads=(), writes=()):
        op = Op()
        op.id = len(self.ops)
        op.eng = eng
        op.fn = fn
        op.is_dma = False
        op.sem_key = None
        op.n_dma = 0
        op.needs_inc = False
        op.final = False
        self._track(op, reads, writes)
        self.ops.append(op)
        self.by_eng[eng].append(op)
        return op

    def dma(self, queue, fn, reads=(), writes=(), sem_key=None, n=1, final=False, inc=16, bulk=False):
        op = Op()
        op.id = len(self.ops)
        op.eng = queue
        op.fn = fn
        op.is_dma = True
        op.sem_key = sem_key
        op.n_dma = n
        op.inc_amt = inc
        op.needs_inc = True
        op.final = final
        if bulk:
            self.bulk.add(sem_key)
        self.dma_count[sem_key] = self.dma_count.get(sem_key, 0) + inc * n
        op.dma_val = self.dma_count[sem_key]
        self._track(op, reads, writes)
        self.ops.append(op)
        self.by_eng[queue].append(op)
        if final:
            self.out_dmas.append(op)
        return op

    def finalize(self):
        for op in self.ops:
            if op.is_dma and op.sem_key in self.bulk:
                op.dma_val = self.dma_count[op.sem_key]
        lcount = {e: 0 for e in self.ENGS}
        for op in self.ops:
            if not op.is_dma:
                lcount[op.eng] += 1
                op.lidx = lcount[op.eng]
        evc = {e: {} for e in self.ENGS}
        for op in self.ops:
            E = op.eng
            my = evc[E]
            waits = []
            for d in sorted(op.deps):
                dop = self.ops[d]
                if dop.is_dma:
                    key = ("D", dop.sem_key)
                    val = dop.dma_val
                else:
                    if dop.eng == E and (E == "pe" or not SAME_ENGINE_SYNC) and not op.is_dma:
                        continue
                    key = ("E", dop.eng)
                    val = dop.lidx
                if my.get(key, 0) >= val:
                    continue
                waits.append(d)
                dop.needs_inc = True
                for k, v in dop.vc.items():
                    if my.get(k, 0) < v:
                        my[k] = v
            op.waits = waits
            vc = dict(my)
            if op.is_dma:
                k = ("D", op.sem_key)
                vc[k] = max(vc.get(k, 0), op.dma_val)
            else:
                vc[("E", E)] = op.lidx
            op.vc = vc
        self.final_waits = {}
        for op in self.out_dmas:
            self.final_waits[op.sem_key] = max(self.final_waits.get(op.sem_key, 0), op.dma_val)
        cnt = {e: 0 for e in self.ENGS}
        for op in self.ops:
            if not op.is_dma:
                if op.needs_inc:
                    cnt[op.eng] += 1
                op.count = cnt[op.eng]
        self.max_counts = cnt

    def emit(self):
        nc = self.nc
        self.finalize()
        with ExitStack() as st:
            esem = {e: st.enter_context(nc.semaphore("es_" + e)) for e in ("pe", "act", "dve", "pool")}
            dsem = {}
            for i, k in enumerate(self.dma_count):
                dsem[k] = st.enter_context(nc.semaphore("ds_%d" % i))
            block = st.enter_context(nc.Block())
            ops = self.ops

            def run(eng_name, eng):
                for op in self.by_eng[eng_name]:
                    wmap = {}
                    for d in op.waits:
                        dop = ops[d]
                        if dop.is_dma:
                            s = dsem[dop.sem_key]
                            v = dop.dma_val
                        else:
                            s = esem[dop.eng]
                            v = dop.count
                        key = id(s)
                        if key not in wmap or wmap[key][1] < v:
                            wmap[key] = (s, v)
                    for s, v in wmap.values():
                        eng.wait_ge(s, v)
                    r = op.fn(eng)
                    if op.is_dma:
                        assert len(r) == op.n_dma, (len(r), op.n_dma)
                        for ins in r:
                            ins.then_inc(dsem[op.sem_key], op.inc_amt)
                    elif op.needs_inc:
                        r.then_inc(esem[eng_name], 1)
                if eng_name == "sp":
                    for k, v in self.final_waits.items():
                        eng.wait_ge(dsem[k], v)

            @block.tensor
            def _(e):
                run("pe", e)

            @block.scalar
            def _(e):
                run("act", e)

            @block.vector
            def _(e):
                run("dve", e)

            @block.gpsimd
            def _(e):
                run("pool", e)

            @block.sync
            def _(e):
                run("sp", e)


C_ID, C_MR, C_AM, C_M4, C_OB, C_PM, C_JM, C_FL, C_RM, C_N = 0, 128, 640, 1152, 1664, 1792, 1920, 1921, 1922, 1923


def host_consts(flag):
    c = np.zeros((128, C_N), np.float32)
    c[:, C_ID:C_ID + 128] = np.eye(128, dtype=np.float32)
    mr = np.ones((512,), np.float32)
    mr[::64] = 0.0
    c[:, C_MR:C_MR + 512] = mr[None, :]
    s = np.arange(64)[:, None]
    t = np.arange(64)[None, :]
    am = ((s <= t) & ((s // 32) == (t // 32))).astype(np.float32)
    c[0:64, C_AM:C_AM + 512] = np.tile(am, (1, 8))
    c[0:32, C_RM] = 1.0
    j = np.arange(128)[:, None]
    q = np.arange(128)[None, :]
    mprev = (j > q).astype(np.float32)
    mcur = (j <= q).astype(np.float32)
    c[:, C_M4:C_M4 + 512] = np.concatenate([mprev, mprev, mcur, mcur], axis=1)
    ob = np.zeros((128, 128), np.float32)
    ob[0:64, 0:64] = 1.0
    ob[64:128, 64:128] = 1.0
    c[:, C_OB:C_OB + 128] = ob
    pm = np.zeros((128, 128), np.float32)
    for p in range(128):
        d = p % 64
        partner = p + 32 if d < 32 else p - 32
        pm[p, partner] = 1.0
    c[:, C_PM:C_PM + 128] = pm
    c[:, C_JM] = 1.0
    c[0, C_JM] = 0.0
    c[:, C_FL] = flag
    return c


def rope_tables(pos):
    half = 32
    inv = (np.float32(10000.0) ** (-np.arange(half, dtype=np.float32) / np.float32(half))).astype(np.float32)
    ang = pos.astype(np.float32)[None, :] * inv[:, None]
    cos = np.cos(ang).astype(np.float32)
    sin = np.sin(ang).astype(np.float32)
    ct = np.concatenate([cos, cos, cos, cos], axis=0)
    st = np.concatenate([-sin, sin, -sin, sin], axis=0)
    return np.ascontiguousarray(ct), np.ascontiguousarray(st)


def build(T, do_sample=True, dbg=False):
    NT = T // 512
    NBLK = T // 128
    nc = bass.Bass("TRN2", target_bir_lowering=False)

    def din(name, shape, dt=F32):
        return nc.dram_tensor(name, list(shape), dt, kind="ExternalInput").ap()

    def dout(name, shape, dt=F32):
        return nc.dram_tensor(name, list(shape), dt, kind="ExternalOutput").ap()

    def dint(name, shape, dt=F32):
        return nc.dram_tensor(name, list(shape), dt, kind="Internal").ap()

    xp = din("xp", [T, D])
    c17 = din("c17", [17, D])
    xs = din("xs", [NS, D])
    st_in = din("st_in", [2, NS, 8, 128, 128])
    ck = din("ck", [NS, 128, 256])
    cv = din("cv", [NS, 128, 256])
    w_ada = din("w_ada", [4, D, 6 * D])
    b_ada = din("b_ada", [4, 6 * D])
    norm1_g = din("norm1_g", [4, D])
    norm2_g = din("norm2_g", [4, D])
    hg_w_in = din("hg_w_in", [2, 8, D, 512])
    hg_w_out = din("hg_w_out", [2, D, D])
    hg_lb = din("hg_lower_bounds", [2, D])
    hg_gn = din("hg_gn_g", [2, 128])
    kv_w_ada = din("kv_w_ada", [D, 2 * D])
    kv_b_ada = din("kv_b_ada", [2 * D])
    kv_norm_g = din("kv_norm_g", [D])
    w_kv = din("w_kv", [D, 512])
    k_norm_g = din("k_norm_g", [64])
    w_q = din("w_q", [2, D, D])
    q_norm_g = din("q_norm_g", [2, 64])
    sinks = din("sinks", [2, 16])
    w_o = din("w_o", [2, D, D])
    w_up = din("w_up", [4, D, 4 * D])
    w_down = din("w_down", [4, 4 * D, D])
    consts = din("consts", [128, C_N])
    costab = din("costab", [128, T])
    sintab = din("sintab", [128, T])
    cs16 = din("cs16", [128, 32])

    yp = dout("yp", [T, D])
    ys = dout("ys", [NS, D])
    sp_state = dout("sp_state", [2, 8, 128, 128])
    kp = dout("kp", [128, 256])
    vp = dout("vp", [128, 256])
    ss = dout("ss", [2, NS, 8, 128, 128])
    ks = dout("ks", [NS, 128, 256])
    vs = dout("vs", [NS, 128, 256])

    DBG = {}
    if dbg:
        for nm in ("mix", "o", "qq", "kk", "e3", "att", "cum", "rs"):
            DBG[nm] = dout("dbg_" + nm, [128, 8, 512] if nm == "mix" else [128, 512])
    xs_scr = [dint("xs_scr%d" % i, [128, 1024]) for i in range(2)]
    xg_scr = [dint("xg_scr%d" % i, [256, 1024]) for i in range(2)]
    kvs_scr = dint("kvs_scr", [(T // 512) * 8, 64, 2048], BF16)
    kv_scr = dint("kv_scr", [128, 512], BF16)
    kvg_scr = dint("kvg_scr", [256, 512], BF16)
    PAIRS = [[0, 1], [2, 3], [4, 5], [6, 7]]

    with ExitStack() as st:
        st.enter_context(nc.allow_low_precision("bf16 matmul operands, fp32 accumulation"))
        st.enter_context(nc.allow_non_contiguous_dma("small strided parameter loads"))

        def sb(name, shape, dt=F32):
            return st.enter_context(nc.sbuf_tensor(name, list(shape), dt))

        S = Sched(nc)
        banks = [st.enter_context(nc.psum_tensor("bank%d" % i, [128, 512], F32)) for i in range(8)]
        bctr = [0]
        psn = [6]

        def PS():
            i = bctr[0] % psn[0]
            bctr[0] = (i + 1) % psn[0]
            return banks[i], ("ps", i)

        lctr = [0]

        def PL():
            i = 6 + lctr[0] % 2
            lctr[0] += 1
            return banks[i], ("ps", i)

        NF = 9
        fts = [sb("ft%d" % i, [128, 512]) for i in range(NF)]
        fctr = [0]

        def FT():
            i = fctr[0]
            fctr[0] = (i + 1) % NF
            return fts[i], ("ft", i)

        NBT = 8
        bts = [sb("bt%d" % i, [128, 512], BF16) for i in range(NBT)]
        bbctr = [0]

        def BT():
            i = bbctr[0]
            bbctr[0] = (i + 1) % NBT
            return bts[i], ("bt", i)

        xT = sb("xT", [128, 8, T])
        cst = sb("cst", [128, C_N])
        NSLOT = 4
        arena = sb("arena", [128, NSLOT * 4096], BF16)

        def slot(i, n=1):
            return arena[:, i * 4096:(i + n) * 4096], [("W", i + r) for r in range(n)]

        R32 = sb("R32", [128, 16384], BF16)
        hTall = R32[:, :].rearrange("p (k t) -> p k t", k=8) if T == 2048 else None
        if T != 2048:
            hTall = R32[:, 0:8 * T].rearrange("p (k t) -> p k t", k=8)
        MIXK = [("mix", c_) for c_ in range(8)]
        if T >= 1024:
            mixT = hTall[:, :, 512:1024]
            MIXW = [("hT", 1)]
        else:
            mixT = sb("mixT", [128, 8, 512], BF16)[:]
            MIXW = []
        tabc = sb("tabc", [128, 512])
        tabs = sb("tabs", [128, 512])
        identb = sb("identb", [128, 128], BF16)
        onesD = sb("onesD", [128, 128])
        ones128 = sb("ones128", [128, 128])
        onesb = sb("onesb", [128, 128], BF16)
        onesDb = sb("onesDb", [128, 128], BF16)
        ones128b = sb("ones128b", [128, 128], BF16)
        onesblkb = sb("onesblkb", [128, 128], BF16)
        permb = sb("permb", [128, 128], BF16)
        epsc = sb("epsc", [128, 1])
        epsl = sb("epsl", [128, 1])
        vrows = sb("vrows", [128, 128])
        vcols = sb("vcols", [128, 128])
        cT = sb("cT", [128, 8, 17])
        cTb = sb("cTb", [128, 8, 17], BF16)
        modT = sb("modT", [128, 48, 17])
        kvmodT = sb("kvmodT", [128, 16, 17])
        aT = sb("aT", [128, 3, 8, 17])
        lbAB = sb("lbAB", [128, 2, 2, 8])
        modP = sb("modP", [128, 4, 48])
        aP = sb("aP", [128, 4, 2, 8])
        kvmodP = sb("kvmodP", [128, 16])
        aKVP = sb("aKVP", [128, 8])
        lbt = sb("lbt", [128, 2, 8])
        esink = sb("esink", [128, 2, 8])
        mask4f = sb("mask4f", [128, 512])
        ones17 = sb("ones17", [1, 17])
        rd = sb("rd", [128, 128])
        KVW = max(2 * (T + 128) + (NBLK + 1) * 256, 8704)
        KVR = sb("KVR", [128, KVW], BF16)
        KT = KVR[:, 0:2 * (T + 128)].rearrange("p (m t) -> p m t", m=2)
        Vtok = KVR[:, 2 * (T + 128):2 * (T + 128) + (NBLK + 1) * 256].rearrange("p (b c) -> p b c", c=256)
        kdtok = KVR[0:64, 0:1024].rearrange("p (c d) -> p c d", c=8)
        vtok = KVR[0:64, 1024:2048].rearrange("p (c d) -> p c d", c=8)
        attsb = KVR[0:64, 2048:2560]
        Sbf = KVR[:, 2560:3584].rearrange("p (h v) -> p h v", h=8)
        Sst = KVR[:, 3584:5632].bitcast(F32).rearrange("p (h v) -> p h v", h=8)
        klast = sb("klast", [128, 2, 128])
        kt1 = sb("kt1", [64, 2048], BF16)
        KTS = [KVR[0:64, 0:2048], kt1[:]]
        p1x = sb("p1x", [128, 512])
        P1F = [KVR[:, 5632:6656].bitcast(F32), KVR[:, 6656:7680].bitcast(F32), KVR[:, 7680:8704].bitcast(F32), p1x[:]]
        p1b = sb("p1b", [128, 4, 512], BF16)

        ident = cst[:, C_ID:C_ID + 128]
        maskreset = cst[:, C_MR:C_MR + 512]
        attmask = cst[0:64, C_AM:C_AM + 512]
        mask4 = cst[:, C_M4:C_M4 + 512]
        onesblk = cst[:, C_OB:C_OB + 128]
        perm = cst[:, C_PM:C_PM + 128]
        jmask = cst[:, C_JM:C_JM + 1]
        flag = cst[:, C_FL:C_FL + 1]
        rowmask = cst[0:64, C_RM:C_RM + 1]
        cbs = sb("cbs", [128, 8, 2])

        V_N1, V_N2, V_KVN, V_LB, V_GN, V_QN, V_KN = 0, 32, 64, 72, 88, 90, 92
        SK = [("S", h) for h in range(8)]

        S.dmas("sp", [(cst[:], consts[:, :])], writes=["cst"], sem_key="ld", bulk=True)
        S.op("dve", "memset", vrows[:], 0.0, writes=["vrows"])
        prs = [(vrows[V_N1:V_N1 + 32, :], norm1_g.rearrange("l (k p) -> (l k) p", p=128)),
               (vrows[V_N2:V_N2 + 32, :], norm2_g.rearrange("l (k p) -> (l k) p", p=128)),
               (vrows[V_KVN:V_KVN + 8, :], kv_norm_g.rearrange("(k p) -> k p", p=128)),
               (vrows[V_LB:V_LB + 16, :], hg_lb.rearrange("l (k p) -> (l k) p", p=128)),
               (vrows[V_GN:V_GN + 2, :], hg_gn[:, :])]
        for l in range(2):
            for a in range(2):
                prs.append((vrows[V_QN + l:V_QN + l + 1, a * 64:(a + 1) * 64], q_norm_g[l:l + 1, :]))
        for a in range(2):
            prs.append((vrows[V_KN:V_KN + 1, a * 64:(a + 1) * 64], k_norm_g.rearrange("(o d) -> o d", o=1)))
        S.dmas("sp", prs, writes=["vrows"], sem_key="ld2", bulk=True)
        prs = []
        for l in range(2):
            sv = sinks[l].rearrange("(m a i) -> a m i", m=2, a=2, i=4)
            for a in range(2):
                prs.append((esink[a * 64:(a + 1) * 64, l, :].rearrange("p (m i) -> p m i", m=2), sv[a:a + 1].to_broadcast([64, 2, 4])))
        S.dmas("sp", prs, writes=["esink"], sem_key="ld", bulk=True)
        S.op("act", "activation", esink[:], esink[:], AF.Exp, reads=["esink"], writes=["esink"])

        S.op("dve", "memset", cbs[:], 0.0, writes=["cbs"])
        S.op("dve", "memset", onesD[:], 1.0 / D, writes=["onesD"])
        S.op("dve", "memset", ones128[:], 1.0 / 128, writes=["ones128"])
        S.op("dve", "memset", onesb[:], 1.0, writes=["onesb"])
        S.op("dve", "memset", onesDb[:], 1.0 / D, writes=["onesDb"])
        S.op("dve", "memset", ones128b[:], 1.0 / 128, writes=["ones128b"])
        S.op("dve", "tensor_copy", onesblkb[:], cst[:, C_OB:C_OB + 128], reads=["cst"], writes=["onesblkb"])
        S.op("dve", "tensor_copy", permb[:], cst[:, C_PM:C_PM + 128], reads=["cst"], writes=["permb"])
        S.op("dve", "memset", epsc[:], EPS, writes=["epsc"])
        S.op("dve", "memset", epsl[:], 1e-7, writes=["epsl"])
        S.op("dve", "memset", ones17[:], 1.0, writes=["ones17"])
        S.op("dve", "tensor_copy", identb[:], ident, reads=["cst"], writes=["identb"])
        S.op("dve", "tensor_scalar", mask4f[:, 0:256], mask4[:, 0:256], flag, None, ALU.mult, reads=["cst"], writes=["mask4f"])
        S.op("dve", "tensor_copy", mask4f[:, 256:512], mask4[:, 256:512], reads=["cst"], writes=["mask4f"])

        pb, pk = PS()
        S.op("pe", "transpose", pb[:, 0:128], vrows[:], ident, reads=["vrows", "cst"], writes=[pk])
        S.op("dve", "tensor_copy", vcols[:], pb[:, 0:128], reads=[pk], writes=["vcols"])
        S.op("dve", "tensor_tensor", lbt[:, 1, :], vcols[:, V_LB:V_LB + 8], vcols[:, V_LB + 8:V_LB + 16], ALU.subtract, reads=["vcols"], writes=["lbt"])
        S.op("act", "activation", lbt[:, 1, :], lbt[:, 1, :], AF.Exp, reads=["lbt"], writes=["lbt"])
        S.op("dve", "tensor_scalar", lbt[:, 1, :], lbt[:, 1, :], 1.0, None, ALU.add, reads=["lbt"], writes=["lbt"])
        S.op("dve", "reciprocal", lbt[:, 1, :], lbt[:, 1, :], reads=["lbt"], writes=["lbt"])
        S.op("dve", "tensor_scalar", lbt[:, 0, :], lbt[:, 1, :], -1.0, 1.0, ALU.mult, ALU.add, reads=["lbt"], writes=["lbt"])
        S.op("dve", "tensor_tensor", lbt[:, 0, :], lbt[:, 0, :], lbt[:, 0, :], ALU.subtract, reads=["lbt"], writes=["lbt"])
        for l in range(2):
            S.op("dve", "tensor_scalar", lbAB[:, l, 0, :], lbt[:, l, :], -0.5, 0.5, ALU.mult, ALU.add, reads=["lbt"], writes=["lbAB"])
            S.op("dve", "tensor_tensor", lbAB[:, l, 1, :], lbAB[:, l, 0, :], lbt[:, l, :], ALU.add, reads=["lbt", "lbAB"], writes=["lbAB"])

        c0, c0k = FT()
        c1, c1k = FT()
        S.dmas("sp", [(c0[0:17, :], c17[:, 0:512]), (c1[0:17, :], c17[:, 512:1024])], writes=[c0k, c1k], sem_key="ld", bulk=True)
        S.op("act", "activation", c0[0:17, :], c0[0:17, :], AF.Silu, reads=[c0k], writes=[c0k])
        S.op("act", "activation", c1[0:17, :], c1[0:17, :], AF.Silu, reads=[c1k], writes=[c1k])
        pb, pk = PS()
        for k in range(8):
            src = (c0 if k < 4 else c1)[0:17, (k % 4) * 128:(k % 4 + 1) * 128]
            S.op("pe", "transpose", pb[:, k * 17:(k + 1) * 17], src, ident[0:17, 0:17], reads=[c0k, c1k, "cst"], writes=[pk])
        S.op("dve", "tensor_copy", cT[:].rearrange("p k s -> p (k s)"), pb[:, 0:136], reads=[pk], writes=["cT"])
        S.op("dve", "tensor_copy", cTb[:], cT[:], reads=["cT"], writes=["cTb"])

        def load_x():
          for b in range(NBLK):
            t = b // 4
            xa, xak = FT()
            xb_, xbk = FT()
            S.dmas("sp", [(xa[:], xp[b * 128:(b + 1) * 128, 0:512]), (xb_[:], xp[b * 128:(b + 1) * 128, 512:1024])],
                   writes=[xak, xbk], sem_key=xak)
            for g in range(2):
                src, srck = (xa, xak) if g == 0 else (xb_, xbk)
                pb, pk = PS()
                for kk in range(4):
                    S.op("pe", "transpose", pb[:, kk * 128:(kk + 1) * 128], src[:, kk * 128:(kk + 1) * 128], ident, reads=[srck, "cst"], writes=[pk])
                dst = xT[:, g * 4:(g + 1) * 4, b * 128:(b + 1) * 128]
                wr = [("xT", g * 4 + kk, t) for kk in range(4)] + ["SMPREGION"]
                if g == 0:
                    S.op("dve", "tensor_copy", dst, pb[:].rearrange("p (k t) -> p k t", k=4), reads=[pk], writes=wr)
                else:
                    S.op("act", "activation", dst, pb[:].rearrange("p (k t) -> p k t", k=4), AF.Copy, reads=[pk], writes=wr)

        actr = [0]

        def ada(wsrc, bsrc, ncolt, dst, dkey):
            for j in range(ncolt):
                wsl, wk = slot(actr[0] % 4)
                actr[0] += 1
                wv = wsl.rearrange("p (k c) -> p k c", k=8)
                S.dmas("pool", [(wv, wsrc.rearrange("(k p) c -> p k c", p=128)[:, :, j * 512:(j + 1) * 512])], writes=wk, sem_key=wk[0])
                br, bk = FT()
                S.dmas("sp", [(br[0:1, :], bsrc[j * 512:(j + 1) * 512].rearrange("(o c) -> o c", o=1))], writes=[bk], sem_key=bk)
                pb, pk = PS()
                for k in range(8):
                    S.op("pe", "matmul", pb[0:17, :], cTb[:, k, :], wv[:, k, :], start=(k == 0), stop=False, reads=wk + ["cTb"], writes=[pk])
                S.op("pe", "matmul", pb[0:17, :], ones17[:], br[0:1, :], start=False, stop=True, reads=[bk, "ones17"], writes=[pk])
                tm, tmk = FT()
                S.op("dve", "tensor_copy", tm[0:17, :], pb[0:17, :], reads=[pk], writes=[tmk])
                pb2, pk2 = PS()
                for fc in range(4):
                    S.op("pe", "transpose", pb2[:, fc * 17:(fc + 1) * 17], tm[0:17, fc * 128:(fc + 1) * 128], ident[0:17, 0:17], reads=[tmk, "cst"], writes=[pk2])
                S.op("dve", "tensor_copy", dst[:, j * 4:(j + 1) * 4, :].rearrange("p c s -> p (c s)"), pb2[:, 0:68], reads=[pk2], writes=[dkey])

        def mod_derive(l):
            for k in range(8):
                S.op("dve", "tensor_scalar", aT[:, 0, k, :], modT[:, 8 + k, :], 1.0, vcols[:, V_N1 + l * 8 + k:V_N1 + l * 8 + k + 1], ALU.add, ALU.mult,
                     reads=["modT", "vcols"], writes=["aT0"])
                S.op("dve", "tensor_scalar", aT[:, 1, k, :], modT[:, 32 + k, :], 1.0, vcols[:, V_N2 + l * 8 + k:V_N2 + l * 8 + k + 1], ALU.add, ALU.mult,
                     reads=["modT", "vcols"], writes=["aT1"])

        def norm_mod(t, acol, shcol, akey, skey, hslot=None):
            ts = slice(t * 512, (t + 1) * 512)
            if hslot is None:
                hslot = t
            hsl = slice(hslot * 512, (hslot + 1) * 512)
            hw = [("hT", hslot)] + (MIXK if (hslot == 1 and T >= 1024) else [])
            pb, pk = PS()
            for k in range(8):
                sq, sk = BT()
                S.op("act", "activation", sq[:], xT[:, k, ts], AF.Square, reads=[("xT", k, t)], writes=[sk])
                S.op("pe", "matmul", pb[:], onesDb[:], sq[:], start=(k == 0), stop=(k == 7), reads=[sk, "onesDb"], writes=[pk])
            rs, rk = FT()
            S.op("act", "activation", rs[:], pb[:], AF.Ln, bias=epsc[:, 0:1], scale=1.0, reads=[pk, "epsc"], writes=[rk])
            S.op("act", "activation", rs[:], rs[:], AF.Exp, scale=-0.5, reads=[rk], writes=[rk])
            for k in range(8):
                tm, tk = FT()
                S.op("dve", "scalar_tensor_tensor", tm[:], xT[:, k, ts], acol(k), rs[:], ALU.mult, ALU.mult, reads=[("xT", k, t), rk, akey], writes=[tk])
                S.op("act", "activation", hTall[:, k, hsl], tm[:], AF.Identity, bias=shcol(k), scale=1.0, reads=[tk, skey], writes=hw)

        def resid_update(t, fc, pb, pk, gcol, gkey):
            ts = slice(t * 512, (t + 1) * 512)
            S.op("dve", "scalar_tensor_tensor", xT[:, fc, ts], pb[:], gcol, xT[:, fc, ts], ALU.mult, ALU.add,
                 reads=[pk, ("xT", fc, t), gkey], writes=[("xT", fc, t)])

        dbgb = {}

        def dump(nm, ap, key, rows=128):
            if not dbg:
                return
            if nm not in dbgb:
                dbgb[nm] = sb("dbgb_" + nm, [128, 512])
            f, fk = dbgb[nm], ("dbgb", nm)
            S.op("dve", "tensor_copy", f[0:rows, :], ap, reads=[key], writes=[fk])
            S.dmas("sp", [(DBG[nm][0:rows, :], f[0:rows, :])], reads=[fk], sem_key=("dbg", nm), final=True)

        def load_w(pairs, keys):
            S.dmas("pool", pairs, writes=keys, sem_key=keys[0])

        def hgrn_stage(l, t, h, wh, wkeys, state_only, par):
            ts = slice(t * 512, (t + 1) * 512)
            A = lbAB[:, l, 0, h:h + 1]
            B = lbAB[:, l, 1, h:h + 1]
            hk = [("hT", 0)]
            X = {}
            kts = KTS[par]
            kdtok_ = kts[:, 0:1024].rearrange("p (c d) -> p c d", c=8)
            vtok_ = kts[:, 1024:2048].rearrange("p (c d) -> p c d", c=8)
            ktk = ("kts", par)
            tf, tfk = P1F[2 * par], ("p1f", 2 * par)
            tq, tqk = P1F[2 * par + 1], ("p1f", 2 * par + 1)
            iTb, ibk = p1b[:, 2 * par, :], ("p1b", 2 * par)
            tg, tgk = p1b[:, 2 * par + 1, :], ("p1b", 2 * par + 1)

            def proj(j):
                pb, pk = PS()
                for k in range(8):
                    S.op("pe", "matmul", pb[:], wh[:, k, j, :], hTall[:, k, 0:512], start=(k == 0), stop=(k == 7), reads=wkeys + hk, writes=[pk])
                return pb, pk

            def P1a():
                X["pf"] = proj(1)
                if state_only:
                    X["pi"] = proj(2)
                else:
                    S.dmas("sp", [(kts, kvs_scr[t * 8 + h])], reads=[("kvs", t, h)], writes=[ktk], sem_key=("kvsl", par))
                    X["pq"] = proj(0)
                    X["pg"] = proj(3)

            def P1b():
                pf, pfk = X["pf"]
                S.op("act", "activation", tf, pf[:], AF.Tanh, scale=0.5, reads=[pfk], writes=[tfk])
                if state_only:
                    pi, pik = X["pi"]
                    S.op("act", "activation", iTb, pi[:], AF.Copy, reads=[pik], writes=[ibk])
                if not state_only:
                    pq, pqk = X["pq"]
                    pg, pgk = X["pg"]
                    S.op("act", "activation", tq, pq[:], AF.Silu, reads=[pqk], writes=[tqk])
                    S.op("act", "activation", tg, pg[:], AF.Silu, reads=[pgk], writes=[tgk])

            def P2a():
                S.op("dve", "tensor_scalar", tf, tf, A, B, ALU.mult, ALU.add, reads=[tfk, "lbAB"], writes=[tfk])
                tl, tlk = FT()
                S.op("act", "activation", tl[:], tf, AF.Ln, bias=epsl[:, 0:1], scale=1.0, reads=[tfk, "epsl"], writes=[tlk])
                tk_, tkk = FT()
                S.op("act", "activation", tk_[:], tf, AF.Identity, bias=1.0, scale=-1.0, reads=[tfk], writes=[tkk])
                cum, cumk = FT()
                S.op("dve", "tensor_tensor_scan", cum[:], maskreset, tl[:], 0.0, ALU.mult, ALU.add, reads=["cst", tlk], writes=[cumk])
                cv3 = cum[:].rearrange("p (c j) -> p c j", j=64)
                e3, e3k = FT()
                S.op("act", "activation", e3[:], cum[:], AF.Exp, reads=[cumk], writes=[e3k])
                X.update(e3=(e3, e3k))
                if state_only:
                    d2, d2k = FT()
                    S.op("pool", "tensor_tensor", d2[:].rearrange("p (c j) -> p c j", j=64), cv3, cv3[:, :, 63:64].to_broadcast([128, 8, 64]), ALU.subtract,
                         reads=[cumk], writes=[d2k])
                    S.op("act", "activation", d2[:], d2[:], AF.Exp, scale=-1.0, reads=[d2k], writes=[d2k])
                    kd, kdk = BT()
                    S.op("dve", "tensor_tensor", kd[:], tk_[:], d2[:], ALU.mult, reads=[tkk, d2k], writes=[kdk])
                    X.update(kd=(kd, kdk))
                if not state_only:
                    S.op("pool", "tensor_copy", cbs[:, :, 1:2], cv3[:, :, 31:32], reads=[cumk], writes=["cbs"])
                    d1, d1k = FT()
                    S.op("dve", "tensor_tensor", d1[:].rearrange("p (c b j) -> p c b j", b=2, j=32), cum[:].rearrange("p (c b j) -> p c b j", b=2, j=32),
                         cbs[:].unsqueeze(3).to_broadcast([128, 8, 2, 32]), ALU.subtract, reads=[cumk, "cbs"], writes=[d1k])
                    e1, e1k = FT()
                    S.op("act", "activation", e1[:], d1[:], AF.Exp, reads=[d1k], writes=[e1k])
                    S.op("act", "activation", d1[:], d1[:], AF.Exp, scale=-1.0, reads=[d1k], writes=[d1k])
                    qq, qqk = BT()
                    S.op("dve", "tensor_tensor", qq[:], tq, e1[:], ALU.mult, reads=[tqk, e1k], writes=[qqk])
                    kk_, kkk = BT()
                    S.op("dve", "scalar_tensor_tensor", kk_[:], d1[:], 1e30, tk_[:], ALU.min, ALU.mult, reads=[tkk, d1k], writes=[kkk])
                    S.op("pool", "tensor_tensor", tl[:].rearrange("p (c j) -> p c j", j=64), cv3, cv3[:, :, 31:32].to_broadcast([128, 8, 64]), ALU.subtract,
                         reads=[cumk], writes=[tlk])
                    S.op("act", "activation", tl[:], tl[:], AF.Exp, scale=-1.0, reads=[tlk], writes=[tlk])
                    k32, k32k = BT()
                    S.op("dve", "scalar_tensor_tensor", k32[:], tl[:], 1.0, tk_[:], ALU.min, ALU.mult, reads=[tkk, tlk], writes=[k32k])
                    qe, qek = BT()
                    S.op("pool", "tensor_tensor", qe[:], tq, e3[:], ALU.mult, reads=[tqk, e3k], writes=[qek])
                    X.update(qq=(qq, qqk), kk=(kk_, kkk), k32=(k32, k32k), qe=(qe, qek), d1=(d1, d1k))

            def P2b():
                e3, e3k = X["e3"]
                if state_only:
                    kd, kdk = X["kd"]
                    pkd, pkdk = PS()
                    pv, pvk = PS()
                    pkd_b = pkd[:].bitcast(BF16)
                    pv_b = pv[:].bitcast(BF16)
                    for c in range(8):
                        S.op("pe", "transpose", pkd_b[0:64, c * 128:(c + 1) * 128], kd[:, c * 64:(c + 1) * 64], identb[:], reads=[kdk, "identb"], writes=[pkdk])
                    for c in range(8):
                        S.op("pe", "transpose", pv_b[0:64, c * 128:(c + 1) * 128], iTb[:, c * 64:(c + 1) * 64], identb[:], reads=[ibk, "identb"], writes=[pvk])
                    S.op("act", "activation", kts[:, 0:1024], pkd_b[0:64, :], AF.Copy, reads=[pkdk], writes=[ktk])
                    S.op("dve", "tensor_copy", kts[:, 1024:2048], pv_b[0:64, :], reads=[pvk, ktk], writes=[ktk])
                    S.dmas("sp", [(kvs_scr[t * 8 + h], kts)], reads=[ktk], writes=[("kvs", t, h)], sem_key=("kvsw", par))
                if not state_only:
                    qq, qqk = X["qq"]
                    kk_, kkk = X["kk"]
                    k32, k32k = X["k32"]
                    qe, qek = X["qe"]
                    patt, pattk = PS()
                    patt2, patt2k = PS()
                    for c in range(8):
                        cs = slice(c * 64, (c + 1) * 64)
                        S.op("pe", "matmul", patt[0:64, cs], kk_[:, cs], qq[:, cs], start=True, stop=True, reads=[kkk, qqk], writes=[pattk])
                    for c in range(8):
                        cs = slice(c * 64, (c + 1) * 64)
                        S.op("pe", "matmul", patt2[0:64, cs], k32[:, cs], qq[:, cs], start=True, stop=True, reads=[k32k, qqk], writes=[patt2k])
                    S.op("dve", "tensor_tensor", attsb, patt[0:64, :], attmask, ALU.mult, reads=[pattk, "cst"], writes=["attsb"])
                    au = attsb.rearrange("p (c b j) -> p c b j", b=2, j=32)[:, :, 1, :]
                    pu = patt2[0:64, :].rearrange("p (c b j) -> p c b j", b=2, j=32)[:, :, 1, :]
                    S.op("dve", "scalar_tensor_tensor", au, pu, rowmask, au, ALU.mult, ALU.add, reads=[patt2k, "cst", "attsb"], writes=["attsb"])
                    po, pok = PL()
                skey = ("S", h)
                sbkey = ("Sbf", h)
                psSs = []
                for c in range(8):
                    bS = banks[4 + c // 4]
                    psSs.append((bS[:, (c % 4) * 128:(c % 4 + 1) * 128], ("ps", 4 + c // 4)))
                    S.op("pe", "matmul", psSs[c][0], kdtok_[:, c, :], vtok_[:, c, :], start=True, stop=True, reads=[ktk], writes=[psSs[c][1]])
                if not state_only:
                    for c in range(8):
                        cs = slice(c * 64, (c + 1) * 64)
                        S.op("pe", "matmul", po[:, cs], vtok_[:, c, :], attsb[:, cs], start=(c == 0), stop=False, skip_group_check=True,
                             reads=[ktk, "attsb"], writes=[pok])
                for c in range(8):
                    cs = slice(c * 64, (c + 1) * 64)
                    if not state_only:
                        S.op("pe", "matmul", po[:, cs], Sbf[:, h, :], qe[:, cs], start=False, stop=True, skip_group_check=True, reads=[sbkey, qek], writes=[pok])
                    S.op("dve", "scalar_tensor_tensor", Sst[:, h, :], Sst[:, h, :], e3[:, c * 64 + 63:c * 64 + 64], psSs[c][0], ALU.mult, ALU.add,
                         reads=[psSs[c][1], e3k, skey], writes=[skey])
                    if not state_only:
                        S.op("act", "activation", Sbf[:, h, :], Sst[:, h, :], AF.Copy, reads=[skey], writes=[sbkey])
                if state_only:
                    return
                d1, d1k = X["d1"]
                osq, osk = BT()
                S.op("act", "activation", osq[:], po[:], AF.Square, reads=[pok], writes=[osk])
                pn, pnk = PS()
                S.op("pe", "matmul", pn[:], ones128b[:], osq[:], start=True, stop=True, reads=[osk, "ones128b"], writes=[pnk])
                rs, rk = d1, d1k
                S.op("act", "activation", rs[:], pn[:], AF.Ln, bias=epsc[:, 0:1], scale=1.0, reads=[pnk, "epsc"], writes=[rk])
                S.op("act", "activation", rs[:], rs[:], AF.Exp, scale=-0.5, reads=[rk], writes=[rk])
                S.op("dve", "tensor_tensor", rs[:], po[:], rs[:], ALU.mult, reads=[pok, rk], writes=[rk])
                S.op("dve", "scalar_tensor_tensor", mixT[:, h, :], rs[:], vcols[:, V_GN + l:V_GN + l + 1], tg, ALU.mult, ALU.mult,
                     reads=[rk, tgk, "vcols"], writes=[("mix", h)] + MIXW)
            return P1a, P1b, P2a, P2b

        hctr = [0]

        def hgrn_pass(l, state_only, wout=None, wokeys=None):
            psn[0] = 4
            bctr[0] = 0
            _hgrn_pass(l, state_only, wout, wokeys)
            psn[0] = 6

        def _hgrn_pass(l, state_only, wout=None, wokeys=None):
            loaded = {}

            def issue(t, h):
                if t >= NT or (t, h) in loaded:
                    return
                wsl, wkeys = slot(2 + hctr[0] % 2)
                hctr[0] += 1
                wh = wsl.rearrange("p (k j c) -> p k j c", k=8, j=4)
                load_w([(wsl.rearrange("p (k c) -> p k c", k=8), hg_w_in[l, h].rearrange("(k p) c -> p k c", p=128))], wkeys)
                loaded[(t, h)] = (wh, wkeys)
            issue(0, 0)
            for t in range(NT):
                norm_mod(t, lambda k: aT[:, 0, k, 0:1], lambda k: modT[:, k, 0:1], "aT0", "modT", hslot=0)
                stages = {}
                wh, wkeys = loaded[(t, 0)]
                stages[0] = hgrn_stage(l, t, 0, wh, wkeys, state_only, 0)
                issue(t, 1)
                stages[0][0]()
                stages[0][1]()
                for h in range(8):
                    if h + 1 < 8:
                        wh, wkeys = loaded[(t, h + 1)]
                        stages[h + 1] = hgrn_stage(l, t, h + 1, wh, wkeys, state_only, (h + 1) % 2)
                        stages[h + 1][0]()
                    stages[h][2]()
                    if h + 1 < 8:
                        stages[h + 1][1]()
                        if h + 2 < 8:
                            issue(t, h + 2)
                        else:
                            issue(t + 1, 0)
                    stages[h][3]()
                if not state_only:
                    if dbg and l == 0 and t == 0:
                        pass
                    for fc in range(8):
                        pb, pk = PS()
                        for k in range(8):
                            S.op("pe", "matmul", pb[:], wout[:, k, fc * 128:(fc + 1) * 128], mixT[:, k, :], start=(k == 0), stop=(k == 7),
                                 reads=wokeys + [("mix", k)], writes=[pk])
                        resid_update(t, fc, pb, pk, modT[:, 16 + fc, 0:1], "modT")

        def exchange_state(l):
            S.dmas("sp", [(xs_scr[l][:, :], Sst.rearrange("p h v -> p (h v)"))], reads=SK, writes=[("xs", l)], sem_key=("xs", l))
            S.dma("pool", lambda e, l=l: [e.collective_compute("AllGather", ALU.bypass, replica_groups=PAIRS, ins=[xs_scr[l][:, :]], outs=[xg_scr[l][:, :]])],
                  reads=[("xs", l)], writes=[("xg", l)], sem_key=("cc", l), inc=1)
            S.dmas("sp", [(Sst.rearrange("p h v -> p (h v)"), xg_scr[l][0:128, :])], reads=[("xg", l)], writes=SK, sem_key=("xgl", l))
            for h in range(8):
                S.op("dve", "tensor_scalar", Sst[:, h, :], Sst[:, h, :], flag, None, ALU.mult, reads=[("S", h), "cst"], writes=[("S", h)])
                S.op("act", "activation", Sbf[:, h, :], Sst[:, h, :], AF.Copy, reads=[("S", h)], writes=[("Sbf", h)])

        def hgrn_layer(l):
            S.op("dve", "memset", Sst, 0.0, writes=SK)
            hgrn_pass(l, True)
            exchange_state(l)
            wsl, wokeys = slot(0, 2)
            wout = wsl.rearrange("p (k c) -> p k c", k=8)
            load_w([(wout, hg_w_out[l].rearrange("(k p) c -> p k c", p=128))], wokeys)
            hgrn_pass(l, False, wout, wokeys)
            S.dmas("sp", [(sp_state[l].rearrange("h k v -> k h v"), Sst)], reads=SK, sem_key=("spst", l), final=True)

        mctr = [0]

        def mlp(l):
            for t in range(NT):
                norm_mod(t, lambda k: aT[:, 1, k, 0:1], lambda k: modT[:, 24 + k, 0:1], "aT1", "modT")
            mloaded = {}

            def missue(g):
                if g >= 8 or g in mloaded:
                    return
                wsl, wk = slot(2 * (mctr[0] % 2), 2)
                mctr[0] += 1
                wup = wsl[:, 0:4096].rearrange("p (k c) -> p k c", k=8)
                wdn = wsl[:, 4096:8192].rearrange("p (k c) -> p k c", k=4)
                load_w([(wup, w_up[l].rearrange("(k p) c -> p k c", p=128)[:, :, g * 512:(g + 1) * 512]),
                        (wdn, w_down[l][g * 512:(g + 1) * 512, :].rearrange("(k p) c -> p k c", p=128))], wk)
                mloaded[g] = (wup, wdn, wk)
            missue(0)
            for hg in range(8):
                missue(hg + 1)
                wup, wdn, wk = mloaded[hg]
                for t in range(NT):
                    ts = slice(t * 512, (t + 1) * 512)
                    hids = []
                    for hc in range(4):
                        pb, pk = PS()
                        for k in range(8):
                            S.op("pe", "matmul", pb[:], wup[:, k, hc * 128:(hc + 1) * 128], hTall[:, k, ts], start=(k == 0), stop=(k == 7),
                                 reads=wk + [("hT", t)], writes=[pk])
                        r, rk = FT()
                        S.op("act", "activation", r[:], pb[:], AF.Relu, reads=[pk], writes=[rk])
                        hd, hdk = BT()
                        S.op("pool", "tensor_tensor", hd[:], r[:], r[:], ALU.mult, reads=[rk], writes=[hdk])
                        hids.append((hd, hdk))
                    for fc in range(8):
                        pb, pk = PS()
                        for hc in range(4):
                            hd, hdk = hids[hc]
                            S.op("pe", "matmul", pb[:], wdn[:, hc, fc * 128:(fc + 1) * 128], hd[:], start=(hc == 0), stop=(hc == 3),
                                 reads=wk + [hdk], writes=[pk])
                        resid_update(t, fc, pb, pk, modT[:, 40 + fc, 0:1], "modT")

        def load_tabs(t):
            S.dmas("sp", [(tabc[:], costab[:, t * 512:(t + 1) * 512]), (tabs[:], sintab[:, t * 512:(t + 1) * 512])], writes=["tab"], sem_key="tab")

        def qk_norm_rope(pb, pk, gcol, out_ap, out_keys, f32_out=None, f32_key=None):
            sq, sk = BT()
            S.op("act", "activation", sq[:], pb[:], AF.Square, reads=[pk], writes=[sk])
            pn, pnk = PS()
            S.op("pe", "matmul", pn[:], onesblkb[:], sq[:], start=True, stop=True, reads=[sk, "onesblkb"], writes=[pnk])
            rs, rk = FT()
            S.op("act", "activation", rs[:], pn[:], AF.Ln, bias=epsc[:, 0:1], scale=1.0 / 64, reads=[pnk, "epsc"], writes=[rk])
            S.op("act", "activation", rs[:], rs[:], AF.Exp, scale=-0.5, reads=[rk], writes=[rk])
            qg, qgk = BT()
            S.op("dve", "tensor_scalar", qg[:], pb[:], gcol, None, ALU.mult, reads=[pk, "vcols"], writes=[qgk])
            pp, ppk = PS()
            S.op("pe", "matmul", pp[:], permb[:], qg[:], start=True, stop=True, reads=[qgk, "permb"], writes=[ppk])
            ta, tak = FT()
            S.op("pool", "tensor_tensor", ta[:], qg[:], tabc[:], ALU.mult, reads=[qgk, "tab"], writes=[tak])
            tb, tbk = FT()
            S.op("dve", "tensor_tensor", tb[:], pp[:], tabs[:], ALU.mult, reads=[ppk, "tab"], writes=[tbk])
            S.op("pool", "tensor_tensor", ta[:], ta[:], tb[:], ALU.add, reads=[tak, tbk], writes=[tak])
            if f32_out is not None:
                S.op("dve", "tensor_tensor", f32_out, ta[:], rs[:], ALU.mult, reads=[tak, rk], writes=[f32_key])
                S.op("act", "activation", out_ap, f32_out, AF.Copy, reads=[f32_key], writes=out_keys)
            else:
                S.op("dve", "tensor_tensor", out_ap, ta[:], rs[:], ALU.mult, reads=[tak, rk], writes=out_keys)

        def kv_phase():
            if do_sample:
                S.op("dve", "tensor_copy", kvmodT[:, :, 0], kvmodP[:], reads=["kvmodP"], writes=["kvmodT"])
                S.op("dve", "tensor_copy", aT[:, 2, :, 0], aKVP[:], reads=["aKVP"], writes=["aT2"])
            else:
                ada(kv_w_ada, kv_b_ada, 4, kvmodT, "kvmodT")
                for k in range(8):
                    S.op("dve", "tensor_scalar", aT[:, 2, k, :], kvmodT[:, 8 + k, :], 1.0, vcols[:, V_KVN + k:V_KVN + k + 1], ALU.add, ALU.mult,
                         reads=["kvmodT", "vcols"], writes=["aT2"])
            wsl, wkk = slot(0)
            wkv = wsl.rearrange("p (k c) -> p k c", k=8)
            load_w([(wkv, w_kv.rearrange("(k p) c -> p k c", p=128))], wkk)
            gk = vcols[:, V_KN:V_KN + 1]
            ALIAS = SK + [("Sbf", h) for h in range(8)] + ["kdtok", "vtok", "attsb", ("kts", 0)] + [("p1f", i_) for i_ in range(3)]
            for t in range(NT):
                ts = slice(t * 512, (t + 1) * 512)
                norm_mod(t, lambda k: aT[:, 2, k, 0:1], lambda k: kvmodT[:, k, 0:1], "aT2", "kvmodT", hslot=0)
                load_tabs(t)
                for m in range(2):
                    pb, pk = PS()
                    for k in range(8):
                        S.op("pe", "matmul", pb[:], wkv[:, k, m * 128:(m + 1) * 128], hTall[:, k, 0:512], start=(k == 0), stop=(k == 7),
                             reads=wkk + [("hT", 0)], writes=[pk])
                    dst = KT[:, m, 128 + t * 512:128 + (t + 1) * 512]
                    if t == NT - 1:
                        kf, kfk = FT()
                        qk_norm_rope(pb, pk, gk, dst, [("KT", m, t)] + ALIAS, f32_out=kf[:], f32_key=kfk)
                        S.op("pool", "tensor_copy", klast[:, m, :], kf[:, 384:512], reads=[kfk], writes=[("klast", m)])
                    else:
                        qk_norm_rope(pb, pk, gk, dst, [("KT", m, t)] + ALIAS)
                for b in range(4):
                    blk = t * 4 + b
                    pb, pk = PS()
                    for k in range(8):
                        S.op("pe", "matmul", pb[:, 0:256], hTall[:, k, b * 128:(b + 1) * 128], wkv[:, k, 256:512],
                             start=(k == 0), stop=(k == 7), reads=wkk + [("hT", 0)], writes=[pk])
                    S.op("act", "activation", Vtok[:, blk + 1, :], pb[:, 0:256], AF.Copy, reads=[pk], writes=[("V", blk + 1)] + ALIAS)
                    if blk == NBLK - 1:
                        vf, vfk = FT()
                        S.op("dve", "tensor_copy", vf[:, 0:256], pb[:, 0:256], reads=[pk], writes=[vfk])
                        S.dmas("sp", [(vp[:, :], vf[:, 0:256])], reads=[vfk], sem_key="vp", final=True)
            ko, kok = FT()
            pb, pk = PS()
            for m in range(2):
                S.op("pe", "transpose", pb[:, m * 128:(m + 1) * 128], klast[:, m, :], ident, reads=[("klast", m), "cst"], writes=[pk])
            S.op("dve", "tensor_copy", ko[:, 0:256], pb[:, 0:256], reads=[pk], writes=[kok])
            S.dmas("sp", [(kp[:, :], ko[:, 0:256])], reads=[kok], sem_key="kp", final=True)
            tlast = NT - 1
            S.dmas("sp", [(kv_scr[:, 0:256].rearrange("p (m t) -> p m t", m=2), KT[:, :, T:T + 128]), (kv_scr[:, 256:512], Vtok[:, NBLK, :])],
                   reads=[("KT", 0, tlast), ("KT", 1, tlast), ("V", NBLK)], writes=["kvscr"], sem_key="kvscr")
            S.dma("pool", lambda e: [e.collective_compute("AllGather", ALU.bypass, replica_groups=PAIRS, ins=[kv_scr[:, :]], outs=[kvg_scr[:, :]])],
                  reads=["kvscr"], writes=["kvg"], sem_key="cckv", inc=1)
            S.dmas("sp", [(KT[:, :, 0:128], kvg_scr[0:128, 0:256].rearrange("p (m t) -> p m t", m=2)), (Vtok[:, 0, :], kvg_scr[0:128, 256:512])],
                   reads=["kvg"], writes=[("KTpre",), ("V", 0)], sem_key="kvgl")

        def attn_layer(l):
            j = l - 2
            wsl, wqk = slot(0, 2)
            wq = wsl.rearrange("p (k c) -> p k c", k=8)
            load_w([(wq, w_q[j].rearrange("(k p) c -> p k c", p=128))], wqk)
            wsl2, wok = slot(2, 2)
            wo = wsl2.rearrange("p (c f) -> p c f", c=8)
            load_w([(wo, w_o[j].rearrange("(c p) f -> p c f", p=128))], wok)
            gq = vcols[:, V_QN + j:V_QN + j + 1]
            for t in range(NT):
                ts = slice(t * 512, (t + 1) * 512)
                norm_mod(t, lambda k: aT[:, 0, k, 0:1], lambda k: modT[:, k, 0:1], "aT0", "modT", hslot=0)
                load_tabs(t)
                for c in range(8):
                    pb, pk = PS()
                    for k in range(8):
                        S.op("pe", "matmul", pb[:], wq[:, k, c * 128:(c + 1) * 128], hTall[:, k, 0:512], start=(k == 0), stop=(k == 7),
                             reads=wqk + [("hT", 0)], writes=[pk])
                    qT, qTk = BT()
                    qk_norm_rope(pb, pk, gq, qT[:], [qTk])
                    m = c // 4
                    st_ = {}

                    def S1(qb):
                        n = t * 4 + qb
                        qs = slice(qb * 128, (qb + 1) * 128)
                        kreads = [("KTpre",)] if n == 0 else [("KT", m, (n - 1) // 4)]
                        kreads.append(("KT", m, n // 4))
                        pbanks = [PS(), PS()]
                        for a in range(2):
                            rows = slice(a * 64, (a + 1) * 64)
                            pss, pssk = pbanks[a]
                            for kb in range(2):
                                ko_ = (n + kb) * 128
                                S.op("pe", "matmul", pss[:, kb * 128:(kb + 1) * 128], KT[rows, m, ko_:ko_ + 128], qT[rows, qs],
                                     start=True, stop=True, reads=kreads + [qTk], writes=[pssk])
                        st_[qb] = {"pbanks": pbanks}

                    def S2(qb):
                        n = t * 4 + qb
                        pbanks = st_[qb]["pbanks"]
                        pe_, pek = FT()
                        pev = pe_[:].rearrange("p (kb a q) -> p kb a q", kb=2, a=2)
                        for a in range(2):
                            pss, pssk = pbanks[a]
                            S.op("act", "activation", pev[:, :, a, :], pss[:, 0:256].rearrange("p (kb q) -> p kb q", kb=2), AF.Exp, scale=SCALE,
                                 reads=[pssk], writes=[pek])
                        pT, pTk = BT()
                        mk = mask4f[:] if n == 0 else mask4
                        S.op("pool", "tensor_tensor", pT[:], pe_[:], mk, ALU.mult, reads=[pek, "cst", "mask4f"], writes=[pTk])
                        st_[qb]["pT"] = (pT, pTk)

                    def S3(qb):
                        n = t * 4 + qb
                        pT, pTk = st_[qb]["pT"]
                        po, pok = PS()
                        S.op("pe", "matmul", po[:, 0:256], Vtok[:, n, m * 128:(m + 1) * 128], pT[:, 0:256], start=True, stop=False, reads=[("V", n), pTk], writes=[pok])
                        S.op("pe", "matmul", po[:, 0:256], Vtok[:, n + 1, m * 128:(m + 1) * 128], pT[:, 256:512], start=False, stop=True, reads=[("V", n + 1), pTk], writes=[pok])
                        S.op("pe", "matmul", po[:, 256:512], onesb[:], pT[:, 0:256], start=True, stop=False, reads=["onesb", pTk], writes=[pok])
                        S.op("pe", "matmul", po[:, 256:512], onesb[:], pT[:, 256:512], start=False, stop=True, reads=["onesb", pTk], writes=[pok])
                        st_[qb]["po"] = (po, pok)

                    def S4(qb):
                        qs = slice(qb * 128, (qb + 1) * 128)
                        po, pok = st_[qb]["po"]
                        for a in range(2):
                            rows = slice(a * 64, (a + 1) * 64)
                            rk_ = ("rd", a)
                            S.op("act", "activation", rd[rows, :], po[rows, 256 + a * 128:256 + (a + 1) * 128], AF.Ln, bias=esink[rows, j, c:c + 1], scale=1.0,
                                 reads=[pok, "esink"], writes=[rk_])
                            S.op("act", "activation", rd[rows, :], rd[rows, :], AF.Exp, scale=-1.0, reads=[rk_], writes=[rk_])
                            S.op("dve", "tensor_tensor", mixT[rows, c, qs], po[rows, a * 128:(a + 1) * 128], rd[rows, :], ALU.mult, reads=[pok, rk_], writes=[("mix", c)] + MIXW)
                    S1(0)
                    S2(0)
                    for qb in range(4):
                        if qb + 1 < 4:
                            S1(qb + 1)
                        S3(qb)
                        if qb + 1 < 4:
                            S2(qb + 1)
                        S4(qb)
                for fc in range(8):
                    pb, pk = PS()
                    for c in range(8):
                        S.op("pe", "matmul", pb[:], wo[:, c, fc * 128:(fc + 1) * 128], mixT[:, c, :], start=(c == 0), stop=(c == 7),
                             reads=wok + [("mix", c)], writes=[pk])
                    resid_update(t, fc, pb, pk, modT[:, 16 + fc, 0:1], "modT")


        SMPW = 9216
        if T >= 2048:
            smp = xT[:].rearrange("p k t -> p (k t)")[:, 0:SMPW]
        else:
            smp = sb("smp", [128, SMPW])[:]
        soff = [0]

        def salloc(words, shape=None, dt=F32):
            a = smp[:, soff[0]:soff[0] + words]
            soff[0] += words
            assert soff[0] <= SMPW
            if dt == BF16:
                a = a.bitcast(BF16)
            return a

        def sop(eng, name, *args, reads=(), writes=(), **kw):
            return S.op(eng, name, *args, reads=list(reads) + ["SMPREGION"], writes=writes, **kw)

        def sdmas(queue, pairs, reads=(), writes=(), **kw):
            return S.dmas(queue, pairs, reads=list(reads) + ["SMPREGION"], writes=writes, **kw)

        xsT = salloc(128).rearrange("p (k s) -> p k s", k=8)
        hsT = salloc(64, dt=BF16).rearrange("p (k s) -> p k s", k=8)
        mixsT = salloc(64, dt=BF16).rearrange("p (k s) -> p k s", k=8)
        qsT = salloc(128).rearrange("p (k s) -> p k s", k=8)
        qsTb = salloc(64, dt=BF16).rearrange("p (k s) -> p k s", k=8)
        qpad = [salloc(64, dt=BF16).rearrange("p (k s) -> p k s", k=8) for _ in range(2)]
        ksT = salloc(32).rearrange("p (m s) -> p m s", m=2)
        vsT = salloc(32).rearrange("p (m s) -> p m s", m=2)
        NSF = 16
        sfs = [salloc(128) for _ in range(NSF)]
        sfc = [0]

        def SF():
            i = sfc[0]
            sfc[0] = (i + 1) % NSF
            return sfs[i], ("sf", i)
        NSB = 8
        sbs = [salloc(64, dt=BF16) for _ in range(NSB)]
        sbc = [0]

        def SB():
            i = sbc[0]
            sbc[0] = (i + 1) % NSB
            return sbs[i], ("sb", i)
        sin_b = [salloc(512).rearrange("p (s v) -> p s v", s=4) for _ in range(2)]
        sout_b = [salloc(512).rearrange("p (s v) -> p s v", s=4) for _ in range(2)]
        km = salloc(1024, dt=BF16)[0:16, :].rearrange("p (s d) -> p s d", s=16)
        ktv = salloc(128, dt=BF16)[0:16, :].rearrange("p (j d) -> p j d", j=2)
        ckT = salloc(512, dt=BF16).rearrange("p (m s j) -> p m s j", m=2, s=4)
        cvt = salloc(512, dt=BF16).rearrange("p (s d) -> p s d", s=4)
        ckin = [salloc(256) for _ in range(2)]
        kvnew = salloc(512)[0:16, :]
        pTs = [salloc(64, dt=BF16) for _ in range(2)]
        tab16 = salloc(32)

        def s_norm_mod(ai, shT, akey, skey):
            sq, sk = SF()
            sop("act", "activation", sq, xsT.rearrange("p k s -> p (k s)"), AF.Square, reads=["xsT"], writes=[sk])
            pb, pk = PS()
            for k in range(8):
                sop("pe", "matmul", pb[:, 0:16], onesD[:], sq[:, k * 16:(k + 1) * 16], start=(k == 0), stop=(k == 7), reads=[sk, "onesD"], writes=[pk])
            rs, rk = SF()
            sop("act", "activation", rs[:, 0:16], pb[:, 0:16], AF.Ln, bias=epsc[:, 0:1], scale=1.0, reads=[pk, "epsc"], writes=[rk])
            sop("act", "activation", rs[:, 0:16], rs[:, 0:16], AF.Exp, scale=-0.5, reads=[rk], writes=[rk])
            t1, t1k = SF()
            t13 = t1.rearrange("p (k s) -> p k s", k=8)
            sop("dve", "tensor_tensor", t13, xsT, rs[:, 0:16].unsqueeze(1).to_broadcast([128, 8, 16]), ALU.mult, reads=["xsT", rk], writes=[t1k])
            sop("dve", "tensor_tensor", t13, t13, aT[:, ai, :, 1:17], ALU.mult, reads=[t1k, akey], writes=[t1k])
            sop("dve", "tensor_tensor", hsT, t13, shT, ALU.add, reads=[t1k, skey], writes=["hsT"])

        def s_resid(pb, pk, gT, gkey):
            t1, t1k = SF()
            t13 = t1.rearrange("p (k s) -> p k s", k=8)
            sop("dve", "tensor_tensor", t13, pb[:, 0:128].rearrange("p (k s) -> p k s", k=8), gT, ALU.mult, reads=[pk, gkey], writes=[t1k])
            sop("dve", "tensor_tensor", xsT, xsT, t13, ALU.add, reads=["xsT", t1k], writes=["xsT"])

        def s_proj_out(w3, wkeys, srcT, srckey, gT, gkey):
            pb, pk = PS()
            for fc in range(8):
                for k in range(8):
                    sop("pe", "matmul", pb[:, fc * 16:(fc + 1) * 16], w3[:, k, fc * 128:(fc + 1) * 128], srcT[:, k, :], start=(k == 0), stop=(k == 7),
                        reads=wkeys + [srckey], writes=[pk])
            s_resid(pb, pk, gT, gkey)

        def s_rope(pb, pk, gcol, out_f32, okey, n=16):
            sq, sk = SF()
            sop("act", "activation", sq[:, 0:n], pb, AF.Square, reads=[pk], writes=[sk])
            pn, pnk = PS()
            sop("pe", "matmul", pn[:, 0:n], onesblk, sq[:, 0:n], start=True, stop=True, reads=[sk, "cst"], writes=[pnk])
            rs, rk = SF()
            sop("act", "activation", rs[:, 0:n], pn[:, 0:n], AF.Ln, bias=epsc[:, 0:1], scale=1.0 / 64, reads=[pnk, "epsc"], writes=[rk])
            sop("act", "activation", rs[:, 0:n], rs[:, 0:n], AF.Exp, scale=-0.5, reads=[rk], writes=[rk])
            qg, qgk = SF()
            sop("dve", "tensor_scalar", qg[:, 0:n], pb, gcol, None, ALU.mult, reads=[pk, "vcols"], writes=[qgk])
            pp, ppk = PS()
            sop("pe", "matmul", pp[:, 0:n], perm, qg[:, 0:n], start=True, stop=True, reads=[qgk, "cst"], writes=[ppk])
            ta, tak = SF()
            sop("dve", "tensor_tensor", ta[:, 0:n], qg[:, 0:n], tab16[:, 0:16], ALU.mult, reads=[qgk, "tab16"], writes=[tak])
            tb, tbk = SF()
            sop("dve", "tensor_tensor", tb[:, 0:n], pp[:, 0:n], tab16[:, 16:32], ALU.mult, reads=[ppk, "tab16"], writes=[tbk])
            sop("dve", "tensor_tensor", ta[:, 0:n], ta[:, 0:n], tb[:, 0:n], ALU.add, reads=[tak, tbk], writes=[tak])
            sop("dve", "tensor_tensor", out_f32, ta[:, 0:n], rs[:, 0:n], ALU.mult, reads=[tak, rk], writes=[okey])

        def s_hgrn_head(l, h, wh, wkeys):
            A = lbAB[:, l, 0, h:h + 1]
            B = lbAB[:, l, 1, h:h + 1]
            pb, pk = PS()
            for j in range(4):
                for k in range(8):
                    sop("pe", "matmul", pb[:, j * 16:(j + 1) * 16], wh[:, k, j, :], hsT[:, k, :], start=(k == 0), stop=(k == 7), reads=wkeys + ["hsT"], writes=[pk])
            qs_, qsk = SF()
            sop("act", "activation", qs_[:, 0:16], pb[:, 0:16], AF.Silu, reads=[pk], writes=[qsk])
            gs_, gsk = SF()
            sop("act", "activation", gs_[:, 0:16], pb[:, 48:64], AF.Silu, reads=[pk], writes=[gsk])
            fg, fgk = SF()
            sop("act", "activation", fg[:, 0:16], pb[:, 16:32], AF.Tanh, scale=0.5, reads=[pk], writes=[fgk])
            kv2, kv2k = SF()
            sop("act", "activation", kv2[:, 16:32], pb[:, 32:48], AF.Copy, reads=[pk], writes=[kv2k])
            sop("dve", "tensor_scalar", fg[:, 0:16], fg[:, 0:16], A, B, ALU.mult, ALU.add, reads=[fgk, "lbAB"], writes=[fgk])
            sop("dve", "tensor_scalar", kv2[:, 0:16], fg[:, 0:16], -1.0, 1.0, ALU.mult, ALU.add, reads=[fgk, kv2k], writes=[kv2k])
            pt, ptk = PS()
            sop("pe", "transpose", pt[0:16, 0:128], kv2[:, 0:16], ident, reads=[kv2k, "cst"], writes=[ptk])
            sop("pe", "transpose", pt[0:16, 128:256], kv2[:, 16:32], ident, reads=[kv2k, "cst"], writes=[ptk])
            sop("dve", "tensor_copy", ktv.rearrange("p j d -> p (j d)"), pt[0:16, 0:256], reads=[ptk], writes=["ktv"])
            sop("dve", "tensor_tensor", km, ktv[:, 0, :].unsqueeze(1).to_broadcast([16, 16, 128]),
                identb[0:16, 0:16].unsqueeze(2).to_broadcast([16, 16, 128]), ALU.mult, reads=["ktv", "identb"], writes=["km"])
            po, pok = PL()
            for bi in range(4):
                si_, so_ = sin_b[bi % 2], sout_b[bi % 2]
                sik, sok = ("sin", bi % 2), ("sout", bi % 2)
                sdmas("sp", [(si_, st_in[l, bi * 4:(bi + 1) * 4, h].rearrange("s k v -> k s v"))], writes=[sik], sem_key=sik)
                for s4 in range(4):
                    s_ = bi * 4 + s4
                    pkv, pkvk = PS()
                    sop("pe", "matmul", pkv[:, 0:128], km[:, s_, :], ktv[:, 1, :], start=True, stop=True, reads=["km", "ktv"], writes=[pkvk])
                    sop("dve", "scalar_tensor_tensor", so_[:, s4, :], si_[:, s4, :], fg[:, s_:s_ + 1], pkv[:, 0:128], ALU.mult, ALU.add,
                        reads=[sik, fgk, pkvk], writes=[sok])
                    sop("pe", "matmul", po[:, s_:s_ + 1], so_[:, s4, :], qs_[:, s_:s_ + 1], start=True, stop=True, reads=[sok, qsk], writes=[pok])
                sdmas("sp", [(ss[l, bi * 4:(bi + 1) * 4, h].rearrange("s k v -> k s v"), so_)], reads=[sok], sem_key=sok, final=True)
            osq, osk = SF()
            sop("act", "activation", osq[:, 0:16], po[:, 0:16], AF.Square, reads=[pok], writes=[osk])
            pn, pnk = PS()
            sop("pe", "matmul", pn[:, 0:16], ones128[:], osq[:, 0:16], start=True, stop=True, reads=[osk, "ones128"], writes=[pnk])
            rs, rk = SF()
            sop("act", "activation", rs[:, 0:16], pn[:, 0:16], AF.Ln, bias=epsc[:, 0:1], scale=1.0, reads=[pnk, "epsc"], writes=[rk])
            sop("act", "activation", rs[:, 0:16], rs[:, 0:16], AF.Exp, scale=-0.5, reads=[rk], writes=[rk])
            sop("dve", "tensor_tensor", rs[:, 0:16], po[:, 0:16], rs[:, 0:16], ALU.mult, reads=[pok, rk], writes=[rk])
            sop("dve", "scalar_tensor_tensor", mixsT[:, h, :], rs[:, 0:16], vcols[:, V_GN + l:V_GN + l + 1], gs_[:, 0:16], ALU.mult, ALU.mult,
                reads=[rk, gsk, "vcols"], writes=["mixsT"])

        def s_hgrn_layer(l):
            s_norm_mod(0, modT[:, 0:8, 1:17], "aT0", "modT")
            hl = {}

            def hissue(h):
                if h >= 8 or h in hl:
                    return
                wsl, wkeys = slot(2 + hctr[0] % 2)
                hctr[0] += 1
                wh = wsl.rearrange("p (k j c) -> p k j c", k=8, j=4)
                load_w([(wsl.rearrange("p (k c) -> p k c", k=8), hg_w_in[l, h].rearrange("(k p) c -> p k c", p=128))], wkeys)
                hl[h] = (wh, wkeys)
            hissue(0)
            for h in range(8):
                hissue(h + 1)
                wh, wkeys = hl[h]
                s_hgrn_head(l, h, wh, wkeys)
            wsl, wokeys = slot(0, 2)
            wout = wsl.rearrange("p (k c) -> p k c", k=8)
            load_w([(wout, hg_w_out[l].rearrange("(k p) c -> p k c", p=128))], wokeys)
            s_proj_out(wout, wokeys, mixsT, "mixsT", modT[:, 16:24, 1:17], "modT")

        def s_mlp(l):
            s_norm_mod(1, modT[:, 24:32, 1:17], "aT1", "modT")
            mloaded = {}

            def missue(g):
                if g >= 8 or g in mloaded:
                    return
                wsl, wk = slot(2 * (mctr[0] % 2), 2)
                mctr[0] += 1
                wup = wsl[:, 0:4096].rearrange("p (k c) -> p k c", k=8)
                wdn = wsl[:, 4096:8192].rearrange("p (k c) -> p k c", k=4)
                load_w([(wup, w_up[l].rearrange("(k p) c -> p k c", p=128)[:, :, g * 512:(g + 1) * 512]),
                        (wdn, w_down[l][g * 512:(g + 1) * 512, :].rearrange("(k p) c -> p k c", p=128))], wk)
                mloaded[g] = (wup, wdn, wk)
            missue(0)
            for hg in range(8):
                missue(hg + 1)
                wup, wdn, wk = mloaded[hg]
                pb, pk = PS()
                for hc in range(4):
                    for k in range(8):
                        sop("pe", "matmul", pb[:, hc * 16:(hc + 1) * 16], wup[:, k, hc * 128:(hc + 1) * 128], hsT[:, k, :], start=(k == 0), stop=(k == 7),
                            reads=wk + ["hsT"], writes=[pk])
                r, rk = SF()
                sop("act", "activation", r[:, 0:64], pb[:, 0:64], AF.Relu, reads=[pk], writes=[rk])
                hd, hdk = SB()
                sop("dve", "tensor_tensor", hd[:, 0:64], r[:, 0:64], r[:, 0:64], ALU.mult, reads=[rk], writes=[hdk])
                pb2, pk2 = PS()
                for fc in range(8):
                    for hc in range(4):
                        sop("pe", "matmul", pb2[:, fc * 16:(fc + 1) * 16], wdn[:, hc, fc * 128:(fc + 1) * 128], hd[:, hc * 16:(hc + 1) * 16],
                            start=(hc == 0), stop=(hc == 3), reads=wk + [hdk], writes=[pk2])
                s_resid(pb2, pk2, modT[:, 40:48, 1:17], "modT")

        def s_kv_phase():
            ada(kv_w_ada, kv_b_ada, 4, kvmodT, "kvmodT")
            for k in range(8):
                S.op("dve", "tensor_scalar", aT[:, 2, k, :], kvmodT[:, 8 + k, :], 1.0, vcols[:, V_KVN + k:V_KVN + k + 1], ALU.add, ALU.mult,
                     reads=["kvmodT", "vcols"], writes=["aT2"])
            wsl, wkk = slot(0)
            wkv = wsl.rearrange("p (k c) -> p k c", k=8)
            load_w([(wkv, w_kv.rearrange("(k p) c -> p k c", p=128))], wkk)
            S.op("dve", "tensor_copy", kvmodP[:], kvmodT[:, :, 0], reads=["kvmodT"], writes=["kvmodP"])
            S.op("dve", "tensor_copy", aKVP[:], aT[:, 2, :, 0], reads=["aT2"], writes=["aKVP"])
            s_norm_mod(2, kvmodT[:, 0:8, 1:17], "aT2", "kvmodT")
            gk = vcols[:, V_KN:V_KN + 1]
            for m in range(2):
                pb, pk = PS()
                for k in range(8):
                    sop("pe", "matmul", pb[:, 0:16], wkv[:, k, m * 128:(m + 1) * 128], hsT[:, k, :], start=(k == 0), stop=(k == 7), reads=wkk + ["hsT"], writes=[pk])
                s_rope(pb[:, 0:16], pk, gk, ksT[:, m, :], "ksT")
                pb, pk = PS()
                for k in range(8):
                    sop("pe", "matmul", pb[:, 0:16], wkv[:, k, 256 + m * 128:256 + (m + 1) * 128], hsT[:, k, :], start=(k == 0), stop=(k == 7), reads=wkk + ["hsT"], writes=[pk])
                sop("act", "activation", vsT[:, m, :], pb[:, 0:16], AF.Copy, reads=[pk], writes=["vsT"])
            pt, ptk = PS()
            for m in range(2):
                sop("pe", "transpose", pt[0:16, m * 128:(m + 1) * 128], ksT[:, m, :], ident, reads=["ksT", "cst"], writes=[ptk])
                sop("pe", "transpose", pt[0:16, 256 + m * 128:256 + (m + 1) * 128], vsT[:, m, :], ident, reads=["vsT", "cst"], writes=[ptk])
            sop("dve", "tensor_copy", kvnew, pt[0:16, :], reads=[ptk], writes=["kvnew"])
            sdmas("sp", [(ks[:, 127, :], kvnew[:, 0:256]), (vs[:, 127, :], kvnew[:, 256:512])], reads=["kvnew"], sem_key="kvnew", final=True)
            S.dmas("sp", [(ks[:, 0:127, :], ck[:, 1:128, :]), (vs[:, 0:127, :], cv[:, 1:128, :])], sem_key="cachecp", final=True)

        def s_attn_layer(l):
            j = l - 2
            wsl, wqk = slot(0, 2)
            wq = wsl.rearrange("p (k c) -> p k c", k=8)
            load_w([(wq, w_q[j].rearrange("(k p) c -> p k c", p=128))], wqk)
            wsl2, wok = slot(2, 2)
            wo = wsl2.rearrange("p (c f) -> p c f", c=8)
            load_w([(wo, w_o[j].rearrange("(c p) f -> p c f", p=128))], wok)
            gq = vcols[:, V_QN + j:V_QN + j + 1]
            s_norm_mod(0, modT[:, 0:8, 1:17], "aT0", "modT")
            for c in range(8):
                pb, pk = PS()
                for k in range(8):
                    sop("pe", "matmul", pb[:, 0:16], wq[:, k, c * 128:(c + 1) * 128], hsT[:, k, :], start=(k == 0), stop=(k == 7), reads=wqk + ["hsT"], writes=[pk])
                s_rope(pb[:, 0:16], pk, gq, qsT[:, c, :], "qsT")
            sop("act", "activation", qsTb, qsT, AF.Copy, reads=["qsT"], writes=["qsTb"])
            for a in range(2):
                sop("dve", "memset", qpad[a], 0.0, writes=[("qpad", a)])
                rows = slice(a * 64, (a + 1) * 64)
                sop("dve", "tensor_copy", qpad[a][rows], qsT[rows], reads=["qsT"], writes=[("qpad", a)])
            pss, pssk = PL()
            po, pok = PL()
            for bi in range(4):
                sdmas("pool", [(cvt, cv[bi * 4:(bi + 1) * 4].rearrange("s j d -> j s d"))], writes=["cvt"], sem_key="cvt")
                for s4 in range(4):
                    s_ = bi * 4 + s4
                    ci, cik = ckin[s4 % 2], ("ckin", s4 % 2)
                    sdmas("sp", [(ci, ck[s_])], writes=[cik], sem_key=cik)
                    pt, ptk = PS()
                    for m in range(2):
                        sop("pe", "transpose", pt[:, m * 128:(m + 1) * 128], ci[:, m * 128:(m + 1) * 128], ident, reads=[cik, "cst"], writes=[ptk])
                    sop("act", "activation", ckT[:, :, s4, :], pt[:, 0:256].rearrange("p (m j) -> p m j", m=2), AF.Copy, reads=[ptk], writes=["ckT"])
                    for a in range(2):
                        for m in range(2):
                            c0_ = a * 128 + s_ * 8 + 4 * m
                            sop("pe", "matmul", pss[:, c0_:c0_ + 4], ckT[:, m, s4, :], qpad[a][:, 4 * m:4 * m + 4, s_],
                                start=True, stop=True, reads=["ckT", ("qpad", a)], writes=[pssk])
                for a in range(2):
                    cols = slice(bi * 32, (bi + 1) * 32)
                    pe_, pek = SF()
                    sop("act", "activation", pe_[:, 0:32], pss[:, a * 128 + bi * 32:a * 128 + (bi + 1) * 32], AF.Exp, scale=SCALE, reads=[pssk], writes=[pek])
                    sop("dve", "tensor_scalar", pTs[a][:, cols], pe_[:, 0:32], jmask, None, ALU.mult, reads=[pek, "cst"], writes=[("pTs", a)])
                for s4 in range(4):
                    s_ = bi * 4 + s4
                    for m in range(2):
                        for a in range(2):
                            cs_ = slice(s_ * 8 + 4 * m, s_ * 8 + 4 * m + 4)
                            sop("pe", "matmul", po[:, a * 128 + s_ * 8 + 4 * m:a * 128 + s_ * 8 + 4 * m + 4], cvt[:, s4, m * 128:(m + 1) * 128], pTs[a][:, cs_],
                                start=True, stop=True, reads=["cvt", ("pTs", a)], writes=[pok])
            for a in range(2):
                sop("pe", "matmul", po[:, 256 + a * 128:256 + (a + 1) * 128], onesb[:], pTs[a][:, 0:128], start=True, stop=True, reads=["onesb", ("pTs", a)], writes=[pok])
            pr, prk = SF()
            sop("dve", "tensor_tensor", pr.rearrange("p (m i s) -> p m i s", m=2, i=4), qsT.rearrange("p (m i) s -> p m i s", m=2),
                ksT.unsqueeze(2).to_broadcast([128, 2, 4, 16]), ALU.mult, reads=["qsT", "ksT"], writes=[prk])
            psn, psnk = PS()
            sop("pe", "matmul", psn[:, 0:128], onesblk, pr, start=True, stop=True, reads=[prk, "cst"], writes=[psnk])
            pn_, pnk_ = SF()
            sop("act", "activation", pn_, psn[:, 0:128], AF.Exp, scale=SCALE, reads=[psnk], writes=[pnk_])
            on_, onk = SF()
            sop("dve", "tensor_tensor", on_.rearrange("p (m i s) -> p m i s", m=2, i=4), pn_.rearrange("p (m i s) -> p m i s", m=2, i=4),
                vsT.unsqueeze(2).to_broadcast([128, 2, 4, 16]), ALU.mult, reads=[pnk_, "vsT"], writes=[onk])
            num, numk = SF()
            den, denk = SF()
            for a in range(2):
                rows = slice(a * 64, (a + 1) * 64)
                pov = po[rows, a * 128:(a + 1) * 128].rearrange("p (s c) -> p c s", c=8)
                dnv = po[rows, 256 + a * 128:256 + (a + 1) * 128].rearrange("p (s c) -> p c s", c=8)
                n3 = num[rows, :].rearrange("p (c s) -> p c s", c=8)
                d3 = den[rows, :].rearrange("p (c s) -> p c s", c=8)
                sop("dve", "tensor_tensor", n3, pov, on_[rows, :].rearrange("p (c s) -> p c s", c=8), ALU.add, reads=[pok, onk], writes=[numk])
                sop("dve", "tensor_tensor", d3, dnv, pn_[rows, :].rearrange("p (c s) -> p c s", c=8), ALU.add, reads=[pok, pnk_], writes=[denk])
                sop("dve", "tensor_tensor", d3, d3, esink[rows, j, :].unsqueeze(2).to_broadcast([64, 8, 16]), ALU.add, reads=[denk, "esink"], writes=[denk])
                sop("dve", "reciprocal", den[rows, :], den[rows, :], reads=[denk], writes=[denk])
                sop("dve", "tensor_tensor", mixsT[rows, :, :], n3, d3, ALU.mult, reads=[numk, denk], writes=["mixsT"])
            s_proj_out(wo, wok, mixsT, "mixsT", modT[:, 16:24, 1:17], "modT")

        def sample_phase():
            sop("dve", "memset", smp, 0.0, writes=["xsT", "hsT", "mixsT", "qsT", "qsTb", "ksT", "vsT", "km", "ktv", "ckT", "cvt", "kvnew",
                                                    ("pTs", 0), ("pTs", 1), "tab16"] + [("sf", i) for i in range(NSF)] + [("sb", i) for i in range(NSB)])
            sdmas("sp", [(tab16, cs16[:, :])], writes=["tab16"], sem_key="tab16")
            xa, xak = FT()
            xb_, xbk = FT()
            S.dmas("sp", [(xa[0:16, :], xs[:, 0:512]), (xb_[0:16, :], xs[:, 512:1024])], writes=[xak, xbk], sem_key=xak)
            pb, pk = PS()
            for k in range(8):
                src = (xa if k < 4 else xb_)[0:16, (k % 4) * 128:(k % 4 + 1) * 128]
                sop("pe", "transpose", pb[:, k * 16:(k + 1) * 16], src, ident[0:16, 0:16], reads=[xak, xbk, "cst"], writes=[pk])
            sop("dve", "tensor_copy", xsT.rearrange("p k s -> p (k s)"), pb[:, 0:128], reads=[pk], writes=["xsT"])
            for l in range(4):
                ada(w_ada[l], b_ada[l], 12, modT, "modT")
                mod_derive(l)
                S.op("dve", "tensor_copy", modP[:, l, :], modT[:, :, 0], reads=["modT"], writes=[("modP", l)])
                S.op("dve", "tensor_copy", aP[:, l, :, :], aT[:, 0:2, :, 0], reads=["aT0", "aT1"], writes=[("aP", l)])
                if l == 2:
                    s_kv_phase()
                if l < 2:
                    s_hgrn_layer(l)
                else:
                    s_attn_layer(l)
                s_mlp(l)
            pb, pk = PS()
            pb2, pk2 = PS()
            for k in range(8):
                dstp = pb if k < 4 else pb2
                dk_ = pk if k < 4 else pk2
                sop("pe", "transpose", dstp[0:16, (k % 4) * 128:(k % 4 + 1) * 128], xsT[:, k, :], ident, reads=["xsT", "cst"], writes=[dk_])
            ya, yak = FT()
            yb, ybk = FT()
            sop("dve", "tensor_copy", ya[0:16, :], pb[0:16, :], reads=[pk], writes=[yak])
            sop("dve", "tensor_copy", yb[0:16, :], pb2[0:16, :], reads=[pk2], writes=[ybk])
            S.dmas("sp", [(ys[:, 0:512], ya[0:16, :]), (ys[:, 512:1024], yb[0:16, :])], reads=[yak, ybk], sem_key=yak, final=True)

        def final_out():
            for b in range(NBLK):
                t = b // 4
                for g in range(2):
                    yo, yk = FT()
                    pb, pk = PS()
                    for kk in range(4):
                        k = g * 4 + kk
                        S.op("pe", "transpose", pb[:, kk * 128:(kk + 1) * 128], xT[:, k, b * 128:(b + 1) * 128], ident, reads=[("xT", k, t), "cst"], writes=[pk])
                    if g == 0:
                        S.op("dve", "tensor_copy", yo[:], pb[:], reads=[pk], writes=[yk])
                    else:
                        S.op("act", "activation", yo[:], pb[:], AF.Copy, reads=[pk], writes=[yk])
                    S.dmas("sp", [(yp[b * 128:(b + 1) * 128, g * 512:(g + 1) * 512], yo[:])], reads=[yk], sem_key=yk, final=True)

        import os
        STG = os.environ.get("KSTAGE", "full")
        if do_sample:
            sample_phase()
        load_x()
        for l in range(4):
            if STG == "io":
                break
            if do_sample:
                S.op("dve", "tensor_copy", modT[:, :, 0], modP[:, l, :], reads=[("modP", l)], writes=["modT"])
                S.op("dve", "tensor_copy", aT[:, 0:2, :, 0], aP[:, l, :, :], reads=[("aP", l)], writes=["aT0", "aT1"])
            else:
                ada(w_ada[l], b_ada[l], 12, modT, "modT")
                mod_derive(l)
            if STG == "ada":
                break
            if l == 2:
                kv_phase()
            if l < 2:
                if STG == "mlp0":
                    pass
                elif STG == "passA":
                    S.op("dve", "memset", Sst, 0.0, writes=SK)
                    hgrn_pass(l, True)
                    S.dmas("sp", [(sp_state[l].rearrange("h k v -> k h v"), Sst)], reads=SK, sem_key=("spst", l), final=True)
                    break
                else:
                    hgrn_layer(l)
            else:
                attn_layer(l)
            if STG == "hgrn0":
                break
            mlp(l)
            if STG in ("l0", "mlp0"):
                break
            if STG == "l1" and l == 1:
                break
            if STG == "l2" and l == 2:
                break
        final_out()

        S.emit()
        print("ops", len(S.ops), "sem counts", S.max_counts, "dma sems", len(S.dma_count), "sbuf left", nc.sbuf_bytes_remaining)
    return nc


_CACHE = {}


def make_in_maps(inputs, T, seq):
    f = lambda a: np.ascontiguousarray(np.asarray(a, dtype=np.float32))
    xp_all = f(inputs["x_prompt"])
    xs_all = f(inputs["x_sample"]).reshape(NSAMP, D)
    cp = f(inputs["c_prompt"])
    cs = f(inputs["c_sample"])
    st_all = f(inputs["state_hgrn"])
    ck_all = f(inputs["cache_k"]).reshape(NSAMP, 128, 256)
    cv_all = f(inputs["cache_v"]).reshape(NSAMP, 128, 256)
    shared = {k: f(inputs[k]) for k in ("w_ada", "b_ada", "norm1_g", "norm2_g", "hg_w_in", "hg_w_out", "hg_lower_bounds",
                                        "hg_gn_g", "kv_w_ada", "kv_b_ada", "kv_norm_g", "w_kv", "k_norm_g", "w_q",
                                        "q_norm_g", "sinks", "w_o", "w_up", "w_down")}
    shared["hg_w_in"] = np.ascontiguousarray(shared["hg_w_in"].reshape(2, D, 4, 8, 128).transpose(0, 3, 1, 2, 4).reshape(2, 8, D, 512))
    shared["w_q"] = np.ascontiguousarray(shared["w_q"].reshape(2, D, 2, 2, 4, 64).transpose(0, 1, 2, 4, 3, 5).reshape(2, D, D))
    shared["w_o"] = np.ascontiguousarray(shared["w_o"].reshape(2, 2, 2, 4, 64, D).transpose(0, 1, 3, 2, 4, 5).reshape(2, D, D))
    ct16, st16 = rope_tables(np.full((16,), PAST, np.int64))
    cs16 = np.ascontiguousarray(np.concatenate([ct16, st16], axis=1))
    maps = []
    for c in range(8):
        b, half = c // 2, c % 2
        m = dict(shared)
        m["xp"] = np.ascontiguousarray(xp_all[b, half * T:(half + 1) * T])
        sl = slice(c * NS, (c + 1) * NS)
        m["c17"] = np.ascontiguousarray(np.concatenate([cp[b:b + 1], cs[sl]], axis=0))
        m["xs"] = np.ascontiguousarray(xs_all[sl])
        m["st_in"] = np.ascontiguousarray(st_all[:, sl])
        m["ck"] = np.ascontiguousarray(ck_all[sl])
        m["cv"] = np.ascontiguousarray(cv_all[sl])
        m["consts"] = host_consts(float(half))
        ct, stb = rope_tables(np.arange(half * T, (half + 1) * T))
        m["costab"] = ct
        m["sintab"] = stb
        m["cs16"] = cs16
        maps.append(m)
    return maps


def run(inputs, T, dbg=False):
    if T not in _CACHE:
        _CACHE[T] = build(T, dbg=dbg)
    nc = _CACHE[T]
    maps = make_in_maps(inputs, T, 2 * T)
    res = run_bass_kernel_spmd(nc, maps, core_ids=list(range(8)))
    R = res.results
    global LAST_RESULTS
    LAST_RESULTS = R
    seq = 2 * T
    y_prompt = np.zeros((NB, seq, D), np.float32)
    hg_p = np.zeros((2, NB, 8, 128, 128), np.float32)
    k_p = np.zeros((NB, 128, 4, 64), np.float32)
    v_p = np.zeros((NB, 128, 4, 64), np.float32)
    y_sample = np.zeros((NSAMP, 1, D), np.float32)
    hg_s = np.zeros((2, NSAMP, 8, 128, 128), np.float32)
    k_s = np.zeros((NSAMP, 128, 4, 64), np.float32)
    v_s = np.zeros((NSAMP, 128, 4, 64), np.float32)
    for c in range(8):
        b, half = c // 2, c % 2
        r = R[c]
        y_prompt[b, half * T:(half + 1) * T] = r["yp"]
        if half == 1:
            hg_p[:, b] = r["sp_state"]
            k_p[b] = r["kp"].reshape(128, 4, 64)
            v_p[b] = r["vp"].reshape(128, 4, 64)
        sl = slice(c * NS, (c + 1) * NS)
        y_sample[sl, 0] = r["ys"]
        hg_s[:, sl] = r["ss"]
        k_s[sl] = r["ks"].reshape(NS, 128, 4, 64)
        v_s[sl] = r["vs"].reshape(NS, 128, 4, 64)
    return (y_prompt, y_sample, hg_p, k_p, v_p, hg_s, k_s, v_s)


def kernel(**inputs):
    return run(inputs, SEQ // 2)
```

```python
import math
from contextlib import ExitStack

import numpy as np
import concourse.bass as bass
import concourse.mybir as mybir
from concourse.bass_utils import run_bass_kernel_spmd

F32 = mybir.dt.float32
BF16 = mybir.dt.bfloat16
ALU = mybir.AluOpType
AF = mybir.ActivationFunctionType

D = 1024
SEQ = 4096
NB = 4
NSAMP = 128
NS = 16
WINDOW = 128
PAST = 8192
EPS = 1e-6
SCALE = 1.0 / 8.0

SAME_ENGINE_SYNC = True
FOLD_WAITS = True


class Op:
    __slots__ = ("id", "eng", "fn", "deps", "is_dma", "sem_key", "n_dma", "waits", "inc_amt",
                 "needs_inc", "lidx", "count", "dma_val", "vc", "final")


class Sched:
    ENGS = ("pe", "act", "dve", "pool", "sp")

    def __init__(self, nc):
        self.nc = nc
        self.ops = []
        self.by_eng = {e: [] for e in self.ENGS}
        self.last_w = {}
        self.readers = {}
        self.dma_count = {}
        self.out_dmas = []
        self.bulk = set()

    def _track(self, op, reads, writes):
        pr = [k for k in reads if isinstance(k, tuple) and k and k[0] == "ps"]
        if pr:
            reads = [k for k in reads if k not in pr]
            writes = list(writes) + pr
        deps = set()
        for k in reads:
            w = self.last_w.get(k)
            if w is not None:
                deps.add(w)
        for k in writes:
            w = self.last_w.get(k)
            if w is not None:
                deps.add(w)
            for r in self.readers.get(k, ()):
                deps.add(r)
        for k in reads:
            self.readers.setdefault(k, []).append(op.id)
        for k in writes:
            self.last_w[k] = op.id
            self.readers[k] = []
        deps.discard(op.id)
        op.deps = deps

    def op(self, eng, name, *args, reads=(), writes=(), **kw):
        def fn(e, name=name, args=args, kw=kw):
            return getattr(e, name)(*args, **kw)
        return self.add(eng, fn, reads, writes)

    def dmas(self, queue, pairs, reads=(), writes=(), sem_key=None, final=False, bulk=False):
        pairs = list(pairs)

        def fn(e, pairs=pairs):
            return [e.dma_start(out=o, in_=i) for (o, i) in pairs]
        return self.dma(queue, fn, reads, writes, sem_key=sem_key, n=len(pairs), final=final, bulk=bulk)

    def add(self, eng, fn, reads=(), writes=()):
        op = Op()
        op.id = len(self.ops)
        op.eng = eng
        op.fn = fn
        op.is_dma = False
        op.sem_key = None
        op.n_dma = 0
        op.needs_inc = False
        op.final = False
        self._track(op, reads, writes)
        self.ops.append(op)
        self.by_eng[eng].append(op)
        return op

    def dma(self, queue, fn, reads=(), writes=(), sem_key=None, n=1, final=False, inc=16, bulk=False):
        op = Op()
        op.id = len(self.ops)
        op.eng = queue
        op.fn = fn
        op.is_dma = True
        op.sem_key = sem_key
        op.n_dma = n
        op.inc_amt = inc
        op.needs_inc = True
        op.final = final
        if bulk:
            self.bulk.add(sem_key)
        self.dma_count[sem_key] = self.dma_count.get(sem_key, 0) + inc * n
        op.dma_val = self.dma_count[sem_key]
        self._track(op, reads, writes)
        self.ops.append(op)
        self.by_eng[queue].append(op)
        if final:
            self.out_dmas.append(op)
        return op

    def finalize(self):
        for op in self.ops:
            if op.is_dma and op.sem_key in self.bulk:
                op.dma_val = self.dma_count[op.sem_key]
        lcount = {e: 0 for e in self.ENGS}
        for op in self.ops:
            if not op.is_dma:
                lcount[op.eng] += 1
                op.lidx = lcount[op.eng]
        evc = {e: {} for e in self.ENGS}
        for op in self.ops:
            E = op.eng
            my = evc[E]
            waits = []
            for d in sorted(op.deps):
                dop = self.ops[d]
                if dop.is_dma:
                    key = ("D", dop.sem_key)
                    val = dop.dma_val
                else:
                    if dop.eng == E and (E == "pe" or not SAME_ENGINE_SYNC) and not op.is_dma:
                        continue
                    key = ("E", dop.eng)
                    val = dop.lidx
                if my.get(key, 0) >= val:
                    continue
                waits.append(d)
                dop.needs_inc = True
                for k, v in dop.vc.items():
                    if my.get(k, 0) < v:
                        my[k] = v
            op.waits = waits
            vc = dict(my)
            if op.is_dma:
                k = ("D", op.sem_key)
                vc[k] = max(vc.get(k, 0), op.dma_val)
            else:
                vc[("E", E)] = op.lidx
            op.vc = vc
        self.final_waits = {}
        for op in self.out_dmas:
            self.final_waits[op.sem_key] = max(self.final_waits.get(op.sem_key, 0), op.dma_val)
        cnt = {e: 0 for e in self.ENGS}
        for op in self.ops:
            if not op.is_dma:
                if op.needs_inc:
                    cnt[op.eng] += 1
                op.count = cnt[op.eng]
        self.max_counts = cnt

    def emit(self):
        nc = self.nc
        self.finalize()
        with ExitStack() as st:
            esem = {e: st.enter_context(nc.semaphore("es_" + e)) for e in ("pe", "act", "dve", "pool")}
            dsem = {}
            for i, k in enumerate(self.dma_count):
                dsem[k] = st.enter_context(nc.semaphore("ds_%d" % i))
            block = st.enter_context(nc.Block())
            ops = self.ops

            def run(eng_name, eng):
                for op in self.by_eng[eng_name]:
                    wmap = {}
                    for d in op.waits:
                        dop = ops[d]
                        if dop.is_dma:
                            s = dsem[dop.sem_key]
                            v = dop.dma_val
                        else:
                            s = esem[dop.eng]
                            v = dop.count
                        key = id(s)
                        if key not in wmap or wmap[key][1] < v:
                            wmap[key] = (s, v)
                    wl = list(wmap.values())
                    fold = None
                    if FOLD_WAITS and not op.is_dma and wl:
                        fold = wl.pop()
                    for s, v in wl:
                        eng.wait_ge(s, v)
                    r = op.fn(eng)
                    if fold is not None:
                        r._wait_ge(fold[0], fold[1])
                    if op.is_dma:
                        assert len(r) == op.n_dma, (len(r), op.n_dma)
                        for ins in r:
                            ins.then_inc(dsem[op.sem_key], op.inc_amt)
                    elif op.needs_inc:
                        r.then_inc(esem[eng_name], 1)
                if eng_name == "sp":
                    for k, v in self.final_waits.items():
                        eng.wait_ge(dsem[k], v)

            @block.tensor
            def _(e):
                run("pe", e)

            @block.scalar
            def _(e):
                run("act", e)

            @block.vector
            def _(e):
                run("dve", e)

            @block.gpsimd
            def _(e):
                run("pool", e)

            @block.sync
            def _(e):
                run("sp", e)


C_ID, C_MR, C_AM, C_M4, C_OB, C_PM, C_JM, C_FL, C_RM, C_N = 0, 128, 640, 1152, 1664, 1792, 1920, 1921, 1922, 1923


def host_consts(flag):
    c = np.zeros((128, C_N), np.float32)
    c[:, C_ID:C_ID + 128] = np.eye(128, dtype=np.float32)
    mr = np.ones((512,), np.float32)
    mr[::64] = 0.0
    c[:, C_MR:C_MR + 512] = mr[None, :]
    s = np.arange(64)[:, None]
    t = np.arange(64)[None, :]
    am = ((s <= t) & ((s // 32) == (t // 32))).astype(np.float32)
    c[0:64, C_AM:C_AM + 512] = np.tile(am, (1, 8))
    c[0:32, C_RM] = 1.0
    j = np.arange(128)[:, None]
    q = np.arange(128)[None, :]
    mprev = (j > q).astype(np.float32)
    mcur = (j <= q).astype(np.float32)
    c[:, C_M4:C_M4 + 512] = np.concatenate([mprev, mprev, mcur, mcur], axis=1)
    ob = np.zeros((128, 128), np.float32)
    ob[0:64, 0:64] = 1.0
    ob[64:128, 64:128] = 1.0
    c[:, C_OB:C_OB + 128] = ob
    pm = np.zeros((128, 128), np.float32)
    for p in range(128):
        d = p % 64
        partner = p + 32 if d < 32 else p - 32
        pm[p, partner] = 1.0
    c[:, C_PM:C_PM + 128] = pm
    c[:, C_JM] = 1.0
    c[0, C_JM] = 0.0
    c[:, C_FL] = flag
    return c


def rope_tables(pos):
    half = 32
    inv = (np.float32(10000.0) ** (-np.arange(half, dtype=np.float32) / np.float32(half))).astype(np.float32)
    ang = pos.astype(np.float32)[None, :] * inv[:, None]
    cos = np.cos(ang).astype(np.float32)
    sin = np.sin(ang).astype(np.float32)
    ct = np.concatenate([cos, cos, cos, cos], axis=0)
    st = np.concatenate([-sin, sin, -sin, sin], axis=0)
    return np.ascontiguousarray(ct), np.ascontiguousarray(st)


def build(T, do_sample=True, dbg=False):
    NT = T // 512
    NBLK = T // 128
    nc = bass.Bass("TRN2", target_bir_lowering=False)

    def din(name, shape, dt=F32):
        return nc.dram_tensor(name, list(shape), dt, kind="ExternalInput").ap()

    def dout(name, shape, dt=F32):
        return nc.dram_tensor(name, list(shape), dt, kind="ExternalOutput").ap()

    def dint(name, shape, dt=F32):
        return nc.dram_tensor(name, list(shape), dt, kind="Internal").ap()

    xp = din("xp", [T, D])
    c17 = din("c17", [17, D])
    xs = din("xs", [NS, D])
    st_in = din("st_in", [2, NS, 8, 128, 128])
    ck = din("ck", [NS, 128, 256])
    cv = din("cv", [NS, 128, 256])
    w_ada = din("w_ada", [4, D, 6 * D])
    b_ada = din("b_ada", [4, 6 * D])
    norm1_g = din("norm1_g", [4, D])
    norm2_g = din("norm2_g", [4, D])
    hg_w_in = din("hg_w_in", [2, 8, D, 512])
    hg_w_out = din("hg_w_out", [2, D, D])
    hg_lb = din("hg_lower_bounds", [2, D])
    hg_gn = din("hg_gn_g", [2, 128])
    kv_w_ada = din("kv_w_ada", [D, 2 * D])
    kv_b_ada = din("kv_b_ada", [2 * D])
    kv_norm_g = din("kv_norm_g", [D])
    w_kv = din("w_kv", [D, 512])
    k_norm_g = din("k_norm_g", [64])
    w_q = din("w_q", [2, D, D])
    q_norm_g = din("q_norm_g", [2, 64])
    sinks = din("sinks", [2, 16])
    w_o = din("w_o", [2, D, D])
    w_up = din("w_up", [4, D, 4 * D])
    w_down = din("w_down", [4, 4 * D, D])
    consts = din("consts", [128, C_N])
    costab = din("costab", [128, T])
    sintab = din("sintab", [128, T])
    cs16 = din("cs16", [128, 32])

    yp = dout("yp", [T, D])
    ys = dout("ys", [NS, D])
    sp_state = dout("sp_state", [2, 8, 128, 128])
    kp = dout("kp", [128, 256])
    vp = dout("vp", [128, 256])
    ss = dout("ss", [2, NS, 8, 128, 128])
    ks = dout("ks", [NS, 128, 256])
    vs = dout("vs", [NS, 128, 256])

    DBG = {}
    if dbg:
        for nm in ("mix", "o", "qq", "kk", "e3", "att", "cum", "rs"):
            DBG[nm] = dout("dbg_" + nm, [128, 8, 512] if nm == "mix" else [128, 512])
    xs_scr = [dint("xs_scr%d" % i, [128, 1024]) for i in range(2)]
    xg_scr = [dint("xg_scr%d" % i, [256, 1024]) for i in range(2)]
    kvs_scr = dint("kvs_scr", [(T // 512) * 8, 64, 2048], BF16)
    kv_scr = dint("kv_scr", [128, 512], BF16)
    kvg_scr = dint("kvg_scr", [256, 512], BF16)
    PAIRS = [[0, 1], [2, 3], [4, 5], [6, 7]]

    with ExitStack() as st:
        st.enter_context(nc.allow_low_precision("bf16 matmul operands, fp32 accumulation"))
        st.enter_context(nc.allow_non_contiguous_dma("small strided parameter loads"))

        def sb(name, shape, dt=F32):
            return st.enter_context(nc.sbuf_tensor(name, list(shape), dt))

        S = Sched(nc)
        banks = [st.enter_context(nc.psum_tensor("bank%d" % i, [128, 512], F32)) for i in range(8)]
        bctr = [0]
        psn = [6]

        def PS():
            i = bctr[0] % psn[0]
            bctr[0] = (i + 1) % psn[0]
            return banks[i], ("ps", i)

        lctr = [0]

        def PL():
            i = 6 + lctr[0] % 2
            lctr[0] += 1
            return banks[i], ("ps", i)

        NF = 9
        fts = [sb("ft%d" % i, [128, 512]) for i in range(NF)]
        fctr = [0]

        def FT():
            i = fctr[0]
            fctr[0] = (i + 1) % NF
            return fts[i], ("ft", i)

        NBT = 8
        bts = [sb("bt%d" % i, [128, 512], BF16) for i in range(NBT)]
        bbctr = [0]

        def BT():
            i = bbctr[0]
            bbctr[0] = (i + 1) % NBT
            return bts[i], ("bt", i)

        xT = sb("xT", [128, 8, T])
        cst = sb("cst", [128, C_N])
        NSLOT = 4
        arena = sb("arena", [128, NSLOT * 4096], BF16)

        def slot(i, n=1):
            return arena[:, i * 4096:(i + n) * 4096], [("W", i + r) for r in range(n)]

        R32 = sb("R32", [128, 16384], BF16)
        hTall = R32[:, :].rearrange("p (k t) -> p k t", k=8) if T == 2048 else None
        if T != 2048:
            hTall = R32[:, 0:8 * T].rearrange("p (k t) -> p k t", k=8)
        MIXK = [("mix", c_) for c_ in range(8)]
        if T >= 1024:
            mixT = hTall[:, :, 512:1024]
            MIXW = [("hT", 1)]
        else:
            mixT = sb("mixT", [128, 8, 512], BF16)[:]
            MIXW = []
        tabc = sb("tabc", [128, 512])
        tabs = sb("tabs", [128, 512])
        identb = sb("identb", [128, 128], BF16)
        onesD = sb("onesD", [128, 128])
        ones128 = sb("ones128", [128, 128])
        onesb = sb("onesb", [128, 128], BF16)
        onesDb = sb("onesDb", [128, 128], BF16)
        ones128b = sb("ones128b", [128, 128], BF16)
        onesblkb = sb("onesblkb", [128, 128], BF16)
        permb = sb("permb", [128, 128], BF16)
        epsc = sb("epsc", [128, 1])
        epsl = sb("epsl", [128, 1])
        vrows = sb("vrows", [128, 128])
        vcols = sb("vcols", [128, 128])
        cT = sb("cT", [128, 8, 17])
        cTb = sb("cTb", [128, 8, 17], BF16)
        modT = sb("modT", [128, 48, 17])
        kvmodT = sb("kvmodT", [128, 16, 17])
        aT = sb("aT", [128, 3, 8, 17])
        lbAB = sb("lbAB", [128, 2, 2, 8])
        modP = sb("modP", [128, 4, 48])
        aP = sb("aP", [128, 4, 2, 8])
        kvmodP = sb("kvmodP", [128, 16])
        aKVP = sb("aKVP", [128, 8])
        lbt = sb("lbt", [128, 2, 8])
        esink = sb("esink", [128, 2, 8])
        mask4f = sb("mask4f", [128, 512])
        ones17 = sb("ones17", [1, 17])
        rd = sb("rd", [128, 128])
        KVW = max(2 * (T + 128) + (NBLK + 1) * 256, 8704)
        KVR = sb("KVR", [128, KVW], BF16)
        KT = KVR[:, 0:2 * (T + 128)].rearrange("p (m t) -> p m t", m=2)
        Vtok = KVR[:, 2 * (T + 128):2 * (T + 128) + (NBLK + 1) * 256].rearrange("p (b c) -> p b c", c=256)
        kdtok = KVR[0:64, 0:1024].rearrange("p (c d) -> p c d", c=8)
        vtok = KVR[0:64, 1024:2048].rearrange("p (c d) -> p c d", c=8)
        attsb = KVR[0:64, 2048:2560]
        Sbf = KVR[:, 2560:3584].rearrange("p (h v) -> p h v", h=8)
        Sst = KVR[:, 3584:5632].bitcast(F32).rearrange("p (h v) -> p h v", h=8)
        klast = sb("klast", [128, 2, 128])
        kt1 = sb("kt1", [64, 2048], BF16)
        KTS = [KVR[0:64, 0:2048], kt1[:]]
        p1x = sb("p1x", [128, 512])
        P1F = [KVR[:, 5632:6656].bitcast(F32), KVR[:, 6656:7680].bitcast(F32), KVR[:, 7680:8704].bitcast(F32), p1x[:]]
        p1b = sb("p1b", [128, 4, 512], BF16)

        ident = cst[:, C_ID:C_ID + 128]
        maskreset = cst[:, C_MR:C_MR + 512]
        attmask = cst[0:64, C_AM:C_AM + 512]
        mask4 = cst[:, C_M4:C_M4 + 512]
        onesblk = cst[:, C_OB:C_OB + 128]
        perm = cst[:, C_PM:C_PM + 128]
        jmask = cst[:, C_JM:C_JM + 1]
        flag = cst[:, C_FL:C_FL + 1]
        rowmask = cst[0:64, C_RM:C_RM + 1]
        cbs = sb("cbs", [128, 8, 2])

        V_N1, V_N2, V_KVN, V_LB, V_GN, V_QN, V_KN = 0, 32, 64, 72, 88, 90, 92
        SK = [("S", h) for h in range(8)]

        S.dmas("sp", [(cst[:], consts[:, :])], writes=["cst"], sem_key="ld", bulk=True)
        S.op("dve", "memset", vrows[:], 0.0, writes=["vrows"])
        prs = [(vrows[V_N1:V_N1 + 32, :], norm1_g.rearrange("l (k p) -> (l k) p", p=128)),
               (vrows[V_N2:V_N2 + 32, :], norm2_g.rearrange("l (k p) -> (l k) p", p=128)),
               (vrows[V_KVN:V_KVN + 8, :], kv_norm_g.rearrange("(k p) -> k p", p=128)),
               (vrows[V_LB:V_LB + 16, :], hg_lb.rearrange("l (k p) -> (l k) p", p=128)),
               (vrows[V_GN:V_GN + 2, :], hg_gn[:, :])]
        for l in range(2):
            for a in range(2):
                prs.append((vrows[V_QN + l:V_QN + l + 1, a * 64:(a + 1) * 64], q_norm_g[l:l + 1, :]))
        for a in range(2):
            prs.append((vrows[V_KN:V_KN + 1, a * 64:(a + 1) * 64], k_norm_g.rearrange("(o d) -> o d", o=1)))
        S.dmas("sp", prs, writes=["vrows"], sem_key="ld2", bulk=True)
        prs = []
        for l in range(2):
            sv = sinks[l].rearrange("(m a i) -> a m i", m=2, a=2, i=4)
            for a in range(2):
                prs.append((esink[a * 64:(a + 1) * 64, l, :].rearrange("p (m i) -> p m i", m=2), sv[a:a + 1].to_broadcast([64, 2, 4])))
        S.dmas("sp", prs, writes=["esink"], sem_key="ld", bulk=True)
        S.op("act", "activation", esink[:], esink[:], AF.Exp, reads=["esink"], writes=["esink"])

        S.op("dve", "memset", cbs[:], 0.0, writes=["cbs"])
        S.op("dve", "memset", onesD[:], 1.0 / D, writes=["onesD"])
        S.op("dve", "memset", ones128[:], 1.0 / 128, writes=["ones128"])
        S.op("dve", "memset", onesb[:], 1.0, writes=["onesb"])
        S.op("dve", "memset", onesDb[:], 1.0 / D, writes=["onesDb"])
        S.op("dve", "memset", ones128b[:], 1.0 / 128, writes=["ones128b"])
        S.op("dve", "tensor_copy", onesblkb[:], cst[:, C_OB:C_OB + 128], reads=["cst"], writes=["onesblkb"])
        S.op("dve", "tensor_copy", permb[:], cst[:, C_PM:C_PM + 128], reads=["cst"], writes=["permb"])
        S.op("dve", "memset", epsc[:], EPS, writes=["epsc"])
        S.op("dve", "memset", epsl[:], 1e-7, writes=["epsl"])
        S.op("dve", "memset", ones17[:], 1.0, writes=["ones17"])
        S.op("dve", "tensor_copy", identb[:], ident, reads=["cst"], writes=["identb"])
        S.op("dve", "tensor_scalar", mask4f[:, 0:256], mask4[:, 0:256], flag, None, ALU.mult, reads=["cst"], writes=["mask4f"])
        S.op("dve", "tensor_copy", mask4f[:, 256:512], mask4[:, 256:512], reads=["cst"], writes=["mask4f"])

        pb, pk = PS()
        S.op("pe", "transpose", pb[:, 0:128], vrows[:], ident, reads=["vrows", "cst"], writes=[pk])
        S.op("dve", "tensor_copy", vcols[:], pb[:, 0:128], reads=[pk], writes=["vcols"])
        S.op("dve", "tensor_tensor", lbt[:, 1, :], vcols[:, V_LB:V_LB + 8], vcols[:, V_LB + 8:V_LB + 16], ALU.subtract, reads=["vcols"], writes=["lbt"])
        S.op("act", "activation", lbt[:, 1, :], lbt[:, 1, :], AF.Exp, reads=["lbt"], writes=["lbt"])
        S.op("dve", "tensor_scalar", lbt[:, 1, :], lbt[:, 1, :], 1.0, None, ALU.add, reads=["lbt"], writes=["lbt"])
        S.op("dve", "reciprocal", lbt[:, 1, :], lbt[:, 1, :], reads=["lbt"], writes=["lbt"])
        S.op("dve", "tensor_scalar", lbt[:, 0, :], lbt[:, 1, :], -1.0, 1.0, ALU.mult, ALU.add, reads=["lbt"], writes=["lbt"])
        S.op("dve", "tensor_tensor", lbt[:, 0, :], lbt[:, 0, :], lbt[:, 0, :], ALU.subtract, reads=["lbt"], writes=["lbt"])
        for l in range(2):
            S.op("dve", "tensor_scalar", lbAB[:, l, 0, :], lbt[:, l, :], -0.5, 0.5, ALU.mult, ALU.add, reads=["lbt"], writes=["lbAB"])
            S.op("dve", "tensor_tensor", lbAB[:, l, 1, :], lbAB[:, l, 0, :], lbt[:, l, :], ALU.add, reads=["lbt", "lbAB"], writes=["lbAB"])

        c0, c0k = FT()
        c1, c1k = FT()
        S.dmas("sp", [(c0[0:17, :], c17[:, 0:512]), (c1[0:17, :], c17[:, 512:1024])], writes=[c0k, c1k], sem_key="ld", bulk=True)
        S.op("act", "activation", c0[0:17, :], c0[0:17, :], AF.Silu, reads=[c0k], writes=[c0k])
        S.op("act", "activation", c1[0:17, :], c1[0:17, :], AF.Silu, reads=[c1k], writes=[c1k])
        pb, pk = PS()
        for k in range(8):
            src = (c0 if k < 4 else c1)[0:17, (k % 4) * 128:(k % 4 + 1) * 128]
            S.op("pe", "transpose", pb[:, k * 17:(k + 1) * 17], src, ident[0:17, 0:17], reads=[c0k, c1k, "cst"], writes=[pk])
        S.op("dve", "tensor_copy", cT[:].rearrange("p k s -> p (k s)"), pb[:, 0:136], reads=[pk], writes=["cT"])
        S.op("dve", "tensor_copy", cTb[:], cT[:], reads=["cT"], writes=["cTb"])

        def load_x():
          for b in range(NBLK):
            t = b // 4
            xa, xak = FT()
            xb_, xbk = FT()
            S.dmas("sp", [(xa[:], xp[b * 128:(b + 1) * 128, 0:512]), (xb_[:], xp[b * 128:(b + 1) * 128, 512:1024])],
                   writes=[xak, xbk], sem_key=xak)
            for g in range(2):
                src, srck = (xa, xak) if g == 0 else (xb_, xbk)
                pb, pk = PS()
                for kk in range(4):
                    S.op("pe", "transpose", pb[:, kk * 128:(kk + 1) * 128], src[:, kk * 128:(kk + 1) * 128], ident, reads=[srck, "cst"], writes=[pk])
                dst = xT[:, g * 4:(g + 1) * 4, b * 128:(b + 1) * 128]
                wr = [("xT", g * 4 + kk, t) for kk in range(4)] + ["SMPREGION"]
                if g == 0:
                    S.op("dve", "tensor_copy", dst, pb[:].rearrange("p (k t) -> p k t", k=4), reads=[pk], writes=wr)
                else:
                    S.op("act", "activation", dst, pb[:].rearrange("p (k t) -> p k t", k=4), AF.Copy, reads=[pk], writes=wr)

        actr = [0]

        def ada(wsrc, bsrc, ncolt, dst, dkey):
            for j in range(ncolt):
                wsl, wk = slot(actr[0] % 4)
                actr[0] += 1
                wv = wsl.rearrange("p (k c) -> p k c", k=8)
                S.dmas("pool", [(wv, wsrc.rearrange("(k p) c -> p k c", p=128)[:, :, j * 512:(j + 1) * 512])], writes=wk, sem_key=wk[0])
                br, bk = FT()
                S.dmas("sp", [(br[0:1, :], bsrc[j * 512:(j + 1) * 512].rearrange("(o c) -> o c", o=1))], writes=[bk], sem_key=bk)
                pb, pk = PS()
                for k in range(8):
                    S.op("pe", "matmul", pb[0:17, :], cTb[:, k, :], wv[:, k, :], start=(k == 0), stop=False, reads=wk + ["cTb"], writes=[pk])
                S.op("pe", "matmul", pb[0:17, :], ones17[:], br[0:1, :], start=False, stop=True, reads=[bk, "ones17"], writes=[pk])
                tm, tmk = FT()
                S.op("dve", "tensor_copy", tm[0:17, :], pb[0:17, :], reads=[pk], writes=[tmk])
                pb2, pk2 = PS()
                for fc in range(4):
                    S.op("pe", "transpose", pb2[:, fc * 17:(fc + 1) * 17], tm[0:17, fc * 128:(fc + 1) * 128], ident[0:17, 0:17], reads=[tmk, "cst"], writes=[pk2])
                S.op("dve", "tensor_copy", dst[:, j * 4:(j + 1) * 4, :].rearrange("p c s -> p (c s)"), pb2[:, 0:68], reads=[pk2], writes=[dkey])

        def mod_derive(l):
            for k in range(8):
                S.op("dve", "tensor_scalar", aT[:, 0, k, :], modT[:, 8 + k, :], 1.0, vcols[:, V_N1 + l * 8 + k:V_N1 + l * 8 + k + 1], ALU.add, ALU.mult,
                     reads=["modT", "vcols"], writes=["aT0"])
                S.op("dve", "tensor_scalar", aT[:, 1, k, :], modT[:, 32 + k, :], 1.0, vcols[:, V_N2 + l * 8 + k:V_N2 + l * 8 + k + 1], ALU.add, ALU.mult,
                     reads=["modT", "vcols"], writes=["aT1"])

        def norm_mod(t, acol, shcol, akey, skey, hslot=None):
            ts = slice(t * 512, (t + 1) * 512)
            if hslot is None:
                hslot = t
            hsl = slice(hslot * 512, (hslot + 1) * 512)
            hw = [("hT", hslot)] + (MIXK if (hslot == 1 and T >= 1024) else [])
            pb, pk = PS()
            for k in range(8):
                sq, sk = BT()
                S.op("act", "activation", sq[:], xT[:, k, ts], AF.Square, reads=[("xT", k, t)], writes=[sk])
                S.op("pe", "matmul", pb[:], onesDb[:], sq[:], start=(k == 0), stop=(k == 7), reads=[sk, "onesDb"], writes=[pk])
            rs, rk = FT()
            S.op("act", "activation", rs[:], pb[:], AF.Ln, bias=epsc[:, 0:1], scale=1.0, reads=[pk, "epsc"], writes=[rk])
            S.op("act", "activation", rs[:], rs[:], AF.Exp, scale=-0.5, reads=[rk], writes=[rk])
            for k in range(8):
                tm, tk = FT()
                S.op("dve", "scalar_tensor_tensor", tm[:], xT[:, k, ts], acol(k), rs[:], ALU.mult, ALU.mult, reads=[("xT", k, t), rk, akey], writes=[tk])
                S.op("act", "activation", hTall[:, k, hsl], tm[:], AF.Identity, bias=shcol(k), scale=1.0, reads=[tk, skey], writes=hw)

        def resid_update(t, fc, pb, pk, gcol, gkey):
            ts = slice(t * 512, (t + 1) * 512)
            S.op("dve", "scalar_tensor_tensor", xT[:, fc, ts], pb[:], gcol, xT[:, fc, ts], ALU.mult, ALU.add,
                 reads=[pk, ("xT", fc, t), gkey], writes=[("xT", fc, t)])

        dbgb = {}

        def dump(nm, ap, key, rows=128):
            if not dbg:
                return
            if nm not in dbgb:
                dbgb[nm] = sb("dbgb_" + nm, [128, 512])
            f, fk = dbgb[nm], ("dbgb", nm)
            S.op("dve", "tensor_copy", f[0:rows, :], ap, reads=[key], writes=[fk])
            S.dmas("sp", [(DBG[nm][0:rows, :], f[0:rows, :])], reads=[fk], sem_key=("dbg", nm), final=True)

        def load_w(pairs, keys):
            S.dmas("pool", pairs, writes=keys, sem_key=keys[0])

        def hgrn_stage(l, t, h, wh, wkeys, state_only, par):
            ts = slice(t * 512, (t + 1) * 512)
            A = lbAB[:, l, 0, h:h + 1]
            B = lbAB[:, l, 1, h:h + 1]
            hk = [("hT", 0)]
            X = {}
            kts = KTS[par]
            kdtok_ = kts[:, 0:1024].rearrange("p (c d) -> p c d", c=8)
            vtok_ = kts[:, 1024:2048].rearrange("p (c d) -> p c d", c=8)
            ktk = ("kts", par)
            tf, tfk = P1F[2 * par], ("p1f", 2 * par)
            tq, tqk = P1F[2 * par + 1], ("p1f", 2 * par + 1)
            iTb, ibk = p1b[:, 2 * par, :], ("p1b", 2 * par)
            tg, tgk = p1b[:, 2 * par + 1, :], ("p1b", 2 * par + 1)

            def proj(j):
                pb, pk = PS()
                for k in range(8):
                    S.op("pe", "matmul", pb[:], wh[:, k, j, :], hTall[:, k, 0:512], start=(k == 0), stop=(k == 7), reads=wkeys + hk, writes=[pk])
                return pb, pk

            def P1a():
                X["pf"] = proj(1)
                if state_only:
                    X["pi"] = proj(2)
                else:
                    S.dmas("sp", [(kts, kvs_scr[t * 8 + h])], reads=[("kvs", t, h)], writes=[ktk], sem_key=("kvsl", par))
                    X["pq"] = proj(0)
                    X["pg"] = proj(3)

            def P1b():
                pf, pfk = X["pf"]
                S.op("act", "activation", tf, pf[:], AF.Tanh, scale=0.5, reads=[pfk], writes=[tfk])
                if state_only:
                    pi, pik = X["pi"]
                    S.op("act", "activation", iTb, pi[:], AF.Copy, reads=[pik], writes=[ibk])
                if not state_only:
                    pq, pqk = X["pq"]
                    pg, pgk = X["pg"]
                    S.op("act", "activation", tq, pq[:], AF.Silu, reads=[pqk], writes=[tqk])
                    S.op("act", "activation", tg, pg[:], AF.Silu, reads=[pgk], writes=[tgk])

            def P2a():
                S.op("dve", "tensor_scalar", tf, tf, A, B, ALU.mult, ALU.add, reads=[tfk, "lbAB"], writes=[tfk])
                tl, tlk = FT()
                S.op("act", "activation", tl[:], tf, AF.Ln, bias=epsl[:, 0:1], scale=1.0, reads=[tfk, "epsl"], writes=[tlk])
                tk_, tkk = FT()
                S.op("act", "activation", tk_[:], tf, AF.Identity, bias=1.0, scale=-1.0, reads=[tfk], writes=[tkk])
                cum, cumk = FT()
                S.op("dve", "tensor_tensor_scan", cum[:], maskreset, tl[:], 0.0, ALU.mult, ALU.add, reads=["cst", tlk], writes=[cumk])
                cv3 = cum[:].rearrange("p (c j) -> p c j", j=64)
                e3, e3k = FT()
                S.op("act", "activation", e3[:], cum[:], AF.Exp, reads=[cumk], writes=[e3k])
                X.update(e3=(e3, e3k))
                if state_only:
                    d2, d2k = FT()
                    S.op("pool", "tensor_tensor", d2[:].rearrange("p (c j) -> p c j", j=64), cv3, cv3[:, :, 63:64].to_broadcast([128, 8, 64]), ALU.subtract,
                         reads=[cumk], writes=[d2k])
                    S.op("act", "activation", d2[:], d2[:], AF.Exp, scale=-1.0, reads=[d2k], writes=[d2k])
                    kd, kdk = BT()
                    S.op("dve", "tensor_tensor", kd[:], tk_[:], d2[:], ALU.mult, reads=[tkk, d2k], writes=[kdk])
                    X.update(kd=(kd, kdk))
                if not state_only:
                    S.op("pool", "tensor_copy", cbs[:, :, 1:2], cv3[:, :, 31:32], reads=[cumk], writes=["cbs"])
                    d1, d1k = FT()
                    S.op("dve", "tensor_tensor", d1[:].rearrange("p (c b j) -> p c b j", b=2, j=32), cum[:].rearrange("p (c b j) -> p c b j", b=2, j=32),
                         cbs[:].unsqueeze(3).to_broadcast([128, 8, 2, 32]), ALU.subtract, reads=[cumk, "cbs"], writes=[d1k])
                    e1, e1k = FT()
                    S.op("act", "activation", e1[:], d1[:], AF.Exp, reads=[d1k], writes=[e1k])
                    S.op("act", "activation", d1[:], d1[:], AF.Exp, scale=-1.0, reads=[d1k], writes=[d1k])
                    qq, qqk = BT()
                    S.op("dve", "tensor_tensor", qq[:], tq, e1[:], ALU.mult, reads=[tqk, e1k], writes=[qqk])
                    kk_, kkk = BT()
                    S.op("dve", "scalar_tensor_tensor", kk_[:], d1[:], 1e30, tk_[:], ALU.min, ALU.mult, reads=[tkk, d1k], writes=[kkk])
                    S.op("pool", "tensor_tensor", tl[:].rearrange("p (c j) -> p c j", j=64), cv3, cv3[:, :, 31:32].to_broadcast([128, 8, 64]), ALU.subtract,
                         reads=[cumk], writes=[tlk])
                    S.op("act", "activation", tl[:], tl[:], AF.Exp, scale=-1.0, reads=[tlk], writes=[tlk])
                    k32, k32k = BT()
                    S.op("dve", "scalar_tensor_tensor", k32[:], tl[:], 1.0, tk_[:], ALU.min, ALU.mult, reads=[tkk, tlk], writes=[k32k])
                    qe, qek = BT()
                    S.op("pool", "tensor_tensor", qe[:], tq, e3[:], ALU.mult, reads=[tqk, e3k], writes=[qek])
                    X.update(qq=(qq, qqk), kk=(kk_, kkk), k32=(k32, k32k), qe=(qe, qek), d1=(d1, d1k))

            def P2b():
                e3, e3k = X["e3"]
                if state_only:
                    kd, kdk = X["kd"]
                    pkd, pkdk = PS()
                    pv, pvk = PS()
                    pkd_b = pkd[:].bitcast(BF16)
                    pv_b = pv[:].bitcast(BF16)
                    for c in range(8):
                        S.op("pe", "transpose", pkd_b[0:64, c * 128:(c + 1) * 128], kd[:, c * 64:(c + 1) * 64], identb[:], reads=[kdk, "identb"], writes=[pkdk])
                    for c in range(8):
                        S.op("pe", "transpose", pv_b[0:64, c * 128:(c + 1) * 128], iTb[:, c * 64:(c + 1) * 64], identb[:], reads=[ibk, "identb"], writes=[pvk])
                    S.op("act", "activation", kts[:, 0:1024], pkd_b[0:64, :], AF.Copy, reads=[pkdk], writes=[ktk])
                    S.op("dve", "tensor_copy", kts[:, 1024:2048], pv_b[0:64, :], reads=[pvk, ktk], writes=[ktk])
                    S.dmas("sp", [(kvs_scr[t * 8 + h], kts)], reads=[ktk], writes=[("kvs", t, h)], sem_key=("kvsw", par))
                if not state_only:
                    qq, qqk = X["qq"]
                    kk_, kkk = X["kk"]
                    k32, k32k = X["k32"]
                    qe, qek = X["qe"]
                    patt, pattk = PS()
                    patt2, patt2k = PS()
                    for c in range(8):
                        cs = slice(c * 64, (c + 1) * 64)
                        S.op("pe", "matmul", patt[0:64, cs], kk_[:, cs], qq[:, cs], start=True, stop=True, reads=[kkk, qqk], writes=[pattk])
                    for c in range(8):
                        cs = slice(c * 64, (c + 1) * 64)
                        S.op("pe", "matmul", patt2[0:64, cs], k32[:, cs], qq[:, cs], start=True, stop=True, reads=[k32k, qqk], writes=[patt2k])
                    S.op("dve", "tensor_tensor", attsb, patt[0:64, :], attmask, ALU.mult, reads=[pattk, "cst"], writes=["attsb"])
                    au = attsb.rearrange("p (c b j) -> p c b j", b=2, j=32)[:, :, 1, :]
                    pu = patt2[0:64, :].rearrange("p (c b j) -> p c b j", b=2, j=32)[:, :, 1, :]
                    S.op("dve", "scalar_tensor_tensor", au, pu, rowmask, au, ALU.mult, ALU.add, reads=[patt2k, "cst", "attsb"], writes=["attsb"])
                    po, pok = PL()
                skey = ("S", h)
                sbkey = ("Sbf", h)
                psSs = []
                for c in range(8):
                    bS = banks[4 + c // 4]
                    psSs.append((bS[:, (c % 4) * 128:(c % 4 + 1) * 128], ("ps", 4 + c // 4)))
                    S.op("pe", "matmul", psSs[c][0], kdtok_[:, c, :], vtok_[:, c, :], start=True, stop=True, reads=[ktk], writes=[psSs[c][1]])
                if not state_only:
                    for c in range(8):
                        cs = slice(c * 64, (c + 1) * 64)
                        S.op("pe", "matmul", po[:, cs], vtok_[:, c, :], attsb[:, cs], start=(c == 0), stop=False, skip_group_check=True,
                             reads=[ktk, "attsb"], writes=[pok])
                for c in range(8):
                    cs = slice(c * 64, (c + 1) * 64)
                    if not state_only:
                        S.op("pe", "matmul", po[:, cs], Sbf[:, h, :], qe[:, cs], start=False, stop=True, skip_group_check=True, reads=[sbkey, qek], writes=[pok])
                    S.op("dve", "scalar_tensor_tensor", Sst[:, h, :], Sst[:, h, :], e3[:, c * 64 + 63:c * 64 + 64], psSs[c][0], ALU.mult, ALU.add,
                         reads=[psSs[c][1], e3k, skey], writes=[skey])
                    if not state_only:
                        S.op("act", "activation", Sbf[:, h, :], Sst[:, h, :], AF.Copy, reads=[skey], writes=[sbkey])
                if state_only:
                    return
                d1, d1k = X["d1"]
                osq, osk = BT()
                S.op("act", "activation", osq[:], po[:], AF.Square, reads=[pok], writes=[osk])
                pn, pnk = PS()
                S.op("pe", "matmul", pn[:], ones128b[:], osq[:], start=True, stop=True, reads=[osk, "ones128b"], writes=[pnk])
                rs, rk = d1, d1k
                S.op("act", "activation", rs[:], pn[:], AF.Ln, bias=epsc[:, 0:1], scale=1.0, reads=[pnk, "epsc"], writes=[rk])
                S.op("act", "activation", rs[:], rs[:], AF.Exp, scale=-0.5, reads=[rk], writes=[rk])
                S.op("dve", "tensor_tensor", rs[:], po[:], rs[:], ALU.mult, reads=[pok, rk], writes=[rk])
                S.op("dve", "scalar_tensor_tensor", mixT[:, h, :], rs[:], vcols[:, V_GN + l:V_GN + l + 1], tg, ALU.mult, ALU.mult,
                     reads=[rk, tgk, "vcols"], writes=[("mix", h)] + MIXW)
            return P1a, P1b, P2a, P2b

        hctr = [0]

        def hgrn_pass(l, state_only, wout=None, wokeys=None):
            psn[0] = 4
            bctr[0] = 0
            _hgrn_pass(l, state_only, wout, wokeys)
            psn[0] = 6

        def _hgrn_pass(l, state_only, wout=None, wokeys=None):
            loaded = {}

            def issue(t, h):
                if t >= NT or (t, h) in loaded:
                    return
                wsl, wkeys = slot(2 + hctr[0] % 2)
                hctr[0] += 1
                wh = wsl.rearrange("p (k j c) -> p k j c", k=8, j=4)
                load_w([(wsl.rearrange("p (k c) -> p k c", k=8), hg_w_in[l, h].rearrange("(k p) c -> p k c", p=128))], wkeys)
                loaded[(t, h)] = (wh, wkeys)
            issue(0, 0)
            for t in range(NT):
                norm_mod(t, lambda k: aT[:, 0, k, 0:1], lambda k: modT[:, k, 0:1], "aT0", "modT", hslot=0)
                stages = {}
                wh, wkeys = loaded[(t, 0)]
                stages[0] = hgrn_stage(l, t, 0, wh, wkeys, state_only, 0)
                issue(t, 1)
                stages[0][0]()
                stages[0][1]()
                for h in range(8):
                    if h + 1 < 8:
                        wh, wkeys = loaded[(t, h + 1)]
                        stages[h + 1] = hgrn_stage(l, t, h + 1, wh, wkeys, state_only, (h + 1) % 2)
                        stages[h + 1][0]()
                    stages[h][2]()
                    if h + 1 < 8:
                        stages[h + 1][1]()
                        if h + 2 < 8:
                            issue(t, h + 2)
                        else:
                            issue(t + 1, 0)
                    stages[h][3]()
                if not state_only:
                    if dbg and l == 0 and t == 0:
                        pass
                    for fc in range(8):
                        pb, pk = PS()
                        for k in range(8):
                            S.op("pe", "matmul", pb[:], wout[:, k, fc * 128:(fc + 1) * 128], mixT[:, k, :], start=(k == 0), stop=(k == 7),
                                 reads=wokeys + [("mix", k)], writes=[pk])
                        resid_update(t, fc, pb, pk, modT[:, 16 + fc, 0:1], "modT")

        def exchange_state(l):
            S.dmas("sp", [(xs_scr[l][:, :], Sst.rearrange("p h v -> p (h v)"))], reads=SK, writes=[("xs", l)], sem_key=("xs", l))
            S.dma("pool", lambda e, l=l: [e.collective_compute("AllGather", ALU.bypass, replica_groups=PAIRS, ins=[xs_scr[l][:, :]], outs=[xg_scr[l][:, :]])],
                  reads=[("xs", l)], writes=[("xg", l)], sem_key=("cc", l), inc=1)
            S.dmas("sp", [(Sst.rearrange("p h v -> p (h v)"), xg_scr[l][0:128, :])], reads=[("xg", l)], writes=SK, sem_key=("xgl", l))
            for h in range(8):
                S.op("dve", "tensor_scalar", Sst[:, h, :], Sst[:, h, :], flag, None, ALU.mult, reads=[("S", h), "cst"], writes=[("S", h)])
                S.op("act", "activation", Sbf[:, h, :], Sst[:, h, :], AF.Copy, reads=[("S", h)], writes=[("Sbf", h)])

        def hgrn_layer(l):
            S.op("dve", "memset", Sst, 0.0, writes=SK)
            hgrn_pass(l, True)
            exchange_state(l)
            wsl, wokeys = slot(0, 2)
            wout = wsl.rearrange("p (k c) -> p k c", k=8)
            load_w([(wout, hg_w_out[l].rearrange("(k p) c -> p k c", p=128))], wokeys)
            hgrn_pass(l, False, wout, wokeys)
            S.dmas("sp", [(sp_state[l].rearrange("h k v -> k h v"), Sst)], reads=SK, sem_key=("spst", l), final=True)

        mctr = [0]

        def mlp(l):
            for t in range(NT):
                norm_mod(t, lambda k: aT[:, 1, k, 0:1], lambda k: modT[:, 24 + k, 0:1], "aT1", "modT")
            mloaded = {}

            def missue(g):
                if g >= 8 or g in mloaded:
                    return
                wsl, wk = slot(2 * (mctr[0] % 2), 2)
                mctr[0] += 1
                wup = wsl[:, 0:4096].rearrange("p (k c) -> p k c", k=8)
                wdn = wsl[:, 4096:8192].rearrange("p (k c) -> p k c", k=4)
                load_w([(wup, w_up[l].rearrange("(k p) c -> p k c", p=128)[:, :, g * 512:(g + 1) * 512]),
                        (wdn, w_down[l][g * 512:(g + 1) * 512, :].rearrange("(k p) c -> p k c", p=128))], wk)
                mloaded[g] = (wup, wdn, wk)
            missue(0)
            for hg in range(8):
                missue(hg + 1)
                wup, wdn, wk = mloaded[hg]
                for t in range(NT):
                    ts = slice(t * 512, (t + 1) * 512)
                    hids = []
                    for hc in range(4):
                        pb, pk = PS()
                        for k in range(8):
                            S.op("pe", "matmul", pb[:], wup[:, k, hc * 128:(hc + 1) * 128], hTall[:, k, ts], start=(k == 0), stop=(k == 7),
                                 reads=wk + [("hT", t)], writes=[pk])
                        r, rk = FT()
                        S.op("act", "activation", r[:], pb[:], AF.Relu, reads=[pk], writes=[rk])
                        hd, hdk = BT()
                        S.op("pool", "tensor_tensor", hd[:], r[:], r[:], ALU.mult, reads=[rk], writes=[hdk])
                        hids.append((hd, hdk))
                    for fc in range(8):
                        pb, pk = PS()
                        for hc in range(4):
                            hd, hdk = hids[hc]
                            S.op("pe", "matmul", pb[:], wdn[:, hc, fc * 128:(fc + 1) * 128], hd[:], start=(hc == 0), stop=(hc == 3),
                                 reads=wk + [hdk], writes=[pk])
                        resid_update(t, fc, pb, pk, modT[:, 40 + fc, 0:1], "modT")

        def load_tabs(t):
            S.dmas("sp", [(tabc[:], costab[:, t * 512:(t + 1) * 512]), (tabs[:], sintab[:, t * 512:(t + 1) * 512])], writes=["tab"], sem_key="tab")

        def qk_norm_rope(pb, pk, gcol, out_ap, out_keys, f32_out=None, f32_key=None):
            sq, sk = BT()
            S.op("act", "activation", sq[:], pb[:], AF.Square, reads=[pk], writes=[sk])
            pn, pnk = PS()
            S.op("pe", "matmul", pn[:], onesblkb[:], sq[:], start=True, stop=True, reads=[sk, "onesblkb"], writes=[pnk])
            rs, rk = FT()
            S.op("act", "activation", rs[:], pn[:], AF.Ln, bias=epsc[:, 0:1], scale=1.0 / 64, reads=[pnk, "epsc"], writes=[rk])
            S.op("act", "activation", rs[:], rs[:], AF.Exp, scale=-0.5, reads=[rk], writes=[rk])
            qg, qgk = BT()
            S.op("dve", "tensor_scalar", qg[:], pb[:], gcol, None, ALU.mult, reads=[pk, "vcols"], writes=[qgk])
            pp, ppk = PS()
            S.op("pe", "matmul", pp[:], permb[:], qg[:], start=True, stop=True, reads=[qgk, "permb"], writes=[ppk])
            ta, tak = FT()
            S.op("pool", "tensor_tensor", ta[:], qg[:], tabc[:], ALU.mult, reads=[qgk, "tab"], writes=[tak])
            tb, tbk = FT()
            S.op("dve", "tensor_tensor", tb[:], pp[:], tabs[:], ALU.mult, reads=[ppk, "tab"], writes=[tbk])
            S.op("pool", "tensor_tensor", ta[:], ta[:], tb[:], ALU.add, reads=[tak, tbk], writes=[tak])
            if f32_out is not None:
                S.op("dve", "tensor_tensor", f32_out, ta[:], rs[:], ALU.mult, reads=[tak, rk], writes=[f32_key])
                S.op("act", "activation", out_ap, f32_out, AF.Copy, reads=[f32_key], writes=out_keys)
            else:
                S.op("dve", "tensor_tensor", out_ap, ta[:], rs[:], ALU.mult, reads=[tak, rk], writes=out_keys)

        def kv_phase():
            if do_sample:
                S.op("dve", "tensor_copy", kvmodT[:, :, 0], kvmodP[:], reads=["kvmodP"], writes=["kvmodT"])
                S.op("dve", "tensor_copy", aT[:, 2, :, 0], aKVP[:], reads=["aKVP"], writes=["aT2"])
            else:
                ada(kv_w_ada, kv_b_ada, 4, kvmodT, "kvmodT")
                for k in range(8):
                    S.op("dve", "tensor_scalar", aT[:, 2, k, :], kvmodT[:, 8 + k, :], 1.0, vcols[:, V_KVN + k:V_KVN + k + 1], ALU.add, ALU.mult,
                         reads=["kvmodT", "vcols"], writes=["aT2"])
            wsl, wkk = slot(0)
            wkv = wsl.rearrange("p (k c) -> p k c", k=8)
            load_w([(wkv, w_kv.rearrange("(k p) c -> p k c", p=128))], wkk)
            gk = vcols[:, V_KN:V_KN + 1]
            ALIAS = SK + [("Sbf", h) for h in range(8)] + ["kdtok", "vtok", "attsb", ("kts", 0)] + [("p1f", i_) for i_ in range(3)]
            for t in range(NT):
                ts = slice(t * 512, (t + 1) * 512)
                norm_mod(t, lambda k: aT[:, 2, k, 0:1], lambda k: kvmodT[:, k, 0:1], "aT2", "kvmodT", hslot=0)
                load_tabs(t)
                for m in range(2):
                    pb, pk = PS()
                    for k in range(8):
                        S.op("pe", "matmul", pb[:], wkv[:, k, m * 128:(m + 1) * 128], hTall[:, k, 0:512], start=(k == 0), stop=(k == 7),
                             reads=wkk + [("hT", 0)], writes=[pk])
                    dst = KT[:, m, 128 + t * 512:128 + (t + 1) * 512]
                    if t == NT - 1:
                        kf, kfk = FT()
                        qk_norm_rope(pb, pk, gk, dst, [("KT", m, t)] + ALIAS, f32_out=kf[:], f32_key=kfk)
                        S.op("pool", "tensor_copy", klast[:, m, :], kf[:, 384:512], reads=[kfk], writes=[("klast", m)])
                    else:
                        qk_norm_rope(pb, pk, gk, dst, [("KT", m, t)] + ALIAS)
                for b in range(4):
                    blk = t * 4 + b
                    pb, pk = PS()
                    for k in range(8):
                        S.op("pe", "matmul", pb[:, 0:256], hTall[:, k, b * 128:(b + 1) * 128], wkv[:, k, 256:512],
                             start=(k == 0), stop=(k == 7), reads=wkk + [("hT", 0)], writes=[pk])
                    S.op("act", "activation", Vtok[:, blk + 1, :], pb[:, 0:256], AF.Copy, reads=[pk], writes=[("V", blk + 1)] + ALIAS)
                    if blk == NBLK - 1:
                        vf, vfk = FT()
                        S.op("dve", "tensor_copy", vf[:, 0:256], pb[:, 0:256], reads=[pk], writes=[vfk])
                        S.dmas("sp", [(vp[:, :], vf[:, 0:256])], reads=[vfk], sem_key="vp", final=True)
            ko, kok = FT()
            pb, pk = PS()
            for m in range(2):
                S.op("pe", "transpose", pb[:, m * 128:(m + 1) * 128], klast[:, m, :], ident, reads=[("klast", m), "cst"], writes=[pk])
            S.op("dve", "tensor_copy", ko[:, 0:256], pb[:, 0:256], reads=[pk], writes=[kok])
            S.dmas("sp", [(kp[:, :], ko[:, 0:256])], reads=[kok], sem_key="kp", final=True)
            tlast = NT - 1
            S.dmas("sp", [(kv_scr[:, 0:256].rearrange("p (m t) -> p m t", m=2), KT[:, :, T:T + 128]), (kv_scr[:, 256:512], Vtok[:, NBLK, :])],
                   reads=[("KT", 0, tlast), ("KT", 1, tlast), ("V", NBLK)], writes=["kvscr"], sem_key="kvscr")
            S.dma("pool", lambda e: [e.collective_compute("AllGather", ALU.bypass, replica_groups=PAIRS, ins=[kv_scr[:, :]], outs=[kvg_scr[:, :]])],
                  reads=["kvscr"], writes=["kvg"], sem_key="cckv", inc=1)
            S.dmas("sp", [(KT[:, :, 0:128], kvg_scr[0:128, 0:256].rearrange("p (m t) -> p m t", m=2)), (Vtok[:, 0, :], kvg_scr[0:128, 256:512])],
                   reads=["kvg"], writes=[("KTpre",), ("V", 0)], sem_key="kvgl")

        def attn_layer(l):
            j = l - 2
            wsl, wqk = slot(0, 2)
            wq = wsl.rearrange("p (k c) -> p k c", k=8)
            load_w([(wq, w_q[j].rearrange("(k p) c -> p k c", p=128))], wqk)
            wsl2, wok = slot(2, 2)
            wo = wsl2.rearrange("p (c f) -> p c f", c=8)
            load_w([(wo, w_o[j].rearrange("(c p) f -> p c f", p=128))], wok)
            gq = vcols[:, V_QN + j:V_QN + j + 1]
            for t in range(NT):
                ts = slice(t * 512, (t + 1) * 512)
                norm_mod(t, lambda k: aT[:, 0, k, 0:1], lambda k: modT[:, k, 0:1], "aT0", "modT", hslot=0)
                load_tabs(t)
                for c in range(8):
                    pb, pk = PS()
                    for k in range(8):
                        S.op("pe", "matmul", pb[:], wq[:, k, c * 128:(c + 1) * 128], hTall[:, k, 0:512], start=(k == 0), stop=(k == 7),
                             reads=wqk + [("hT", 0)], writes=[pk])
                    qT, qTk = BT()
                    qk_norm_rope(pb, pk, gq, qT[:], [qTk])
                    m = c // 4
                    st_ = {}

                    def S1(qb):
                        n = t * 4 + qb
                        qs = slice(qb * 128, (qb + 1) * 128)
                        kreads = [("KTpre",)] if n == 0 else [("KT", m, (n - 1) // 4)]
                        kreads.append(("KT", m, n // 4))
                        pbanks = [PS(), PS()]
                        for a in range(2):
                            rows = slice(a * 64, (a + 1) * 64)
                            pss, pssk = pbanks[a]
                            for kb in range(2):
                                ko_ = (n + kb) * 128
                                S.op("pe", "matmul", pss[:, kb * 128:(kb + 1) * 128], KT[rows, m, ko_:ko_ + 128], qT[rows, qs],
                                     start=True, stop=True, reads=kreads + [qTk], writes=[pssk])
                        st_[qb] = {"pbanks": pbanks}

                    def S2(qb):
                        n = t * 4 + qb
                        pbanks = st_[qb]["pbanks"]
                        pe_, pek = FT()
                        pev = pe_[:].rearrange("p (kb a q) -> p kb a q", kb=2, a=2)
                        for a in range(2):
                            pss, pssk = pbanks[a]
                            S.op("act", "activation", pev[:, :, a, :], pss[:, 0:256].rearrange("p (kb q) -> p kb q", kb=2), AF.Exp, scale=SCALE,
                                 reads=[pssk], writes=[pek])
                        pT, pTk = BT()
                        mk = mask4f[:] if n == 0 else mask4
                        S.op("pool", "tensor_tensor", pT[:], pe_[:], mk, ALU.mult, reads=[pek, "cst", "mask4f"], writes=[pTk])
                        st_[qb]["pT"] = (pT, pTk)

                    def S3(qb):
                        n = t * 4 + qb
                        pT, pTk = st_[qb]["pT"]
                        po, pok = PS()
                        S.op("pe", "matmul", po[:, 0:256], Vtok[:, n, m * 128:(m + 1) * 128], pT[:, 0:256], start=True, stop=False, reads=[("V", n), pTk], writes=[pok])
                        S.op("pe", "matmul", po[:, 0:256], Vtok[:, n + 1, m * 128:(m + 1) * 128], pT[:, 256:512], start=False, stop=True, reads=[("V", n + 1), pTk], writes=[pok])
                        S.op("pe", "matmul", po[:, 256:512], onesb[:], pT[:, 0:256], start=True, stop=False, reads=["onesb", pTk], writes=[pok])
                        S.op("pe", "matmul", po[:, 256:512], onesb[:], pT[:, 256:512], start=False, stop=True, reads=["onesb", pTk], writes=[pok])
                        st_[qb]["po"] = (po, pok)

                    def S4(qb):
                        qs = slice(qb * 128, (qb + 1) * 128)
                        po, pok = st_[qb]["po"]
                        for a in range(2):
                            rows = slice(a * 64, (a + 1) * 64)
                            rk_ = ("rd", a)
                            S.op("act", "activation", rd[rows, :], po[rows, 256 + a * 128:256 + (a + 1) * 128], AF.Ln, bias=esink[rows, j, c:c + 1], scale=1.0,
                                 reads=[pok, "esink"], writes=[rk_])
                            S.op("act", "activation", rd[rows, :], rd[rows, :], AF.Exp, scale=-1.0, reads=[rk_], writes=[rk_])
                            S.op("dve", "tensor_tensor", mixT[rows, c, qs], po[rows, a * 128:(a + 1) * 128], rd[rows, :], ALU.mult, reads=[pok, rk_], writes=[("mix", c)] + MIXW)
                    S1(0)
                    S2(0)
                    for qb in range(4):
                        if qb + 1 < 4:
                            S1(qb + 1)
                        S3(qb)
                        if qb + 1 < 4:
                            S2(qb + 1)
                        S4(qb)
                for fc in range(8):
                    pb, pk = PS()
                    for c in range(8):
                        S.op("pe", "matmul", pb[:], wo[:, c, fc * 128:(fc + 1) * 128], mixT[:, c, :], start=(c == 0), stop=(c == 7),
                             reads=wok + [("mix", c)], writes=[pk])
                    resid_update(t, fc, pb, pk, modT[:, 16 + fc, 0:1], "modT")


        SMPW = 9216
        if T >= 2048:
            smp = xT[:].rearrange("p k t -> p (k t)")[:, 0:SMPW]
        else:
            smp = sb("smp", [128, SMPW])[:]
        soff = [0]

        def salloc(words, shape=None, dt=F32):
            a = smp[:, soff[0]:soff[0] + words]
            soff[0] += words
            assert soff[0] <= SMPW
            if dt == BF16:
                a = a.bitcast(BF16)
            return a

        def sop(eng, name, *args, reads=(), writes=(), **kw):
            return S.op(eng, name, *args, reads=list(reads) + ["SMPREGION"], writes=writes, **kw)

        def sdmas(queue, pairs, reads=(), writes=(), **kw):
            return S.dmas(queue, pairs, reads=list(reads) + ["SMPREGION"], writes=writes, **kw)

        xsT = salloc(128).rearrange("p (k s) -> p k s", k=8)
        hsT = salloc(64, dt=BF16).rearrange("p (k s) -> p k s", k=8)
        mixsT = salloc(64, dt=BF16).rearrange("p (k s) -> p k s", k=8)
        qsT = salloc(128).rearrange("p (k s) -> p k s", k=8)
        qsTb = salloc(64, dt=BF16).rearrange("p (k s) -> p k s", k=8)
        qpad = [salloc(64, dt=BF16).rearrange("p (k s) -> p k s", k=8) for _ in range(2)]
        ksT = salloc(32).rearrange("p (m s) -> p m s", m=2)
        vsT = salloc(32).rearrange("p (m s) -> p m s", m=2)
        NSF = 16
        sfs = [salloc(128) for _ in range(NSF)]
        sfc = [0]

        def SF():
            i = sfc[0]
            sfc[0] = (i + 1) % NSF
            return sfs[i], ("sf", i)
        NSB = 8
        sbs = [salloc(64, dt=BF16) for _ in range(NSB)]
        sbc = [0]

        def SB():
            i = sbc[0]
            sbc[0] = (i + 1) % NSB
            return sbs[i], ("sb", i)
        sin_b = [salloc(512).rearrange("p (s v) -> p s v", s=4) for _ in range(2)]
        sout_b = [salloc(512).rearrange("p (s v) -> p s v", s=4) for _ in range(2)]
        km = salloc(1024, dt=BF16)[0:16, :].rearrange("p (s d) -> p s d", s=16)
        ktv = salloc(128, dt=BF16)[0:16, :].rearrange("p (j d) -> p j d", j=2)
        ckT = salloc(512, dt=BF16).rearrange("p (m s j) -> p m s j", m=2, s=4)
        cvt = salloc(512, dt=BF16).rearrange("p (s d) -> p s d", s=4)
        ckin = [salloc(256) for _ in range(2)]
        kvnew = salloc(512)[0:16, :]
        pTs = [salloc(64, dt=BF16) for _ in range(2)]
        tab16 = salloc(32)

        def s_norm_mod(ai, shT, akey, skey):
            sq, sk = SF()
            sop("act", "activation", sq, xsT.rearrange("p k s -> p (k s)"), AF.Square, reads=["xsT"], writes=[sk])
            pb, pk = PS()
            for k in range(8):
                sop("pe", "matmul", pb[:, 0:16], onesD[:], sq[:, k * 16:(k + 1) * 16], start=(k == 0), stop=(k == 7), reads=[sk, "onesD"], writes=[pk])
            rs, rk = SF()
            sop("act", "activation", rs[:, 0:16], pb[:, 0:16], AF.Ln, bias=epsc[:, 0:1], scale=1.0, reads=[pk, "epsc"], writes=[rk])
            sop("act", "activation", rs[:, 0:16], rs[:, 0:16], AF.Exp, scale=-0.5, reads=[rk], writes=[rk])
            t1, t1k = SF()
            t13 = t1.rearrange("p (k s) -> p k s", k=8)
            sop("dve", "tensor_tensor", t13, xsT, rs[:, 0:16].unsqueeze(1).to_broadcast([128, 8, 16]), ALU.mult, reads=["xsT", rk], writes=[t1k])
            sop("dve", "tensor_tensor", t13, t13, aT[:, ai, :, 1:17], ALU.mult, reads=[t1k, akey], writes=[t1k])
            sop("dve", "tensor_tensor", hsT, t13, shT, ALU.add, reads=[t1k, skey], writes=["hsT"])

        def s_resid(pb, pk, gT, gkey):
            t1, t1k = SF()
            t13 = t1.rearrange("p (k s) -> p k s", k=8)
            sop("dve", "tensor_tensor", t13, pb[:, 0:128].rearrange("p (k s) -> p k s", k=8), gT, ALU.mult, reads=[pk, gkey], writes=[t1k])
            sop("dve", "tensor_tensor", xsT, xsT, t13, ALU.add, reads=["xsT", t1k], writes=["xsT"])

        def s_proj_out(w3, wkeys, srcT, srckey, gT, gkey):
            pb, pk = PS()
            for fc in range(8):
                for k in range(8):
                    sop("pe", "matmul", pb[:, fc * 16:(fc + 1) * 16], w3[:, k, fc * 128:(fc + 1) * 128], srcT[:, k, :], start=(k == 0), stop=(k == 7),
                        reads=wkeys + [srckey], writes=[pk])
            s_resid(pb, pk, gT, gkey)

        def s_rope(pb, pk, gcol, out_f32, okey, n=16):
            sq, sk = SF()
            sop("act", "activation", sq[:, 0:n], pb, AF.Square, reads=[pk], writes=[sk])
            pn, pnk = PS()
            sop("pe", "matmul", pn[:, 0:n], onesblk, sq[:, 0:n], start=True, stop=True, reads=[sk, "cst"], writes=[pnk])
            rs, rk = SF()
            sop("act", "activation", rs[:, 0:n], pn[:, 0:n], AF.Ln, bias=epsc[:, 0:1], scale=1.0 / 64, reads=[pnk, "epsc"], writes=[rk])
            sop("act", "activation", rs[:, 0:n], rs[:, 0:n], AF.Exp, scale=-0.5, reads=[rk], writes=[rk])
            qg, qgk = SF()
            sop("dve", "tensor_scalar", qg[:, 0:n], pb, gcol, None, ALU.mult, reads=[pk, "vcols"], writes=[qgk])
            pp, ppk = PS()
            sop("pe", "matmul", pp[:, 0:n], perm, qg[:, 0:n], start=True, stop=True, reads=[qgk, "cst"], writes=[ppk])
            ta, tak = SF()
            sop("dve", "tensor_tensor", ta[:, 0:n], qg[:, 0:n], tab16[:, 0:16], ALU.mult, reads=[qgk, "tab16"], writes=[tak])
            tb, tbk = SF()
            sop("dve", "tensor_tensor", tb[:, 0:n], pp[:, 0:n], tab16[:, 16:32], ALU.mult, reads=[ppk, "tab16"], writes=[tbk])
            sop("dve", "tensor_tensor", ta[:, 0:n], ta[:, 0:n], tb[:, 0:n], ALU.add, reads=[tak, tbk], writes=[tak])
            sop("dve", "tensor_tensor", out_f32, ta[:, 0:n], rs[:, 0:n], ALU.mult, reads=[tak, rk], writes=[okey])

        def s_hgrn_head(l, h, wh, wkeys):
            A = lbAB[:, l, 0, h:h + 1]
            B = lbAB[:, l, 1, h:h + 1]
            pb, pk = PS()
            for j in range(4):
                for k in range(8):
                    sop("pe", "matmul", pb[:, j * 16:(j + 1) * 16], wh[:, k, j, :], hsT[:, k, :], start=(k == 0), stop=(k == 7), reads=wkeys + ["hsT"], writes=[pk])
            qs_, qsk = SF()
            sop("act", "activation", qs_[:, 0:16], pb[:, 0:16], AF.Silu, reads=[pk], writes=[qsk])
            gs_, gsk = SF()
            sop("act", "activation", gs_[:, 0:16], pb[:, 48:64], AF.Silu, reads=[pk], writes=[gsk])
            fg, fgk = SF()
            sop("act", "activation", fg[:, 0:16], pb[:, 16:32], AF.Tanh, scale=0.5, reads=[pk], writes=[fgk])
            kv2, kv2k = SF()
            sop("act", "activation", kv2[:, 16:32], pb[:, 32:48], AF.Copy, reads=[pk], writes=[kv2k])
            sop("dve", "tensor_scalar", fg[:, 0:16], fg[:, 0:16], A, B, ALU.mult, ALU.add, reads=[fgk, "lbAB"], writes=[fgk])
            sop("dve", "tensor_scalar", kv2[:, 0:16], fg[:, 0:16], -1.0, 1.0, ALU.mult, ALU.add, reads=[fgk, kv2k], writes=[kv2k])
            pt, ptk = PS()
            sop("pe", "transpose", pt[0:16, 0:128], kv2[:, 0:16], ident, reads=[kv2k, "cst"], writes=[ptk])
            sop("pe", "transpose", pt[0:16, 128:256], kv2[:, 16:32], ident, reads=[kv2k, "cst"], writes=[ptk])
            sop("dve", "tensor_copy", ktv.rearrange("p j d -> p (j d)"), pt[0:16, 0:256], reads=[ptk], writes=["ktv"])
            sop("dve", "tensor_tensor", km, ktv[:, 0, :].unsqueeze(1).to_broadcast([16, 16, 128]),
                identb[0:16, 0:16].unsqueeze(2).to_broadcast([16, 16, 128]), ALU.mult, reads=["ktv", "identb"], writes=["km"])
            po, pok = PL()
            for bi in range(4):
                si_, so_ = sin_b[bi % 2], sout_b[bi % 2]
                sik, sok = ("sin", bi % 2), ("sout", bi % 2)
                sdmas("sp", [(si_, st_in[l, bi * 4:(bi + 1) * 4, h].rearrange("s k v -> k s v"))], writes=[sik], sem_key=sik)
                for s4 in range(4):
                    s_ = bi * 4 + s4
                    pkv, pkvk = PS()
                    sop("pe", "matmul", pkv[:, 0:128], km[:, s_, :], ktv[:, 1, :], start=True, stop=True, reads=["km", "ktv"], writes=[pkvk])
                    sop("dve", "scalar_tensor_tensor", so_[:, s4, :], si_[:, s4, :], fg[:, s_:s_ + 1], pkv[:, 0:128], ALU.mult, ALU.add,
                        reads=[sik, fgk, pkvk], writes=[sok])
                    sop("pe", "matmul", po[:, s_:s_ + 1], so_[:, s4, :], qs_[:, s_:s_ + 1], start=True, stop=True, reads=[sok, qsk], writes=[pok])
                sdmas("sp", [(ss[l, bi * 4:(bi + 1) * 4, h].rearrange("s k v -> k s v"), so_)], reads=[sok], sem_key=sok, final=True)
            osq, osk = SF()
            sop("act", "activation", osq[:, 0:16], po[:, 0:16], AF.Square, reads=[pok], writes=[osk])
            pn, pnk = PS()
            sop("pe", "matmul", pn[:, 0:16], ones128[:], osq[:, 0:16], start=True, stop=True, reads=[osk, "ones128"], writes=[pnk])
            rs, rk = SF()
            sop("act", "activation", rs[:, 0:16], pn[:, 0:16], AF.Ln, bias=epsc[:, 0:1], scale=1.0, reads=[pnk, "epsc"], writes=[rk])
            sop("act", "activation", rs[:, 0:16], rs[:, 0:16], AF.Exp, scale=-0.5, reads=[rk], writes=[rk])
            sop("dve", "tensor_tensor", rs[:, 0:16], po[:, 0:16], rs[:, 0:16], ALU.mult, reads=[pok, rk], writes=[rk])
            sop("dve", "scalar_tensor_tensor", mixsT[:, h, :], rs[:, 0:16], vcols[:, V_GN + l:V_GN + l + 1], gs_[:, 0:16], ALU.mult, ALU.mult,
                reads=[rk, gsk, "vcols"], writes=["mixsT"])

        def s_hgrn_layer(l):
            s_norm_mod(0, modT[:, 0:8, 1:17], "aT0", "modT")
            hl = {}

            def hissue(h):
                if h >= 8 or h in hl:
                    return
                wsl, wkeys = slot(2 + hctr[0] % 2)
                hctr[0] += 1
                wh = wsl.rearrange("p (k j c) -> p k j c", k=8, j=4)
                load_w([(wsl.rearrange("p (k c) -> p k c", k=8), hg_w_in[l, h].rearrange("(k p) c -> p k c", p=128))], wkeys)
                hl[h] = (wh, wkeys)
            hissue(0)
            for h in range(8):
                hissue(h + 1)
                wh, wkeys = hl[h]
                s_hgrn_head(l, h, wh, wkeys)
            wsl, wokeys = slot(0, 2)
            wout = wsl.rearrange("p (k c) -> p k c", k=8)
            load_w([(wout, hg_w_out[l].rearrange("(k p) c -> p k c", p=128))], wokeys)
            s_proj_out(wout, wokeys, mixsT, "mixsT", modT[:, 16:24, 1:17], "modT")

        def s_mlp(l):
            s_norm_mod(1, modT[:, 24:32, 1:17], "aT1", "modT")
            mloaded = {}

            def missue(g):
                if g >= 8 or g in mloaded:
                    return
                wsl, wk = slot(2 * (mctr[0] % 2), 2)
                mctr[0] += 1
                wup = wsl[:, 0:4096].rearrange("p (k c) -> p k c", k=8)
                wdn = wsl[:, 4096:8192].rearrange("p (k c) -> p k c", k=4)
                load_w([(wup, w_up[l].rearrange("(k p) c -> p k c", p=128)[:, :, g * 512:(g + 1) * 512]),
                        (wdn, w_down[l][g * 512:(g + 1) * 512, :].rearrange("(k p) c -> p k c", p=128))], wk)
                mloaded[g] = (wup, wdn, wk)
            missue(0)
            for hg in range(8):
                missue(hg + 1)
                wup, wdn, wk = mloaded[hg]
                pb, pk = PS()
                for hc in range(4):
                    for k in range(8):
                        sop("pe", "matmul", pb[:, hc * 16:(hc + 1) * 16], wup[:, k, hc * 128:(hc + 1) * 128], hsT[:, k, :], start=(k == 0), stop=(k == 7),
                            reads=wk + ["hsT"], writes=[pk])
                r, rk = SF()
                sop("act", "activation", r[:, 0:64], pb[:, 0:64], AF.Relu, reads=[pk], writes=[rk])
                hd, hdk = SB()
                sop("dve", "tensor_tensor", hd[:, 0:64], r[:, 0:64], r[:, 0:64], ALU.mult, reads=[rk], writes=[hdk])
                pb2, pk2 = PS()
                for fc in range(8):
                    for hc in range(4):
                        sop("pe", "matmul", pb2[:, fc * 16:(fc + 1) * 16], wdn[:, hc, fc * 128:(fc + 1) * 128], hd[:, hc * 16:(hc + 1) * 16],
                            start=(hc == 0), stop=(hc == 3), reads=wk + [hdk], writes=[pk2])
                s_resid(pb2, pk2, modT[:, 40:48, 1:17], "modT")

        def s_kv_phase():
            ada(kv_w_ada, kv_b_ada, 4, kvmodT, "kvmodT")
            for k in range(8):
                S.op("dve", "tensor_scalar", aT[:, 2, k, :], kvmodT[:, 8 + k, :], 1.0, vcols[:, V_KVN + k:V_KVN + k + 1], ALU.add, ALU.mult,
                     reads=["kvmodT", "vcols"], writes=["aT2"])
            wsl, wkk = slot(0)
            wkv = wsl.rearrange("p (k c) -> p k c", k=8)
            load_w([(wkv, w_kv.rearrange("(k p) c -> p k c", p=128))], wkk)
            S.op("dve", "tensor_copy", kvmodP[:], kvmodT[:, :, 0], reads=["kvmodT"], writes=["kvmodP"])
            S.op("dve", "tensor_copy", aKVP[:], aT[:, 2, :, 0], reads=["aT2"], writes=["aKVP"])
            s_norm_mod(2, kvmodT[:, 0:8, 1:17], "aT2", "kvmodT")
            gk = vcols[:, V_KN:V_KN + 1]
            for m in range(2):
                pb, pk = PS()
                for k in range(8):
                    sop("pe", "matmul", pb[:, 0:16], wkv[:, k, m * 128:(m + 1) * 128], hsT[:, k, :], start=(k == 0), stop=(k == 7), reads=wkk + ["hsT"], writes=[pk])
                s_rope(pb[:, 0:16], pk, gk, ksT[:, m, :], "ksT")
                pb, pk = PS()
                for k in range(8):
                    sop("pe", "matmul", pb[:, 0:16], wkv[:, k, 256 + m * 128:256 + (m + 1) * 128], hsT[:, k, :], start=(k == 0), stop=(k == 7), reads=wkk + ["hsT"], writes=[pk])
                sop("act", "activation", vsT[:, m, :], pb[:, 0:16], AF.Copy, reads=[pk], writes=["vsT"])
            pt, ptk = PS()
            for m in range(2):
                sop("pe", "transpose", pt[0:16, m * 128:(m + 1) * 128], ksT[:, m, :], ident, reads=["ksT", "cst"], writes=[ptk])
                sop("pe", "transpose", pt[0:16, 256 + m * 128:256 + (m + 1) * 128], vsT[:, m, :], ident, reads=["vsT", "cst"], writes=[ptk])
            sop("dve", "tensor_copy", kvnew, pt[0:16, :], reads=[ptk], writes=["kvnew"])
            sdmas("sp", [(ks[:, 127, :], kvnew[:, 0:256]), (vs[:, 127, :], kvnew[:, 256:512])], reads=["kvnew"], sem_key="kvnew", final=True)
            S.dmas("sp", [(ks[:, 0:127, :], ck[:, 1:128, :]), (vs[:, 0:127, :], cv[:, 1:128, :])], sem_key="cachecp", final=True)

        def s_attn_layer(l):
            j = l - 2
            wsl, wqk = slot(0, 2)
            wq = wsl.rearrange("p (k c) -> p k c", k=8)
            load_w([(wq, w_q[j].rearrange("(k p) c -> p k c", p=128))], wqk)
            wsl2, wok = slot(2, 2)
            wo = wsl2.rearrange("p (c f) -> p c f", c=8)
            load_w([(wo, w_o[j].rearrange("(c p) f -> p c f", p=128))], wok)
            gq = vcols[:, V_QN + j:V_QN + j + 1]
            s_norm_mod(0, modT[:, 0:8, 1:17], "aT0", "modT")
            for c in range(8):
                pb, pk = PS()
                for k in range(8):
                    sop("pe", "matmul", pb[:, 0:16], wq[:, k, c * 128:(c + 1) * 128], hsT[:, k, :], start=(k == 0), stop=(k == 7), reads=wqk + ["hsT"], writes=[pk])
                s_rope(pb[:, 0:16], pk, gq, qsT[:, c, :], "qsT")
            sop("act", "activation", qsTb, qsT, AF.Copy, reads=["qsT"], writes=["qsTb"])
            for a in range(2):
                sop("dve", "memset", qpad[a], 0.0, writes=[("qpad", a)])
                rows = slice(a * 64, (a + 1) * 64)
                sop("dve", "tensor_copy", qpad[a][rows], qsT[rows], reads=["qsT"], writes=[("qpad", a)])
            pss, pssk = PL()
            po, pok = PL()
            for bi in range(4):
                sdmas("pool", [(cvt, cv[bi * 4:(bi + 1) * 4].rearrange("s j d -> j s d"))], writes=["cvt"], sem_key="cvt")
                for s4 in range(4):
                    s_ = bi * 4 + s4
                    ci, cik = ckin[s4 % 2], ("ckin", s4 % 2)
                    sdmas("sp", [(ci, ck[s_])], writes=[cik], sem_key=cik)
                    pt, ptk = PS()
                    for m in range(2):
                        sop("pe", "transpose", pt[:, m * 128:(m + 1) * 128], ci[:, m * 128:(m + 1) * 128], ident, reads=[cik, "cst"], writes=[ptk])
                    sop("act", "activation", ckT[:, :, s4, :], pt[:, 0:256].rearrange("p (m j) -> p m j", m=2), AF.Copy, reads=[ptk], writes=["ckT"])
                    for a in range(2):
                        for m in range(2):
                            c0_ = a * 128 + s_ * 8 + 4 * m
                            sop("pe", "matmul", pss[:, c0_:c0_ + 4], ckT[:, m, s4, :], qpad[a][:, 4 * m:4 * m + 4, s_],
                                start=True, stop=True, reads=["ckT", ("qpad", a)], writes=[pssk])
                for a in range(2):
                    cols = slice(bi * 32, (bi + 1) * 32)
                    pe_, pek = SF()
                    sop("act", "activation", pe_[:, 0:32], pss[:, a * 128 + bi * 32:a * 128 + (bi + 1) * 32], AF.Exp, scale=SCALE, reads=[pssk], writes=[pek])
                    sop("dve", "tensor_scalar", pTs[a][:, cols], pe_[:, 0:32], jmask, None, ALU.mult, reads=[pek, "cst"], writes=[("pTs", a)])
                for s4 in range(4):
                    s_ = bi * 4 + s4
                    for m in range(2):
                        for a in range(2):
                            cs_ = slice(s_ * 8 + 4 * m, s_ * 8 + 4 * m + 4)
                            sop("pe", "matmul", po[:, a * 128 + s_ * 8 + 4 * m:a * 128 + s_ * 8 + 4 * m + 4], cvt[:, s4, m * 128:(m + 1) * 128], pTs[a][:, cs_],
                                start=True, stop=True, reads=["cvt", ("pTs", a)], writes=[pok])
            for a in range(2):
                sop("pe", "matmul", po[:, 256 + a * 128:256 + (a + 1) * 128], onesb[:], pTs[a][:, 0:128], start=True, stop=True, reads=["onesb", ("pTs", a)], writes=[pok])
            pr, prk = SF()
            sop("dve", "tensor_tensor", pr.rearrange("p (m i s) -> p m i s", m=2, i=4), qsT.rearrange("p (m i) s -> p m i s", m=2),
                ksT.unsqueeze(2).to_broadcast([128, 2, 4, 16]), ALU.mult, reads=["qsT", "ksT"], writes=[prk])
            psn, psnk = PS()
            sop("pe", "matmul", psn[:, 0:128], onesblk, pr, start=True, stop=True, reads=[prk, "cst"], writes=[psnk])
            pn_, pnk_ = SF()
            sop("act", "activation", pn_, psn[:, 0:128], AF.Exp, scale=SCALE, reads=[psnk], writes=[pnk_])
            on_, onk = SF()
            sop("dve", "tensor_tensor", on_.rearrange("p (m i s) -> p m i s", m=2, i=4), pn_.rearrange("p (m i s) -> p m i s", m=2, i=4),
                vsT.unsqueeze(2).to_broadcast([128, 2, 4, 16]), ALU.mult, reads=[pnk_, "vsT"], writes=[onk])
            num, numk = SF()
            den, denk = SF()
            for a in range(2):
                rows = slice(a * 64, (a + 1) * 64)
                pov = po[rows, a * 128:(a + 1) * 128].rearrange("p (s c) -> p c s", c=8)
                dnv = po[rows, 256 + a * 128:256 + (a + 1) * 128].rearrange("p (s c) -> p c s", c=8)
                n3 = num[rows, :].rearrange("p (c s) -> p c s", c=8)
                d3 = den[rows, :].rearrange("p (c s) -> p c s", c=8)
                sop("dve", "tensor_tensor", n3, pov, on_[rows, :].rearrange("p (c s) -> p c s", c=8), ALU.add, reads=[pok, onk], writes=[numk])
                sop("dve", "tensor_tensor", d3, dnv, pn_[rows, :].rearrange("p (c s) -> p c s", c=8), ALU.add, reads=[pok, pnk_], writes=[denk])
                sop("dve", "tensor_tensor", d3, d3, esink[rows, j, :].unsqueeze(2).to_broadcast([64, 8, 16]), ALU.add, reads=[denk, "esink"], writes=[denk])
                sop("dve", "reciprocal", den[rows, :], den[rows, :], reads=[denk], writes=[denk])
                sop("dve", "tensor_tensor", mixsT[rows, :, :], n3, d3, ALU.mult, reads=[numk, denk], writes=["mixsT"])
            s_proj_out(wo, wok, mixsT, "mixsT", modT[:, 16:24, 1:17], "modT")

        def sample_phase():
            sop("dve", "memset", smp, 0.0, writes=["xsT", "hsT", "mixsT", "qsT", "qsTb", "ksT", "vsT", "km", "ktv", "ckT", "cvt", "kvnew",
                                                    ("pTs", 0), ("pTs", 1), "tab16"] + [("sf", i) for i in range(NSF)] + [("sb", i) for i in range(NSB)])
            sdmas("sp", [(tab16, cs16[:, :])], writes=["tab16"], sem_key="tab16")
            xa, xak = FT()
            xb_, xbk = FT()
            S.dmas("sp", [(xa[0:16, :], xs[:, 0:512]), (xb_[0:16, :], xs[:, 512:1024])], writes=[xak, xbk], sem_key=xak)
            pb, pk = PS()
            for k in range(8):
                src = (xa if k < 4 else xb_)[0:16, (k % 4) * 128:(k % 4 + 1) * 128]
                sop("pe", "transpose", pb[:, k * 16:(k + 1) * 16], src, ident[0:16, 0:16], reads=[xak, xbk, "cst"], writes=[pk])
            sop("dve", "tensor_copy", xsT.rearrange("p k s -> p (k s)"), pb[:, 0:128], reads=[pk], writes=["xsT"])
            for l in range(4):
                ada(w_ada[l], b_ada[l], 12, modT, "modT")
                mod_derive(l)
                S.op("dve", "tensor_copy", modP[:, l, :], modT[:, :, 0], reads=["modT"], writes=[("modP", l)])
                S.op("dve", "tensor_copy", aP[:, l, :, :], aT[:, 0:2, :, 0], reads=["aT0", "aT1"], writes=[("aP", l)])
                if l == 2:
                    s_kv_phase()
                if l < 2:
                    s_hgrn_layer(l)
                else:
                    s_attn_layer(l)
                s_mlp(l)
            pb, pk = PS()
            pb2, pk2 = PS()
            for k in range(8):
                dstp = pb if k < 4 else pb2
                dk_ = pk if k < 4 else pk2
                sop("pe", "transpose", dstp[0:16, (k % 4) * 128:(k % 4 + 1) * 128], xsT[:, k, :], ident, reads=["xsT", "cst"], writes=[dk_])
            ya, yak = FT()
            yb, ybk = FT()
            sop("dve", "tensor_copy", ya[0:16, :], pb[0:16, :], reads=[pk], writes=[yak])
            sop("dve", "tensor_copy", yb[0:16, :], pb2[0:16, :], reads=[pk2], writes=[ybk])
            S.dmas("sp", [(ys[:, 0:512], ya[0:16, :]), (ys[:, 512:1024], yb[0:16, :])], reads=[yak, ybk], sem_key=yak, final=True)

        def final_out():
            for b in range(NBLK):
                t = b // 4
                for g in range(2):
                    yo, yk = FT()
                    pb, pk = PS()
                    for kk in range(4):
                        k = g * 4 + kk
                        S.op("pe", "transpose", pb[:, kk * 128:(kk + 1) * 128], xT[:, k, b * 128:(b + 1) * 128], ident, reads=[("xT", k, t), "cst"], writes=[pk])
                    if g == 0:
                        S.op("dve", "tensor_copy", yo[:], pb[:], reads=[pk], writes=[yk])
                    else:
                        S.op("act", "activation", yo[:], pb[:], AF.Copy, reads=[pk], writes=[yk])
                    S.dmas("sp", [(yp[b * 128:(b + 1) * 128, g * 512:(g + 1) * 512], yo[:])], reads=[yk], sem_key=yk, final=True)

        import os
        STG = os.environ.get("KSTAGE", "full")
        if do_sample:
            sample_phase()
        load_x()
        for l in range(4):
            if STG == "io":
                break
            if do_sample:
                S.op("dve", "tensor_copy", modT[:, :, 0], modP[:, l, :], reads=[("modP", l)], writes=["modT"])
                S.op("dve", "tensor_copy", aT[:, 0:2, :, 0], aP[:, l, :, :], reads=[("aP", l)], writes=["aT0", "aT1"])
            else:
                ada(w_ada[l], b_ada[l], 12, modT, "modT")
                mod_derive(l)
            if STG == "ada":
                break
            if l == 2:
                kv_phase()
            if l < 2:
                if STG == "mlp0":
                    pass
                elif STG == "passA":
                    S.op("dve", "memset", Sst, 0.0, writes=SK)
                    hgrn_pass(l, True)
                    S.dmas("sp", [(sp_state[l].rearrange("h k v -> k h v"), Sst)], reads=SK, sem_key=("spst", l), final=True)
                    break
                else:
                    hgrn_layer(l)
            else:
                attn_layer(l)
            if STG == "hgrn0":
                break
            mlp(l)
            if STG in ("l0", "mlp0"):
                break
            if STG == "l1" and l == 1:
                break
            if STG == "l2" and l == 2:
                break
        final_out()

        S.emit()
        print("ops", len(S.ops), "sem counts", S.max_counts, "dma sems", len(S.dma_count), "sbuf left", nc.sbuf_bytes_remaining)
    return nc


_CACHE = {}


def make_in_maps(inputs, T, seq):
    f = lambda a: np.ascontiguousarray(np.asarray(a, dtype=np.float32))
    xp_all = f(inputs["x_prompt"])
    xs_all = f(inputs["x_sample"]).reshape(NSAMP, D)
    cp = f(inputs["c_prompt"])
    cs = f(inputs["c_sample"])
    st_all = f(inputs["state_hgrn"])
    ck_all = f(inputs["cache_k"]).reshape(NSAMP, 128, 256)
    cv_all = f(inputs["cache_v"]).reshape(NSAMP, 128, 256)
    shared = {k: f(inputs[k]) for k in ("w_ada", "b_ada", "norm1_g", "norm2_g", "hg_w_in", "hg_w_out", "hg_lower_bounds",
                                        "hg_gn_g", "kv_w_ada", "kv_b_ada", "kv_norm_g", "w_kv", "k_norm_g", "w_q",
                                        "q_norm_g", "sinks", "w_o", "w_up", "w_down")}
    shared["hg_w_in"] = np.ascontiguousarray(shared["hg_w_in"].reshape(2, D, 4, 8, 128).transpose(0, 3, 1, 2, 4).reshape(2, 8, D, 512))
    shared["w_q"] = np.ascontiguousarray(shared["w_q"].reshape(2, D, 2, 2, 4, 64).transpose(0, 1, 2, 4, 3, 5).reshape(2, D, D))
    shared["w_o"] = np.ascontiguousarray(shared["w_o"].reshape(2, 2, 2, 4, 64, D).transpose(0, 1, 3, 2, 4, 5).reshape(2, D, D))
    ct16, st16 = rope_tables(np.full((16,), PAST, np.int64))
    cs16 = np.ascontiguousarray(np.concatenate([ct16, st16], axis=1))
    maps = []
    for c in range(8):
        b, half = c // 2, c % 2
        m = dict(shared)
        m["xp"] = np.ascontiguousarray(xp_all[b, half * T:(half + 1) * T])
        sl = slice(c * NS, (c + 1) * NS)
        m["c17"] = np.ascontiguousarray(np.concatenate([cp[b:b + 1], cs[sl]], axis=0))
        m["xs"] = np.ascontiguousarray(xs_all[sl])
        m["st_in"] = np.ascontiguousarray(st_all[:, sl])
        m["ck"] = np.ascontiguousarray(ck_all[sl])
        m["cv"] = np.ascontiguousarray(cv_all[sl])
        m["consts"] = host_consts(float(half))
        ct, stb = rope_tables(np.arange(half * T, (half + 1) * T))
        m["costab"] = ct
        m["sintab"] = stb
        m["cs16"] = cs16
        maps.append(m)
    return maps


def run(inputs, T, dbg=False):
    if T not in _CACHE:
        _CACHE[T] = build(T, dbg=dbg)
    nc = _CACHE[T]
    maps = make_in_maps(inputs, T, 2 * T)
    res = run_bass_kernel_spmd(nc, maps, core_ids=list(range(8)))
    R = res.results
    global LAST_RESULTS
    LAST_RESULTS = R
    seq = 2 * T
    y_prompt = np.zeros((NB, seq, D), np.float32)
    hg_p = np.zeros((2, NB, 8, 128, 128), np.float32)
    k_p = np.zeros((NB, 128, 4, 64), np.float32)
    v_p = np.zeros((NB, 128, 4, 64), np.float32)
    y_sample = np.zeros((NSAMP, 1, D), np.float32)
    hg_s = np.zeros((2, NSAMP, 8, 128, 128), np.float32)
    k_s = np.zeros((NSAMP, 128, 4, 64), np.float32)
    v_s = np.zeros((NSAMP, 128, 4, 64), np.float32)
    for c in range(8):
        b, half = c // 2, c % 2
        r = R[c]
        y_prompt[b, half * T:(half + 1) * T] = r["yp"]
        if half == 1:
            hg_p[:, b] = r["sp_state"]
            k_p[b] = r["kp"].reshape(128, 4, 64)
            v_p[b] = r["vp"].reshape(128, 4, 64)
        sl = slice(c * NS, (c + 1) * NS)
        y_sample[sl, 0] = r["ys"]
        hg_s[:, sl] = r["ss"]
        k_s[sl] = r["ks"].reshape(NS, 128, 4, 64)
        v_s[sl] = r["vs"].reshape(NS, 128, 4, 64)
    return (y_prompt, y_sample, hg_p, k_p, v_p, hg_s, k_s, v_s)


def kernel(**inputs):
    return run(inputs, SEQ // 2)
```

```python
import math
from contextlib import ExitStack

import numpy as np
import concourse.bass as bass
import concourse.mybir as mybir
from concourse.bass_utils import run_bass_kernel_spmd

F32 = mybir.dt.float32
BF16 = mybir.dt.bfloat16
ALU = mybir.AluOpType
AF = mybir.ActivationFunctionType

D = 1024
SEQ = 4096
NB = 4
NSAMP = 128
NS = 16
WINDOW = 128
PAST = 8192
EPS = 1e-6
SCALE = 1.0 / 8.0

SAME_ENGINE_SYNC = True
FOLD_WAITS = True


class Op:
    __slots__ = ("id", "eng", "fn", "deps", "is_dma", "sem_key", "n_dma", "waits", "inc_amt",
                 "needs_inc", "lidx", "count", "dma_val", "vc", "final")


class Sched:
    ENGS = ("pe", "act", "dve", "pool", "sp")

    def __init__(self, nc):
        self.nc = nc
        self.ops = []
        self.by_eng = {e: [] for e in self.ENGS}
        self.last_w = {}
        self.readers = {}
        self.dma_count = {}
        self.out_dmas = []
        self.bulk = set()

    def _track(self, op, reads, writes):
        pr = [k for k in reads if isinstance(k, tuple) and k and k[0] == "ps"]
        if pr:
            reads = [k for k in reads if k not in pr]
            writes = list(writes) + pr
        deps = set()
        for k in reads:
            w = self.last_w.get(k)
            if w is not None:
                deps.add(w)
        for k in writes:
            w = self.last_w.get(k)
            if w is not None:
                deps.add(w)
            for r in self.readers.get(k, ()):
                deps.add(r)
        for k in reads:
            self.readers.setdefault(k, []).append(op.id)
        for k in writes:
            self.last_w[k] = op.id
            self.readers[k] = []
        deps.discard(op.id)
        op.deps = deps

    def op(self, eng, name, *args, reads=(), writes=(), **kw):
        def fn(e, name=name, args=args, kw=kw):
            return getattr(e, name)(*args, **kw)
        return self.add(eng, fn, reads, writes)

    def dmas(self, queue, pairs, reads=(), writes=(), sem_key=None, final=False, bulk=False):
        pairs = list(pairs)

        def fn(e, pairs=pairs):
            return [e.dma_start(out=o, in_=i) for (o, i) in pairs]
        return self.dma(queue, fn, reads, writes, sem_key=sem_key, n=len(pairs), final=final, bulk=bulk)

    def add(self, eng, fn, reads=(), writes=()):
        op = Op()
        op.id = len(self.ops)
        op.eng = eng
        op.fn = fn
        op.is_dma = False
        op.sem_key = None
        op.n_dma = 0
        op.needs_inc = False
        op.final = False
        self._track(op, reads, writes)
        self.ops.append(op)
        self.by_eng[eng].append(op)
        return op

    def dma(self, queue, fn, reads=(), writes=(), sem_key=None, n=1, final=False, inc=16, bulk=False):
        op = Op()
        op.id = len(self.ops)
        op.eng = queue
        op.fn = fn
        op.is_dma = True
        op.sem_key = sem_key
        op.n_dma = n
        op.inc_amt = inc
        op.needs_inc = True
        op.final = final
        if bulk:
            self.bulk.add(sem_key)
        self.dma_count[sem_key] = self.dma_count.get(sem_key, 0) + inc * n
        op.dma_val = self.dma_count[sem_key]
        self._track(op, reads, writes)
        self.ops.append(op)
        self.by_eng[queue].append(op)
        if final:
            self.out_dmas.append(op)
        return op

    def finalize(self):
        for op in self.ops:
            if op.is_dma and op.sem_key in self.bulk:
                op.dma_val = self.dma_count[op.sem_key]
        lcount = {e: 0 for e in self.ENGS}
        for op in self.ops:
            if not op.is_dma:
                lcount[op.eng] += 1
                op.lidx = lcount[op.eng]
        evc = {e: {} for e in self.ENGS}
        for op in self.ops:
            E = op.eng
            my = evc[E]
            waits = []
            for d in sorted(op.deps):
                dop = self.ops[d]
                if dop.is_dma:
                    key = ("D", dop.sem_key)
                    val = dop.dma_val
                else:
                    if dop.eng == E and (E == "pe" or not SAME_ENGINE_SYNC) and not op.is_dma:
                        continue
                    key = ("E", dop.eng)
                    val = dop.lidx
                if my.get(key, 0) >= val:
                    continue
                waits.append(d)
                dop.needs_inc = True
                for k, v in dop.vc.items():
                    if my.get(k, 0) < v:
                        my[k] = v
            op.waits = waits
            vc = dict(my)
            if op.is_dma:
                k = ("D", op.sem_key)
                vc[k] = max(vc.get(k, 0), op.dma_val)
            else:
                vc[("E", E)] = op.lidx
            op.vc = vc
        self.final_waits = {}
        for op in self.out_dmas:
            self.final_waits[op.sem_key] = max(self.final_waits.get(op.sem_key, 0), op.dma_val)
        cnt = {e: 0 for e in self.ENGS}
        for op in self.ops:
            if not op.is_dma:
                if op.needs_inc:
                    cnt[op.eng] += 1
                op.count = cnt[op.eng]
        self.max_counts = cnt

    def emit(self):
        nc = self.nc
        self.finalize()
        with ExitStack() as st:
            esem = {e: st.enter_context(nc.semaphore("es_" + e)) for e in ("pe", "act", "dve", "pool")}
            dsem = {}
            for i, k in enumerate(self.dma_count):
                dsem[k] = st.enter_context(nc.semaphore("ds_%d" % i))
            block = st.enter_context(nc.Block())
            ops = self.ops

            def run(eng_name, eng):
                for op in self.by_eng[eng_name]:
                    wmap = {}
                    for d in op.waits:
                        dop = ops[d]
                        if dop.is_dma:
                            s = dsem[dop.sem_key]
                            v = dop.dma_val
                        else:
                            s = esem[dop.eng]
                            v = dop.count
                        key = id(s)
                        if key not in wmap or wmap[key][1] < v:
                            wmap[key] = (s, v)
                    wl = list(wmap.values())
                    fold = None
                    if FOLD_WAITS and wl and not (op.is_dma and op.inc_amt != 16):
                        fold = wl.pop()
                    for s, v in wl:
                        eng.wait_ge(s, v)
                    r = op.fn(eng)
                    if fold is not None:
                        (r[0] if op.is_dma else r)._wait_ge(fold[0], fold[1])
                    if op.is_dma:
                        assert len(r) == op.n_dma, (len(r), op.n_dma)
                        for ins in r:
                            ins.then_inc(dsem[op.sem_key], op.inc_amt)
                    elif op.needs_inc:
                        r.then_inc(esem[eng_name], 1)
                if eng_name == "sp":
                    for k, v in self.final_waits.items():
                        eng.wait_ge(dsem[k], v)

            @block.tensor
            def _(e):
                run("pe", e)

            @block.scalar
            def _(e):
                run("act", e)

            @block.vector
            def _(e):
                run("dve", e)

            @block.gpsimd
            def _(e):
                run("pool", e)

            @block.sync
            def _(e):
                run("sp", e)


C_ID, C_MR, C_AM, C_M4, C_OB, C_PM, C_JM, C_FL, C_RM, C_N = 0, 128, 640, 1152, 1664, 1792, 1920, 1921, 1922, 1923


def host_consts(flag):
    c = np.zeros((128, C_N), np.float32)
    c[:, C_ID:C_ID + 128] = np.eye(128, dtype=np.float32)
    mr = np.ones((512,), np.float32)
    mr[::64] = 0.0
    c[:, C_MR:C_MR + 512] = mr[None, :]
    s = np.arange(64)[:, None]
    t = np.arange(64)[None, :]
    am = ((s <= t) & ((s // 32) == (t // 32))).astype(np.float32)
    c[0:64, C_AM:C_AM + 512] = np.tile(am, (1, 8))
    c[0:32, C_RM] = 1.0
    j = np.arange(128)[:, None]
    q = np.arange(128)[None, :]
    mprev = (j > q).astype(np.float32)
    mcur = (j <= q).astype(np.float32)
    c[:, C_M4:C_M4 + 512] = np.concatenate([mprev, mprev, mcur, mcur], axis=1)
    ob = np.zeros((128, 128), np.float32)
    ob[0:64, 0:64] = 1.0
    ob[64:128, 64:128] = 1.0
    c[:, C_OB:C_OB + 128] = ob
    pm = np.zeros((128, 128), np.float32)
    for p in range(128):
        d = p % 64
        partner = p + 32 if d < 32 else p - 32
        pm[p, partner] = 1.0
    c[:, C_PM:C_PM + 128] = pm
    c[:, C_JM] = 1.0
    c[0, C_JM] = 0.0
    c[:, C_FL] = flag
    return c


def rope_tables(pos):
    half = 32
    inv = (np.float32(10000.0) ** (-np.arange(half, dtype=np.float32) / np.float32(half))).astype(np.float32)
    ang = pos.astype(np.float32)[None, :] * inv[:, None]
    cos = np.cos(ang).astype(np.float32)
    sin = np.sin(ang).astype(np.float32)
    ct = np.concatenate([cos, cos, cos, cos], axis=0)
    st = np.concatenate([-sin, sin, -sin, sin], axis=0)
    return np.ascontiguousarray(ct), np.ascontiguousarray(st)


def build(T, do_sample=True, dbg=False):
    NT = T // 512
    NBLK = T // 128
    nc = bass.Bass("TRN2", target_bir_lowering=False)

    def din(name, shape, dt=F32):
        return nc.dram_tensor(name, list(shape), dt, kind="ExternalInput").ap()

    def dout(name, shape, dt=F32):
        return nc.dram_tensor(name, list(shape), dt, kind="ExternalOutput").ap()

    def dint(name, shape, dt=F32):
        return nc.dram_tensor(name, list(shape), dt, kind="Internal").ap()

    xp = din("xp", [T, D])
    c17 = din("c17", [17, D])
    xs = din("xs", [NS, D])
    st_in = din("st_in", [2, NS, 8, 128, 128])
    ck = din("ck", [NS, 128, 256])
    cv = din("cv", [NS, 128, 256])
    w_ada = din("w_ada", [4, D, 6 * D])
    b_ada = din("b_ada", [4, 6 * D])
    norm1_g = din("norm1_g", [4, D])
    norm2_g = din("norm2_g", [4, D])
    hg_w_in = din("hg_w_in", [2, 8, D, 512])
    hg_w_out = din("hg_w_out", [2, D, D])
    hg_lb = din("hg_lower_bounds", [2, D])
    hg_gn = din("hg_gn_g", [2, 128])
    kv_w_ada = din("kv_w_ada", [D, 2 * D])
    kv_b_ada = din("kv_b_ada", [2 * D])
    kv_norm_g = din("kv_norm_g", [D])
    w_kv = din("w_kv", [D, 512])
    k_norm_g = din("k_norm_g", [64])
    w_q = din("w_q", [2, D, D])
    q_norm_g = din("q_norm_g", [2, 64])
    sinks = din("sinks", [2, 16])
    w_o = din("w_o", [2, D, D])
    w_up = din("w_up", [4, D, 4 * D])
    w_down = din("w_down", [4, 4 * D, D])
    consts = din("consts", [128, C_N])
    costab = din("costab", [128, T])
    sintab = din("sintab", [128, T])
    cs16 = din("cs16", [128, 32])

    yp = dout("yp", [T, D])
    ys = dout("ys", [NS, D])
    sp_state = dout("sp_state", [2, 8, 128, 128])
    kp = dout("kp", [128, 256])
    vp = dout("vp", [128, 256])
    ss = dout("ss", [2, NS, 8, 128, 128])
    ks = dout("ks", [NS, 128, 256])
    vs = dout("vs", [NS, 128, 256])

    DBG = {}
    if dbg:
        for nm in ("mix", "o", "qq", "kk", "e3", "att", "cum", "rs"):
            DBG[nm] = dout("dbg_" + nm, [128, 8, 512] if nm == "mix" else [128, 512])
    xs_scr = [dint("xs_scr%d" % i, [128, 1024]) for i in range(2)]
    xg_scr = [dint("xg_scr%d" % i, [256, 1024]) for i in range(2)]
    kvs_scr = dint("kvs_scr", [(T // 512) * 8, 64, 2048], BF16)
    kv_scr = dint("kv_scr", [128, 512], BF16)
    kvg_scr = dint("kvg_scr", [256, 512], BF16)
    PAIRS = [[0, 1], [2, 3], [4, 5], [6, 7]]

    with ExitStack() as st:
        st.enter_context(nc.allow_low_precision("bf16 matmul operands, fp32 accumulation"))
        st.enter_context(nc.allow_non_contiguous_dma("small strided parameter loads"))

        def sb(name, shape, dt=F32):
            return st.enter_context(nc.sbuf_tensor(name, list(shape), dt))

        S = Sched(nc)
        banks = [st.enter_context(nc.psum_tensor("bank%d" % i, [128, 512], F32)) for i in range(8)]
        bctr = [0]
        psn = [6]

        def PS():
            i = bctr[0] % psn[0]
            bctr[0] = (i + 1) % psn[0]
            return banks[i], ("ps", i)

        lctr = [0]

        def PL():
            i = 6 + lctr[0] % 2
            lctr[0] += 1
            return banks[i], ("ps", i)

        NF = 9
        fts = [sb("ft%d" % i, [128, 512]) for i in range(NF)]
        fctr = [0]

        def FT():
            i = fctr[0]
            fctr[0] = (i + 1) % NF
            return fts[i], ("ft", i)

        NBT = 8
        bts = [sb("bt%d" % i, [128, 512], BF16) for i in range(NBT)]
        bbctr = [0]

        def BT():
            i = bbctr[0]
            bbctr[0] = (i + 1) % NBT
            return bts[i], ("bt", i)

        xT = sb("xT", [128, 8, T])
        cst = sb("cst", [128, C_N])
        NSLOT = 4
        arena = sb("arena", [128, NSLOT * 4096], BF16)

        def slot(i, n=1):
            return arena[:, i * 4096:(i + n) * 4096], [("W", i + r) for r in range(n)]

        R32 = sb("R32", [128, 16384], BF16)
        hTall = R32[:, :].rearrange("p (k t) -> p k t", k=8) if T == 2048 else None
        if T != 2048:
            hTall = R32[:, 0:8 * T].rearrange("p (k t) -> p k t", k=8)
        MIXK = [("mix", c_) for c_ in range(8)]
        if T >= 1024:
            mixT = hTall[:, :, 512:1024]
            MIXW = [("hT", 1)]
        else:
            mixT = sb("mixT", [128, 8, 512], BF16)[:]
            MIXW = []
        tabc = sb("tabc", [128, 512])
        tabs = sb("tabs", [128, 512])
        identb = sb("identb", [128, 128], BF16)
        onesD = sb("onesD", [128, 128])
        ones128 = sb("ones128", [128, 128])
        onesb = sb("onesb", [128, 128], BF16)
        onesDb = sb("onesDb", [128, 128], BF16)
        ones128b = sb("ones128b", [128, 128], BF16)
        onesblkb = sb("onesblkb", [128, 128], BF16)
        permb = sb("permb", [128, 128], BF16)
        epsc = sb("epsc", [128, 1])
        epsl = sb("epsl", [128, 1])
        vrows = sb("vrows", [128, 128])
        vcols = sb("vcols", [128, 128])
        cT = sb("cT", [128, 8, 17])
        cTb = sb("cTb", [128, 8, 17], BF16)
        modT = sb("modT", [128, 48, 17])
        kvmodT = sb("kvmodT", [128, 16, 17])
        aT = sb("aT", [128, 3, 8, 17])
        lbAB = sb("lbAB", [128, 2, 2, 8])
        modP = sb("modP", [128, 4, 48])
        aP = sb("aP", [128, 4, 2, 8])
        kvmodP = sb("kvmodP", [128, 16])
        aKVP = sb("aKVP", [128, 8])
        lbt = sb("lbt", [128, 2, 8])
        esink = sb("esink", [128, 2, 8])
        mask4f = sb("mask4f", [128, 512])
        ones17 = sb("ones17", [1, 17])
        rd = sb("rd", [128, 128])
        KVW = max(2 * (T + 128) + (NBLK + 1) * 256, 8704)
        KVR = sb("KVR", [128, KVW], BF16)
        KT = KVR[:, 0:2 * (T + 128)].rearrange("p (m t) -> p m t", m=2)
        Vtok = KVR[:, 2 * (T + 128):2 * (T + 128) + (NBLK + 1) * 256].rearrange("p (b c) -> p b c", c=256)
        kdtok = KVR[0:64, 0:1024].rearrange("p (c d) -> p c d", c=8)
        vtok = KVR[0:64, 1024:2048].rearrange("p (c d) -> p c d", c=8)
        attsb = KVR[0:64, 2048:2560]
        Sbf = KVR[:, 2560:3584].rearrange("p (h v) -> p h v", h=8)
        Sst = KVR[:, 3584:5632].bitcast(F32).rearrange("p (h v) -> p h v", h=8)
        klast = sb("klast", [128, 2, 128])
        kt1 = sb("kt1", [64, 2048], BF16)
        KTS = [KVR[0:64, 0:2048], kt1[:]]
        p1x = sb("p1x", [128, 512])
        P1F = [KVR[:, 5632:6656].bitcast(F32), KVR[:, 6656:7680].bitcast(F32), KVR[:, 7680:8704].bitcast(F32), p1x[:]]
        p1b = sb("p1b", [128, 4, 512], BF16)

        ident = cst[:, C_ID:C_ID + 128]
        maskreset = cst[:, C_MR:C_MR + 512]
        attmask = cst[0:64, C_AM:C_AM + 512]
        mask4 = cst[:, C_M4:C_M4 + 512]
        onesblk = cst[:, C_OB:C_OB + 128]
        perm = cst[:, C_PM:C_PM + 128]
        jmask = cst[:, C_JM:C_JM + 1]
        flag = cst[:, C_FL:C_FL + 1]
        rowmask = cst[0:64, C_RM:C_RM + 1]
        cbs = sb("cbs", [128, 8, 2])

        V_N1, V_N2, V_KVN, V_LB, V_GN, V_QN, V_KN = 0, 32, 64, 72, 88, 90, 92
        SK = [("S", h) for h in range(8)]

        S.dmas("sp", [(cst[:], consts[:, :])], writes=["cst"], sem_key="ld", bulk=True)
        S.op("dve", "memset", vrows[:], 0.0, writes=["vrows"])
        prs = [(vrows[V_N1:V_N1 + 32, :], norm1_g.rearrange("l (k p) -> (l k) p", p=128)),
               (vrows[V_N2:V_N2 + 32, :], norm2_g.rearrange("l (k p) -> (l k) p", p=128)),
               (vrows[V_KVN:V_KVN + 8, :], kv_norm_g.rearrange("(k p) -> k p", p=128)),
               (vrows[V_LB:V_LB + 16, :], hg_lb.rearrange("l (k p) -> (l k) p", p=128)),
               (vrows[V_GN:V_GN + 2, :], hg_gn[:, :])]
        for l in range(2):
            for a in range(2):
                prs.append((vrows[V_QN + l:V_QN + l + 1, a * 64:(a + 1) * 64], q_norm_g[l:l + 1, :]))
        for a in range(2):
            prs.append((vrows[V_KN:V_KN + 1, a * 64:(a + 1) * 64], k_norm_g.rearrange("(o d) -> o d", o=1)))
        S.dmas("sp", prs, writes=["vrows"], sem_key="ld2", bulk=True)
        prs = []
        for l in range(2):
            sv = sinks[l].rearrange("(m a i) -> a m i", m=2, a=2, i=4)
            for a in range(2):
                prs.append((esink[a * 64:(a + 1) * 64, l, :].rearrange("p (m i) -> p m i", m=2), sv[a:a + 1].to_broadcast([64, 2, 4])))
        S.dmas("sp", prs, writes=["esink"], sem_key="ld", bulk=True)
        S.op("act", "activation", esink[:], esink[:], AF.Exp, reads=["esink"], writes=["esink"])

        S.op("dve", "memset", cbs[:], 0.0, writes=["cbs"])
        S.op("dve", "memset", onesD[:], 1.0 / D, writes=["onesD"])
        S.op("dve", "memset", ones128[:], 1.0 / 128, writes=["ones128"])
        S.op("dve", "memset", onesb[:], 1.0, writes=["onesb"])
        S.op("dve", "memset", onesDb[:], 1.0 / D, writes=["onesDb"])
        S.op("dve", "memset", ones128b[:], 1.0 / 128, writes=["ones128b"])
        S.op("dve", "tensor_copy", onesblkb[:], cst[:, C_OB:C_OB + 128], reads=["cst"], writes=["onesblkb"])
        S.op("dve", "tensor_copy", permb[:], cst[:, C_PM:C_PM + 128], reads=["cst"], writes=["permb"])
        S.op("dve", "memset", epsc[:], EPS, writes=["epsc"])
        S.op("dve", "memset", epsl[:], 1e-7, writes=["epsl"])
        S.op("dve", "memset", ones17[:], 1.0, writes=["ones17"])
        S.op("dve", "tensor_copy", identb[:], ident, reads=["cst"], writes=["identb"])
        S.op("dve", "tensor_scalar", mask4f[:, 0:256], mask4[:, 0:256], flag, None, ALU.mult, reads=["cst"], writes=["mask4f"])
        S.op("dve", "tensor_copy", mask4f[:, 256:512], mask4[:, 256:512], reads=["cst"], writes=["mask4f"])

        pb, pk = PS()
        S.op("pe", "transpose", pb[:, 0:128], vrows[:], ident, reads=["vrows", "cst"], writes=[pk])
        S.op("dve", "tensor_copy", vcols[:], pb[:, 0:128], reads=[pk], writes=["vcols"])
        S.op("dve", "tensor_tensor", lbt[:, 1, :], vcols[:, V_LB:V_LB + 8], vcols[:, V_LB + 8:V_LB + 16], ALU.subtract, reads=["vcols"], writes=["lbt"])
        S.op("act", "activation", lbt[:, 1, :], lbt[:, 1, :], AF.Exp, reads=["lbt"], writes=["lbt"])
        S.op("dve", "tensor_scalar", lbt[:, 1, :], lbt[:, 1, :], 1.0, None, ALU.add, reads=["lbt"], writes=["lbt"])
        S.op("dve", "reciprocal", lbt[:, 1, :], lbt[:, 1, :], reads=["lbt"], writes=["lbt"])
        S.op("dve", "tensor_scalar", lbt[:, 0, :], lbt[:, 1, :], -1.0, 1.0, ALU.mult, ALU.add, reads=["lbt"], writes=["lbt"])
        S.op("dve", "tensor_tensor", lbt[:, 0, :], lbt[:, 0, :], lbt[:, 0, :], ALU.subtract, reads=["lbt"], writes=["lbt"])
        for l in range(2):
            S.op("dve", "tensor_scalar", lbAB[:, l, 0, :], lbt[:, l, :], -0.5, 0.5, ALU.mult, ALU.add, reads=["lbt"], writes=["lbAB"])
            S.op("dve", "tensor_tensor", lbAB[:, l, 1, :], lbAB[:, l, 0, :], lbt[:, l, :], ALU.add, reads=["lbt", "lbAB"], writes=["lbAB"])

        c0, c0k = FT()
        c1, c1k = FT()
        S.dmas("sp", [(c0[0:17, :], c17[:, 0:512]), (c1[0:17, :], c17[:, 512:1024])], writes=[c0k, c1k], sem_key="ld", bulk=True)
        S.op("act", "activation", c0[0:17, :], c0[0:17, :], AF.Silu, reads=[c0k], writes=[c0k])
        S.op("act", "activation", c1[0:17, :], c1[0:17, :], AF.Silu, reads=[c1k], writes=[c1k])
        pb, pk = PS()
        for k in range(8):
            src = (c0 if k < 4 else c1)[0:17, (k % 4) * 128:(k % 4 + 1) * 128]
            S.op("pe", "transpose", pb[:, k * 17:(k + 1) * 17], src, ident[0:17, 0:17], reads=[c0k, c1k, "cst"], writes=[pk])
        S.op("dve", "tensor_copy", cT[:].rearrange("p k s -> p (k s)"), pb[:, 0:136], reads=[pk], writes=["cT"])
        S.op("dve", "tensor_copy", cTb[:], cT[:], reads=["cT"], writes=["cTb"])

        def load_x():
          for b in range(NBLK):
            t = b // 4
            xa, xak = FT()
            xb_, xbk = FT()
            S.dmas("sp", [(xa[:], xp[b * 128:(b + 1) * 128, 0:512]), (xb_[:], xp[b * 128:(b + 1) * 128, 512:1024])],
                   writes=[xak, xbk], sem_key=xak)
            for g in range(2):
                src, srck = (xa, xak) if g == 0 else (xb_, xbk)
                pb, pk = PS()
                for kk in range(4):
                    S.op("pe", "transpose", pb[:, kk * 128:(kk + 1) * 128], src[:, kk * 128:(kk + 1) * 128], ident, reads=[srck, "cst"], writes=[pk])
                dst = xT[:, g * 4:(g + 1) * 4, b * 128:(b + 1) * 128]
                wr = [("xT", g * 4 + kk, t) for kk in range(4)] + ["SMPREGION"]
                if g == 0:
                    S.op("dve", "tensor_copy", dst, pb[:].rearrange("p (k t) -> p k t", k=4), reads=[pk], writes=wr)
                else:
                    S.op("act", "activation", dst, pb[:].rearrange("p (k t) -> p k t", k=4), AF.Copy, reads=[pk], writes=wr)

        actr = [0]

        def ada(wsrc, bsrc, ncolt, dst, dkey):
            for j in range(ncolt):
                wsl, wk = slot(actr[0] % 4)
                actr[0] += 1
                wv = wsl.rearrange("p (k c) -> p k c", k=8)
                S.dmas("pool", [(wv, wsrc.rearrange("(k p) c -> p k c", p=128)[:, :, j * 512:(j + 1) * 512])], writes=wk, sem_key=wk[0])
                br, bk = FT()
                S.dmas("sp", [(br[0:1, :], bsrc[j * 512:(j + 1) * 512].rearrange("(o c) -> o c", o=1))], writes=[bk], sem_key=bk)
                pb, pk = PS()
                for k in range(8):
                    S.op("pe", "matmul", pb[0:17, :], cTb[:, k, :], wv[:, k, :], start=(k == 0), stop=False, reads=wk + ["cTb"], writes=[pk])
                S.op("pe", "matmul", pb[0:17, :], ones17[:], br[0:1, :], start=False, stop=True, reads=[bk, "ones17"], writes=[pk])
                tm, tmk = FT()
                S.op("dve", "tensor_copy", tm[0:17, :], pb[0:17, :], reads=[pk], writes=[tmk])
                pb2, pk2 = PS()
                for fc in range(4):
                    S.op("pe", "transpose", pb2[:, fc * 17:(fc + 1) * 17], tm[0:17, fc * 128:(fc + 1) * 128], ident[0:17, 0:17], reads=[tmk, "cst"], writes=[pk2])
                S.op("dve", "tensor_copy", dst[:, j * 4:(j + 1) * 4, :].rearrange("p c s -> p (c s)"), pb2[:, 0:68], reads=[pk2], writes=[dkey])

        def mod_derive(l):
            for k in range(8):
                S.op("dve", "tensor_scalar", aT[:, 0, k, :], modT[:, 8 + k, :], 1.0, vcols[:, V_N1 + l * 8 + k:V_N1 + l * 8 + k + 1], ALU.add, ALU.mult,
                     reads=["modT", "vcols"], writes=["aT0"])
                S.op("dve", "tensor_scalar", aT[:, 1, k, :], modT[:, 32 + k, :], 1.0, vcols[:, V_N2 + l * 8 + k:V_N2 + l * 8 + k + 1], ALU.add, ALU.mult,
                     reads=["modT", "vcols"], writes=["aT1"])

        def norm_mod(t, acol, shcol, akey, skey, hslot=None):
            ts = slice(t * 512, (t + 1) * 512)
            if hslot is None:
                hslot = t
            hsl = slice(hslot * 512, (hslot + 1) * 512)
            hw = [("hT", hslot)] + (MIXK if (hslot == 1 and T >= 1024) else [])
            pb, pk = PS()
            for k in range(8):
                sq, sk = BT()
                S.op("act", "activation", sq[:], xT[:, k, ts], AF.Square, reads=[("xT", k, t)], writes=[sk])
                S.op("pe", "matmul", pb[:], onesDb[:], sq[:], start=(k == 0), stop=(k == 7), reads=[sk, "onesDb"], writes=[pk])
            rs, rk = FT()
            S.op("act", "activation", rs[:], pb[:], AF.Ln, bias=epsc[:, 0:1], scale=1.0, reads=[pk, "epsc"], writes=[rk])
            S.op("act", "activation", rs[:], rs[:], AF.Exp, scale=-0.5, reads=[rk], writes=[rk])
            for k in range(8):
                tm, tk = FT()
                S.op("dve", "scalar_tensor_tensor", tm[:], xT[:, k, ts], acol(k), rs[:], ALU.mult, ALU.mult, reads=[("xT", k, t), rk, akey], writes=[tk])
                S.op("act", "activation", hTall[:, k, hsl], tm[:], AF.Identity, bias=shcol(k), scale=1.0, reads=[tk, skey], writes=hw)

        def resid_update(t, fc, pb, pk, gcol, gkey):
            ts = slice(t * 512, (t + 1) * 512)
            S.op("dve", "scalar_tensor_tensor", xT[:, fc, ts], pb[:], gcol, xT[:, fc, ts], ALU.mult, ALU.add,
                 reads=[pk, ("xT", fc, t), gkey], writes=[("xT", fc, t)])

        dbgb = {}

        def dump(nm, ap, key, rows=128):
            if not dbg:
                return
            if nm not in dbgb:
                dbgb[nm] = sb("dbgb_" + nm, [128, 512])
            f, fk = dbgb[nm], ("dbgb", nm)
            S.op("dve", "tensor_copy", f[0:rows, :], ap, reads=[key], writes=[fk])
            S.dmas("sp", [(DBG[nm][0:rows, :], f[0:rows, :])], reads=[fk], sem_key=("dbg", nm), final=True)

        def load_w(pairs, keys):
            S.dmas("pool", pairs, writes=keys, sem_key=keys[0])

        def hgrn_stage(l, t, h, wh, wkeys, state_only, par):
            ts = slice(t * 512, (t + 1) * 512)
            A = lbAB[:, l, 0, h:h + 1]
            B = lbAB[:, l, 1, h:h + 1]
            hk = [("hT", 0)]
            X = {}
            kts = KTS[par]
            kdtok_ = kts[:, 0:1024].rearrange("p (c d) -> p c d", c=8)
            vtok_ = kts[:, 1024:2048].rearrange("p (c d) -> p c d", c=8)
            ktk = ("kts", par)
            tf, tfk = P1F[2 * par], ("p1f", 2 * par)
            tq, tqk = P1F[2 * par + 1], ("p1f", 2 * par + 1)
            iTb, ibk = p1b[:, 2 * par, :], ("p1b", 2 * par)
            tg, tgk = p1b[:, 2 * par + 1, :], ("p1b", 2 * par + 1)

            def proj(j):
                pb, pk = PS()
                for k in range(8):
                    S.op("pe", "matmul", pb[:], wh[:, k, j, :], hTall[:, k, 0:512], start=(k == 0), stop=(k == 7), reads=wkeys + hk, writes=[pk])
                return pb, pk

            def P1a():
                X["pf"] = proj(1)
                if state_only:
                    X["pi"] = proj(2)
                else:
                    S.dmas("sp", [(kts, kvs_scr[t * 8 + h])], reads=[("kvs", t, h)], writes=[ktk], sem_key=("kvsl", par))
                    X["pq"] = proj(0)
                    X["pg"] = proj(3)

            def P1b():
                pf, pfk = X["pf"]
                S.op("act", "activation", tf, pf[:], AF.Tanh, scale=0.5, reads=[pfk], writes=[tfk])
                if state_only:
                    pi, pik = X["pi"]
                    S.op("act", "activation", iTb, pi[:], AF.Copy, reads=[pik], writes=[ibk])
                if not state_only:
                    pq, pqk = X["pq"]
                    pg, pgk = X["pg"]
                    S.op("act", "activation", tq, pq[:], AF.Silu, reads=[pqk], writes=[tqk])
                    S.op("act", "activation", tg, pg[:], AF.Silu, reads=[pgk], writes=[tgk])

            def P2a():
                S.op("dve", "tensor_scalar", tf, tf, A, B, ALU.mult, ALU.add, reads=[tfk, "lbAB"], writes=[tfk])
                tl, tlk = FT()
                S.op("act", "activation", tl[:], tf, AF.Ln, bias=epsl[:, 0:1], scale=1.0, reads=[tfk, "epsl"], writes=[tlk])
                tk_, tkk = FT()
                S.op("act", "activation", tk_[:], tf, AF.Identity, bias=1.0, scale=-1.0, reads=[tfk], writes=[tkk])
                cum, cumk = FT()
                S.op("dve", "tensor_tensor_scan", cum[:], maskreset, tl[:], 0.0, ALU.mult, ALU.add, reads=["cst", tlk], writes=[cumk])
                cv3 = cum[:].rearrange("p (c j) -> p c j", j=64)
                e3, e3k = FT()
                S.op("act", "activation", e3[:], cum[:], AF.Exp, reads=[cumk], writes=[e3k])
                X.update(e3=(e3, e3k))
                if state_only:
                    d2, d2k = FT()
                    S.op("pool", "tensor_tensor", d2[:].rearrange("p (c j) -> p c j", j=64), cv3, cv3[:, :, 63:64].to_broadcast([128, 8, 64]), ALU.subtract,
                         reads=[cumk], writes=[d2k])
                    S.op("act", "activation", d2[:], d2[:], AF.Exp, scale=-1.0, reads=[d2k], writes=[d2k])
                    kd, kdk = BT()
                    S.op("dve", "tensor_tensor", kd[:], tk_[:], d2[:], ALU.mult, reads=[tkk, d2k], writes=[kdk])
                    X.update(kd=(kd, kdk))
                if not state_only:
                    S.op("pool", "tensor_copy", cbs[:, :, 1:2], cv3[:, :, 31:32], reads=[cumk], writes=["cbs"])
                    d1, d1k = FT()
                    S.op("dve", "tensor_tensor", d1[:].rearrange("p (c b j) -> p c b j", b=2, j=32), cum[:].rearrange("p (c b j) -> p c b j", b=2, j=32),
                         cbs[:].unsqueeze(3).to_broadcast([128, 8, 2, 32]), ALU.subtract, reads=[cumk, "cbs"], writes=[d1k])
                    e1, e1k = FT()
                    S.op("act", "activation", e1[:], d1[:], AF.Exp, reads=[d1k], writes=[e1k])
                    S.op("act", "activation", d1[:], d1[:], AF.Exp, scale=-1.0, reads=[d1k], writes=[d1k])
                    qq, qqk = BT()
                    S.op("dve", "tensor_tensor", qq[:], tq, e1[:], ALU.mult, reads=[tqk, e1k], writes=[qqk])
                    kk_, kkk = BT()
                    S.op("dve", "scalar_tensor_tensor", kk_[:], d1[:], 1e30, tk_[:], ALU.min, ALU.mult, reads=[tkk, d1k], writes=[kkk])
                    S.op("pool", "tensor_tensor", tl[:].rearrange("p (c j) -> p c j", j=64), cv3, cv3[:, :, 31:32].to_broadcast([128, 8, 64]), ALU.subtract,
                         reads=[cumk], writes=[tlk])
                    S.op("act", "activation", tl[:], tl[:], AF.Exp, scale=-1.0, reads=[tlk], writes=[tlk])
                    k32, k32k = BT()
                    S.op("dve", "scalar_tensor_tensor", k32[:], tl[:], 1.0, tk_[:], ALU.min, ALU.mult, reads=[tkk, tlk], writes=[k32k])
                    qe, qek = BT()
                    S.op("pool", "tensor_tensor", qe[:], tq, e3[:], ALU.mult, reads=[tqk, e3k], writes=[qek])
                    X.update(qq=(qq, qqk), kk=(kk_, kkk), k32=(k32, k32k), qe=(qe, qek), d1=(d1, d1k))

            def P2b():
                e3, e3k = X["e3"]
                if state_only:
                    kd, kdk = X["kd"]
                    pkd, pkdk = PS()
                    pv, pvk = PS()
                    pkd_b = pkd[:].bitcast(BF16)
                    pv_b = pv[:].bitcast(BF16)
                    for c in range(8):
                        S.op("pe", "transpose", pkd_b[0:64, c * 128:(c + 1) * 128], kd[:, c * 64:(c + 1) * 64], identb[:], reads=[kdk, "identb"], writes=[pkdk])
                    for c in range(8):
                        S.op("pe", "transpose", pv_b[0:64, c * 128:(c + 1) * 128], iTb[:, c * 64:(c + 1) * 64], identb[:], reads=[ibk, "identb"], writes=[pvk])
                    S.op("act", "activation", kts[:, 0:1024], pkd_b[0:64, :], AF.Copy, reads=[pkdk], writes=[ktk])
                    S.op("dve", "tensor_copy", kts[:, 1024:2048], pv_b[0:64, :], reads=[pvk, ktk], writes=[ktk])
                    S.dmas("sp", [(kvs_scr[t * 8 + h], kts)], reads=[ktk], writes=[("kvs", t, h)], sem_key=("kvsw", par))
                if not state_only:
                    qq, qqk = X["qq"]
                    kk_, kkk = X["kk"]
                    k32, k32k = X["k32"]
                    qe, qek = X["qe"]
                    patt, pattk = PS()
                    patt2, patt2k = PS()
                    for c in range(8):
                        cs = slice(c * 64, (c + 1) * 64)
                        S.op("pe", "matmul", patt[0:64, cs], kk_[:, cs], qq[:, cs], start=True, stop=True, reads=[kkk, qqk], writes=[pattk])
                    for c in range(8):
                        cs = slice(c * 64, (c + 1) * 64)
                        S.op("pe", "matmul", patt2[0:64, cs], k32[:, cs], qq[:, cs], start=True, stop=True, reads=[k32k, qqk], writes=[patt2k])
                    S.op("dve", "tensor_tensor", attsb, patt[0:64, :], attmask, ALU.mult, reads=[pattk, "cst"], writes=["attsb"])
                    au = attsb.rearrange("p (c b j) -> p c b j", b=2, j=32)[:, :, 1, :]
                    pu = patt2[0:64, :].rearrange("p (c b j) -> p c b j", b=2, j=32)[:, :, 1, :]
                    S.op("dve", "scalar_tensor_tensor", au, pu, rowmask, au, ALU.mult, ALU.add, reads=[patt2k, "cst", "attsb"], writes=["attsb"])
                    po, pok = PL()
                skey = ("S", h)
                sbkey = ("Sbf", h)
                psSs = []
                for c in range(8):
                    bS = banks[4 + c // 4]
                    psSs.append((bS[:, (c % 4) * 128:(c % 4 + 1) * 128], ("ps", 4 + c // 4)))
                    S.op("pe", "matmul", psSs[c][0], kdtok_[:, c, :], vtok_[:, c, :], start=True, stop=True, reads=[ktk], writes=[psSs[c][1]])
                if not state_only:
                    for c in range(8):
                        cs = slice(c * 64, (c + 1) * 64)
                        S.op("pe", "matmul", po[:, cs], vtok_[:, c, :], attsb[:, cs], start=(c == 0), stop=False, skip_group_check=True,
                             reads=[ktk, "attsb"], writes=[pok])
                for c in range(8):
                    cs = slice(c * 64, (c + 1) * 64)
                    if not state_only:
                        S.op("pe", "matmul", po[:, cs], Sbf[:, h, :], qe[:, cs], start=False, stop=True, skip_group_check=True, reads=[sbkey, qek], writes=[pok])
                    S.op("dve", "scalar_tensor_tensor", Sst[:, h, :], Sst[:, h, :], e3[:, c * 64 + 63:c * 64 + 64], psSs[c][0], ALU.mult, ALU.add,
                         reads=[psSs[c][1], e3k, skey], writes=[skey])
                    if not state_only:
                        S.op("act", "activation", Sbf[:, h, :], Sst[:, h, :], AF.Copy, reads=[skey], writes=[sbkey])
                if state_only:
                    return
                d1, d1k = X["d1"]
                osq, osk = BT()
                S.op("act", "activation", osq[:], po[:], AF.Square, reads=[pok], writes=[osk])
                pn, pnk = PS()
                S.op("pe", "matmul", pn[:], ones128b[:], osq[:], start=True, stop=True, reads=[osk, "ones128b"], writes=[pnk])
                rs, rk = d1, d1k
                S.op("act", "activation", rs[:], pn[:], AF.Ln, bias=epsc[:, 0:1], scale=1.0, reads=[pnk, "epsc"], writes=[rk])
                S.op("act", "activation", rs[:], rs[:], AF.Exp, scale=-0.5, reads=[rk], writes=[rk])
                S.op("dve", "tensor_tensor", rs[:], po[:], rs[:], ALU.mult, reads=[pok, rk], writes=[rk])
                S.op("dve", "scalar_tensor_tensor", mixT[:, h, :], rs[:], vcols[:, V_GN + l:V_GN + l + 1], tg, ALU.mult, ALU.mult,
                     reads=[rk, tgk, "vcols"], writes=[("mix", h)] + MIXW)
            return P1a, P1b, P2a, P2b

        hctr = [0]

        def hgrn_pass(l, state_only, wout=None, wokeys=None):
            psn[0] = 4
            bctr[0] = 0
            _hgrn_pass(l, state_only, wout, wokeys)
            psn[0] = 6

        def _hgrn_pass(l, state_only, wout=None, wokeys=None):
            loaded = {}

            def issue(t, h):
                if t >= NT or (t, h) in loaded:
                    return
                wsl, wkeys = slot(2 + hctr[0] % 2)
                hctr[0] += 1
                wh = wsl.rearrange("p (k j c) -> p k j c", k=8, j=4)
                load_w([(wsl.rearrange("p (k c) -> p k c", k=8), hg_w_in[l, h].rearrange("(k p) c -> p k c", p=128))], wkeys)
                loaded[(t, h)] = (wh, wkeys)
            issue(0, 0)
            for t in range(NT):
                norm_mod(t, lambda k: aT[:, 0, k, 0:1], lambda k: modT[:, k, 0:1], "aT0", "modT", hslot=0)
                stages = {}
                wh, wkeys = loaded[(t, 0)]
                stages[0] = hgrn_stage(l, t, 0, wh, wkeys, state_only, 0)
                issue(t, 1)
                stages[0][0]()
                stages[0][1]()
                for h in range(8):
                    if h + 1 < 8:
                        wh, wkeys = loaded[(t, h + 1)]
                        stages[h + 1] = hgrn_stage(l, t, h + 1, wh, wkeys, state_only, (h + 1) % 2)
                        stages[h + 1][0]()
                    stages[h][2]()
                    if h + 1 < 8:
                        stages[h + 1][1]()
                        if h + 2 < 8:
                            issue(t, h + 2)
                        else:
                            issue(t + 1, 0)
                    stages[h][3]()
                if not state_only:
                    if dbg and l == 0 and t == 0:
                        pass
                    for fc in range(8):
                        pb, pk = PS()
                        for k in range(8):
                            S.op("pe", "matmul", pb[:], wout[:, k, fc * 128:(fc + 1) * 128], mixT[:, k, :], start=(k == 0), stop=(k == 7),
                                 reads=wokeys + [("mix", k)], writes=[pk])
                        resid_update(t, fc, pb, pk, modT[:, 16 + fc, 0:1], "modT")

        def exchange_state(l):
            S.dmas("sp", [(xs_scr[l][:, :], Sst.rearrange("p h v -> p (h v)"))], reads=SK, writes=[("xs", l)], sem_key=("xs", l))
            S.dma("pool", lambda e, l=l: [e.collective_compute("AllGather", ALU.bypass, replica_groups=PAIRS, ins=[xs_scr[l][:, :]], outs=[xg_scr[l][:, :]])],
                  reads=[("xs", l)], writes=[("xg", l)], sem_key=("cc", l), inc=1)
            S.dmas("sp", [(Sst.rearrange("p h v -> p (h v)"), xg_scr[l][0:128, :])], reads=[("xg", l)], writes=SK, sem_key=("xgl", l))
            for h in range(8):
                S.op("dve", "tensor_scalar", Sst[:, h, :], Sst[:, h, :], flag, None, ALU.mult, reads=[("S", h), "cst"], writes=[("S", h)])
                S.op("act", "activation", Sbf[:, h, :], Sst[:, h, :], AF.Copy, reads=[("S", h)], writes=[("Sbf", h)])

        def hgrn_layer(l):
            S.op("dve", "memset", Sst, 0.0, writes=SK)
            hgrn_pass(l, True)
            exchange_state(l)
            wsl, wokeys = slot(0, 2)
            wout = wsl.rearrange("p (k c) -> p k c", k=8)
            load_w([(wout, hg_w_out[l].rearrange("(k p) c -> p k c", p=128))], wokeys)
            hgrn_pass(l, False, wout, wokeys)
            S.dmas("sp", [(sp_state[l].rearrange("h k v -> k h v"), Sst)], reads=SK, sem_key=("spst", l), final=True)

        mctr = [0]

        def mlp(l):
            for t in range(NT):
                norm_mod(t, lambda k: aT[:, 1, k, 0:1], lambda k: modT[:, 24 + k, 0:1], "aT1", "modT")
            mloaded = {}

            def missue(g):
                if g >= 8 or g in mloaded:
                    return
                wsl, wk = slot(2 * (mctr[0] % 2), 2)
                mctr[0] += 1
                wup = wsl[:, 0:4096].rearrange("p (k c) -> p k c", k=8)
                wdn = wsl[:, 4096:8192].rearrange("p (k c) -> p k c", k=4)
                load_w([(wup, w_up[l].rearrange("(k p) c -> p k c", p=128)[:, :, g * 512:(g + 1) * 512]),
                        (wdn, w_down[l][g * 512:(g + 1) * 512, :].rearrange("(k p) c -> p k c", p=128))], wk)
                mloaded[g] = (wup, wdn, wk)
            missue(0)
            for hg in range(8):
                missue(hg + 1)
                wup, wdn, wk = mloaded[hg]
                for t in range(NT):
                    ts = slice(t * 512, (t + 1) * 512)
                    hids = []
                    for hc in range(4):
                        pb, pk = PS()
                        for k in range(8):
                            S.op("pe", "matmul", pb[:], wup[:, k, hc * 128:(hc + 1) * 128], hTall[:, k, ts], start=(k == 0), stop=(k == 7),
                                 reads=wk + [("hT", t)], writes=[pk])
                        r, rk = FT()
                        S.op("act", "activation", r[:], pb[:], AF.Relu, reads=[pk], writes=[rk])
                        hd, hdk = BT()
                        S.op("pool", "tensor_tensor", hd[:], r[:], r[:], ALU.mult, reads=[rk], writes=[hdk])
                        hids.append((hd, hdk))
                    for fc in range(8):
                        pb, pk = PS()
                        for hc in range(4):
                            hd, hdk = hids[hc]
                            S.op("pe", "matmul", pb[:], wdn[:, hc, fc * 128:(fc + 1) * 128], hd[:], start=(hc == 0), stop=(hc == 3),
                                 reads=wk + [hdk], writes=[pk])
                        resid_update(t, fc, pb, pk, modT[:, 40 + fc, 0:1], "modT")

        def load_tabs(t):
            S.dmas("sp", [(tabc[:], costab[:, t * 512:(t + 1) * 512]), (tabs[:], sintab[:, t * 512:(t + 1) * 512])], writes=["tab"], sem_key="tab")

        def qk_norm_rope(pb, pk, gcol, out_ap, out_keys, f32_out=None, f32_key=None):
            sq, sk = BT()
            S.op("act", "activation", sq[:], pb[:], AF.Square, reads=[pk], writes=[sk])
            pn, pnk = PS()
            S.op("pe", "matmul", pn[:], onesblkb[:], sq[:], start=True, stop=True, reads=[sk, "onesblkb"], writes=[pnk])
            rs, rk = FT()
            S.op("act", "activation", rs[:], pn[:], AF.Ln, bias=epsc[:, 0:1], scale=1.0 / 64, reads=[pnk, "epsc"], writes=[rk])
            S.op("act", "activation", rs[:], rs[:], AF.Exp, scale=-0.5, reads=[rk], writes=[rk])
            qg, qgk = BT()
            S.op("dve", "tensor_scalar", qg[:], pb[:], gcol, None, ALU.mult, reads=[pk, "vcols"], writes=[qgk])
            pp, ppk = PS()
            S.op("pe", "matmul", pp[:], permb[:], qg[:], start=True, stop=True, reads=[qgk, "permb"], writes=[ppk])
            ta, tak = FT()
            S.op("pool", "tensor_tensor", ta[:], qg[:], tabc[:], ALU.mult, reads=[qgk, "tab"], writes=[tak])
            tb, tbk = FT()
            S.op("dve", "tensor_tensor", tb[:], pp[:], tabs[:], ALU.mult, reads=[ppk, "tab"], writes=[tbk])
            S.op("pool", "tensor_tensor", ta[:], ta[:], tb[:], ALU.add, reads=[tak, tbk], writes=[tak])
            if f32_out is not None:
                S.op("dve", "tensor_tensor", f32_out, ta[:], rs[:], ALU.mult, reads=[tak, rk], writes=[f32_key])
                S.op("act", "activation", out_ap, f32_out, AF.Copy, reads=[f32_key], writes=out_keys)
            else:
                S.op("dve", "tensor_tensor", out_ap, ta[:], rs[:], ALU.mult, reads=[tak, rk], writes=out_keys)

        def kv_phase():
            if do_sample:
                S.op("dve", "tensor_copy", kvmodT[:, :, 0], kvmodP[:], reads=["kvmodP"], writes=["kvmodT"])
                S.op("dve", "tensor_copy", aT[:, 2, :, 0], aKVP[:], reads=["aKVP"], writes=["aT2"])
            else:
                ada(kv_w_ada, kv_b_ada, 4, kvmodT, "kvmodT")
                for k in range(8):
                    S.op("dve", "tensor_scalar", aT[:, 2, k, :], kvmodT[:, 8 + k, :], 1.0, vcols[:, V_KVN + k:V_KVN + k + 1], ALU.add, ALU.mult,
                         reads=["kvmodT", "vcols"], writes=["aT2"])
            wsl, wkk = slot(0)
            wkv = wsl.rearrange("p (k c) -> p k c", k=8)
            load_w([(wkv, w_kv.rearrange("(k p) c -> p k c", p=128))], wkk)
            gk = vcols[:, V_KN:V_KN + 1]
            ALIAS = SK + [("Sbf", h) for h in range(8)] + ["kdtok", "vtok", "attsb", ("kts", 0)] + [("p1f", i_) for i_ in range(3)]
            for t in range(NT):
                ts = slice(t * 512, (t + 1) * 512)
                norm_mod(t, lambda k: aT[:, 2, k, 0:1], lambda k: kvmodT[:, k, 0:1], "aT2", "kvmodT", hslot=0)
                load_tabs(t)
                for m in range(2):
                    pb, pk = PS()
                    for k in range(8):
                        S.op("pe", "matmul", pb[:], wkv[:, k, m * 128:(m + 1) * 128], hTall[:, k, 0:512], start=(k == 0), stop=(k == 7),
                             reads=wkk + [("hT", 0)], writes=[pk])
                    dst = KT[:, m, 128 + t * 512:128 + (t + 1) * 512]
                    if t == NT - 1:
                        kf, kfk = FT()
                        qk_norm_rope(pb, pk, gk, dst, [("KT", m, t)] + ALIAS, f32_out=kf[:], f32_key=kfk)
                        S.op("pool", "tensor_copy", klast[:, m, :], kf[:, 384:512], reads=[kfk], writes=[("klast", m)])
                    else:
                        qk_norm_rope(pb, pk, gk, dst, [("KT", m, t)] + ALIAS)
                for b in range(4):
                    blk = t * 4 + b
                    pb, pk = PS()
                    for k in range(8):
                        S.op("pe", "matmul", pb[:, 0:256], hTall[:, k, b * 128:(b + 1) * 128], wkv[:, k, 256:512],
                             start=(k == 0), stop=(k == 7), reads=wkk + [("hT", 0)], writes=[pk])
                    S.op("act", "activation", Vtok[:, blk + 1, :], pb[:, 0:256], AF.Copy, reads=[pk], writes=[("V", blk + 1)] + ALIAS)
                    if blk == NBLK - 1:
                        vf, vfk = FT()
                        S.op("dve", "tensor_copy", vf[:, 0:256], pb[:, 0:256], reads=[pk], writes=[vfk])
                        S.dmas("sp", [(vp[:, :], vf[:, 0:256])], reads=[vfk], sem_key="vp", final=True)
            ko, kok = FT()
            pb, pk = PS()
            for m in range(2):
                S.op("pe", "transpose", pb[:, m * 128:(m + 1) * 128], klast[:, m, :], ident, reads=[("klast", m), "cst"], writes=[pk])
            S.op("dve", "tensor_copy", ko[:, 0:256], pb[:, 0:256], reads=[pk], writes=[kok])
            S.dmas("sp", [(kp[:, :], ko[:, 0:256])], reads=[kok], sem_key="kp", final=True)
            tlast = NT - 1
            S.dmas("sp", [(kv_scr[:, 0:256].rearrange("p (m t) -> p m t", m=2), KT[:, :, T:T + 128]), (kv_scr[:, 256:512], Vtok[:, NBLK, :])],
                   reads=[("KT", 0, tlast), ("KT", 1, tlast), ("V", NBLK)], writes=["kvscr"], sem_key="kvscr")
            S.dma("pool", lambda e: [e.collective_compute("AllGather", ALU.bypass, replica_groups=PAIRS, ins=[kv_scr[:, :]], outs=[kvg_scr[:, :]])],
                  reads=["kvscr"], writes=["kvg"], sem_key="cckv", inc=1)
            S.dmas("sp", [(KT[:, :, 0:128], kvg_scr[0:128, 0:256].rearrange("p (m t) -> p m t", m=2)), (Vtok[:, 0, :], kvg_scr[0:128, 256:512])],
                   reads=["kvg"], writes=[("KTpre",), ("V", 0)], sem_key="kvgl")

        def attn_layer(l):
            j = l - 2
            wsl, wqk = slot(0, 2)
            wq = wsl.rearrange("p (k c) -> p k c", k=8)
            load_w([(wq, w_q[j].rearrange("(k p) c -> p k c", p=128))], wqk)
            wsl2, wok = slot(2, 2)
            wo = wsl2.rearrange("p (c f) -> p c f", c=8)
            load_w([(wo, w_o[j].rearrange("(c p) f -> p c f", p=128))], wok)
            gq = vcols[:, V_QN + j:V_QN + j + 1]
            for t in range(NT):
                ts = slice(t * 512, (t + 1) * 512)
                norm_mod(t, lambda k: aT[:, 0, k, 0:1], lambda k: modT[:, k, 0:1], "aT0", "modT", hslot=0)
                load_tabs(t)
                for c in range(8):
                    pb, pk = PS()
                    for k in range(8):
                        S.op("pe", "matmul", pb[:], wq[:, k, c * 128:(c + 1) * 128], hTall[:, k, 0:512], start=(k == 0), stop=(k == 7),
                             reads=wqk + [("hT", 0)], writes=[pk])
                    qT, qTk = BT()
                    qk_norm_rope(pb, pk, gq, qT[:], [qTk])
                    m = c // 4
                    st_ = {}

                    def S1(qb):
                        n = t * 4 + qb
                        qs = slice(qb * 128, (qb + 1) * 128)
                        kreads = [("KTpre",)] if n == 0 else [("KT", m, (n - 1) // 4)]
                        kreads.append(("KT", m, n // 4))
                        pbanks = [PS(), PS()]
                        for a in range(2):
                            rows = slice(a * 64, (a + 1) * 64)
                            pss, pssk = pbanks[a]
                            for kb in range(2):
                                ko_ = (n + kb) * 128
                                S.op("pe", "matmul", pss[:, kb * 128:(kb + 1) * 128], KT[rows, m, ko_:ko_ + 128], qT[rows, qs],
                                     start=True, stop=True, reads=kreads + [qTk], writes=[pssk])
                        st_[qb] = {"pbanks": pbanks}

                    def S2(qb):
                        n = t * 4 + qb
                        pbanks = st_[qb]["pbanks"]
                        pe_, pek = FT()
                        pev = pe_[:].rearrange("p (kb a q) -> p kb a q", kb=2, a=2)
                        for a in range(2):
                            pss, pssk = pbanks[a]
                            S.op("act", "activation", pev[:, :, a, :], pss[:, 0:256].rearrange("p (kb q) -> p kb q", kb=2), AF.Exp, scale=SCALE,
                                 reads=[pssk], writes=[pek])
                        pT, pTk = BT()
                        mk = mask4f[:] if n == 0 else mask4
                        S.op("pool", "tensor_tensor", pT[:], pe_[:], mk, ALU.mult, reads=[pek, "cst", "mask4f"], writes=[pTk])
                        st_[qb]["pT"] = (pT, pTk)

                    def S3(qb):
                        n = t * 4 + qb
                        pT, pTk = st_[qb]["pT"]
                        po, pok = PS()
                        S.op("pe", "matmul", po[:, 0:256], Vtok[:, n, m * 128:(m + 1) * 128], pT[:, 0:256], start=True, stop=False, reads=[("V", n), pTk], writes=[pok])
                        S.op("pe", "matmul", po[:, 0:256], Vtok[:, n + 1, m * 128:(m + 1) * 128], pT[:, 256:512], start=False, stop=True, reads=[("V", n + 1), pTk], writes=[pok])
                        S.op("pe", "matmul", po[:, 256:512], onesb[:], pT[:, 0:256], start=True, stop=False, reads=["onesb", pTk], writes=[pok])
                        S.op("pe", "matmul", po[:, 256:512], onesb[:], pT[:, 256:512], start=False, stop=True, reads=["onesb", pTk], writes=[pok])
                        st_[qb]["po"] = (po, pok)

                    def S4(qb):
                        qs = slice(qb * 128, (qb + 1) * 128)
                        po, pok = st_[qb]["po"]
                        for a in range(2):
                            rows = slice(a * 64, (a + 1) * 64)
                            rk_ = ("rd", a)
                            S.op("act", "activation", rd[rows, :], po[rows, 256 + a * 128:256 + (a + 1) * 128], AF.Ln, bias=esink[rows, j, c:c + 1], scale=1.0,
                                 reads=[pok, "esink"], writes=[rk_])
                            S.op("act", "activation", rd[rows, :], rd[rows, :], AF.Exp, scale=-1.0, reads=[rk_], writes=[rk_])
                            S.op("dve", "tensor_tensor", mixT[rows, c, qs], po[rows, a * 128:(a + 1) * 128], rd[rows, :], ALU.mult, reads=[pok, rk_], writes=[("mix", c)] + MIXW)
                    S1(0)
                    S2(0)
                    for qb in range(4):
                        if qb + 1 < 4:
                            S1(qb + 1)
                        S3(qb)
                        if qb + 1 < 4:
                            S2(qb + 1)
                        S4(qb)
                for fc in range(8):
                    pb, pk = PS()
                    for c in range(8):
                        S.op("pe", "matmul", pb[:], wo[:, c, fc * 128:(fc + 1) * 128], mixT[:, c, :], start=(c == 0), stop=(c == 7),
                             reads=wok + [("mix", c)], writes=[pk])
                    resid_update(t, fc, pb, pk, modT[:, 16 + fc, 0:1], "modT")


        SMPW = 9216
        if T >= 2048:
            smp = xT[:].rearrange("p k t -> p (k t)")[:, 0:SMPW]
        else:
            smp = sb("smp", [128, SMPW])[:]
        soff = [0]

        def salloc(words, shape=None, dt=F32):
            a = smp[:, soff[0]:soff[0] + words]
            soff[0] += words
            assert soff[0] <= SMPW
            if dt == BF16:
                a = a.bitcast(BF16)
            return a

        def sop(eng, name, *args, reads=(), writes=(), **kw):
            return S.op(eng, name, *args, reads=list(reads) + ["SMPREGION"], writes=writes, **kw)

        def sdmas(queue, pairs, reads=(), writes=(), **kw):
            return S.dmas(queue, pairs, reads=list(reads) + ["SMPREGION"], writes=writes, **kw)

        xsT = salloc(128).rearrange("p (k s) -> p k s", k=8)
        hsT = salloc(64, dt=BF16).rearrange("p (k s) -> p k s", k=8)
        mixsT = salloc(64, dt=BF16).rearrange("p (k s) -> p k s", k=8)
        qsT = salloc(128).rearrange("p (k s) -> p k s", k=8)
        qsTb = salloc(64, dt=BF16).rearrange("p (k s) -> p k s", k=8)
        qpad = [salloc(64, dt=BF16).rearrange("p (k s) -> p k s", k=8) for _ in range(2)]
        ksT = salloc(32).rearrange("p (m s) -> p m s", m=2)
        vsT = salloc(32).rearrange("p (m s) -> p m s", m=2)
        NSF = 16
        sfs = [salloc(128) for _ in range(NSF)]
        sfc = [0]

        def SF():
            i = sfc[0]
            sfc[0] = (i + 1) % NSF
            return sfs[i], ("sf", i)
        NSB = 8
        sbs = [salloc(64, dt=BF16) for _ in range(NSB)]
        sbc = [0]

        def SB():
            i = sbc[0]
            sbc[0] = (i + 1) % NSB
            return sbs[i], ("sb", i)
        sin_b = [salloc(512).rearrange("p (s v) -> p s v", s=4) for _ in range(2)]
        sout_b = [salloc(512).rearrange("p (s v) -> p s v", s=4) for _ in range(2)]
        km = salloc(1024, dt=BF16)[0:16, :].rearrange("p (s d) -> p s d", s=16)
        ktv = salloc(128, dt=BF16)[0:16, :].rearrange("p (j d) -> p j d", j=2)
        ckT = salloc(512, dt=BF16).rearrange("p (m s j) -> p m s j", m=2, s=4)
        cvt = salloc(512, dt=BF16).rearrange("p (s d) -> p s d", s=4)
        ckin = [salloc(256) for _ in range(2)]
        kvnew = salloc(512)[0:16, :]
        pTs = [salloc(64, dt=BF16) for _ in range(2)]
        tab16 = salloc(32)

        def s_norm_mod(ai, shT, akey, skey):
            sq, sk = SF()
            sop("act", "activation", sq, xsT.rearrange("p k s -> p (k s)"), AF.Square, reads=["xsT"], writes=[sk])
            pb, pk = PS()
            for k in range(8):
                sop("pe", "matmul", pb[:, 0:16], onesD[:], sq[:, k * 16:(k + 1) * 16], start=(k == 0), stop=(k == 7), reads=[sk, "onesD"], writes=[pk])
            rs, rk = SF()
            sop("act", "activation", rs[:, 0:16], pb[:, 0:16], AF.Ln, bias=epsc[:, 0:1], scale=1.0, reads=[pk, "epsc"], writes=[rk])
            sop("act", "activation", rs[:, 0:16], rs[:, 0:16], AF.Exp, scale=-0.5, reads=[rk], writes=[rk])
            t1, t1k = SF()
            t13 = t1.rearrange("p (k s) -> p k s", k=8)
            sop("dve", "tensor_tensor", t13, xsT, rs[:, 0:16].unsqueeze(1).to_broadcast([128, 8, 16]), ALU.mult, reads=["xsT", rk], writes=[t1k])
            sop("dve", "tensor_tensor", t13, t13, aT[:, ai, :, 1:17], ALU.mult, reads=[t1k, akey], writes=[t1k])
            sop("dve", "tensor_tensor", hsT, t13, shT, ALU.add, reads=[t1k, skey], writes=["hsT"])

        def s_resid(pb, pk, gT, gkey):
            t1, t1k = SF()
            t13 = t1.rearrange("p (k s) -> p k s", k=8)
            sop("dve", "tensor_tensor", t13, pb[:, 0:128].rearrange("p (k s) -> p k s", k=8), gT, ALU.mult, reads=[pk, gkey], writes=[t1k])
            sop("dve", "tensor_tensor", xsT, xsT, t13, ALU.add, reads=["xsT", t1k], writes=["xsT"])

        def s_proj_out(w3, wkeys, srcT, srckey, gT, gkey):
            pb, pk = PS()
            for fc in range(8):
                for k in range(8):
                    sop("pe", "matmul", pb[:, fc * 16:(fc + 1) * 16], w3[:, k, fc * 128:(fc + 1) * 128], srcT[:, k, :], start=(k == 0), stop=(k == 7),
                        reads=wkeys + [srckey], writes=[pk])
            s_resid(pb, pk, gT, gkey)

        def s_rope(pb, pk, gcol, out_f32, okey, n=16):
            sq, sk = SF()
            sop("act", "activation", sq[:, 0:n], pb, AF.Square, reads=[pk], writes=[sk])
            pn, pnk = PS()
            sop("pe", "matmul", pn[:, 0:n], onesblk, sq[:, 0:n], start=True, stop=True, reads=[sk, "cst"], writes=[pnk])
            rs, rk = SF()
            sop("act", "activation", rs[:, 0:n], pn[:, 0:n], AF.Ln, bias=epsc[:, 0:1], scale=1.0 / 64, reads=[pnk, "epsc"], writes=[rk])
            sop("act", "activation", rs[:, 0:n], rs[:, 0:n], AF.Exp, scale=-0.5, reads=[rk], writes=[rk])
            qg, qgk = SF()
            sop("dve", "tensor_scalar", qg[:, 0:n], pb, gcol, None, ALU.mult, reads=[pk, "vcols"], writes=[qgk])
            pp, ppk = PS()
            sop("pe", "matmul", pp[:, 0:n], perm, qg[:, 0:n], start=True, stop=True, reads=[qgk, "cst"], writes=[ppk])
            ta, tak = SF()
            sop("dve", "tensor_tensor", ta[:, 0:n], qg[:, 0:n], tab16[:, 0:16], ALU.mult, reads=[qgk, "tab16"], writes=[tak])
            tb, tbk = SF()
            sop("dve", "tensor_tensor", tb[:, 0:n], pp[:, 0:n], tab16[:, 16:32], ALU.mult, reads=[ppk, "tab16"], writes=[tbk])
            sop("dve", "tensor_tensor", ta[:, 0:n], ta[:, 0:n], tb[:, 0:n], ALU.add, reads=[tak, tbk], writes=[tak])
            sop("dve", "tensor_tensor", out_f32, ta[:, 0:n], rs[:, 0:n], ALU.mult, reads=[tak, rk], writes=[okey])

        def s_hgrn_head(l, h, wh, wkeys):
            A = lbAB[:, l, 0, h:h + 1]
            B = lbAB[:, l, 1, h:h + 1]
            pb, pk = PS()
            for j in range(4):
                for k in range(8):
                    sop("pe", "matmul", pb[:, j * 16:(j + 1) * 16], wh[:, k, j, :], hsT[:, k, :], start=(k == 0), stop=(k == 7), reads=wkeys + ["hsT"], writes=[pk])
            qs_, qsk = SF()
            sop("act", "activation", qs_[:, 0:16], pb[:, 0:16], AF.Silu, reads=[pk], writes=[qsk])
            gs_, gsk = SF()
            sop("act", "activation", gs_[:, 0:16], pb[:, 48:64], AF.Silu, reads=[pk], writes=[gsk])
            fg, fgk = SF()
            sop("act", "activation", fg[:, 0:16], pb[:, 16:32], AF.Tanh, scale=0.5, reads=[pk], writes=[fgk])
            kv2, kv2k = SF()
            sop("act", "activation", kv2[:, 16:32], pb[:, 32:48], AF.Copy, reads=[pk], writes=[kv2k])
            sop("dve", "tensor_scalar", fg[:, 0:16], fg[:, 0:16], A, B, ALU.mult, ALU.add, reads=[fgk, "lbAB"], writes=[fgk])
            sop("dve", "tensor_scalar", kv2[:, 0:16], fg[:, 0:16], -1.0, 1.0, ALU.mult, ALU.add, reads=[fgk, kv2k], writes=[kv2k])
            pt, ptk = PS()
            sop("pe", "transpose", pt[0:16, 0:128], kv2[:, 0:16], ident, reads=[kv2k, "cst"], writes=[ptk])
            sop("pe", "transpose", pt[0:16, 128:256], kv2[:, 16:32], ident, reads=[kv2k, "cst"], writes=[ptk])
            sop("dve", "tensor_copy", ktv.rearrange("p j d -> p (j d)"), pt[0:16, 0:256], reads=[ptk], writes=["ktv"])
            sop("dve", "tensor_tensor", km, ktv[:, 0, :].unsqueeze(1).to_broadcast([16, 16, 128]),
                identb[0:16, 0:16].unsqueeze(2).to_broadcast([16, 16, 128]), ALU.mult, reads=["ktv", "identb"], writes=["km"])
            po, pok = PL()
            for bi in range(4):
                si_, so_ = sin_b[bi % 2], sout_b[bi % 2]
                sik, sok = ("sin", bi % 2), ("sout", bi % 2)
                sdmas("sp", [(si_, st_in[l, bi * 4:(bi + 1) * 4, h].rearrange("s k v -> k s v"))], writes=[sik], sem_key=sik)
                for s4 in range(4):
                    s_ = bi * 4 + s4
                    pkv, pkvk = PS()
                    sop("pe", "matmul", pkv[:, 0:128], km[:, s_, :], ktv[:, 1, :], start=True, stop=True, reads=["km", "ktv"], writes=[pkvk])
                    sop("dve", "scalar_tensor_tensor", so_[:, s4, :], si_[:, s4, :], fg[:, s_:s_ + 1], pkv[:, 0:128], ALU.mult, ALU.add,
                        reads=[sik, fgk, pkvk], writes=[sok])
                    sop("pe", "matmul", po[:, s_:s_ + 1], so_[:, s4, :], qs_[:, s_:s_ + 1], start=True, stop=True, reads=[sok, qsk], writes=[pok])
                sdmas("sp", [(ss[l, bi * 4:(bi + 1) * 4, h].rearrange("s k v -> k s v"), so_)], reads=[sok], sem_key=sok, final=True)
            osq, osk = SF()
            sop("act", "activation", osq[:, 0:16], po[:, 0:16], AF.Square, reads=[pok], writes=[osk])
            pn, pnk = PS()
            sop("pe", "matmul", pn[:, 0:16], ones128[:], osq[:, 0:16], start=True, stop=True, reads=[osk, "ones128"], writes=[pnk])
            rs, rk = SF()
            sop("act", "activation", rs[:, 0:16], pn[:, 0:16], AF.Ln, bias=epsc[:, 0:1], scale=1.0, reads=[pnk, "epsc"], writes=[rk])
            sop("act", "activation", rs[:, 0:16], rs[:, 0:16], AF.Exp, scale=-0.5, reads=[rk], writes=[rk])
            sop("dve", "tensor_tensor", rs[:, 0:16], po[:, 0:16], rs[:, 0:16], ALU.mult, reads=[pok, rk], writes=[rk])
            sop("dve", "scalar_tensor_tensor", mixsT[:, h, :], rs[:, 0:16], vcols[:, V_GN + l:V_GN + l + 1], gs_[:, 0:16], ALU.mult, ALU.mult,
                reads=[rk, gsk, "vcols"], writes=["mixsT"])

        def s_hgrn_layer(l):
            s_norm_mod(0, modT[:, 0:8, 1:17], "aT0", "modT")
            hl = {}

            def hissue(h):
                if h >= 8 or h in hl:
                    return
                wsl, wkeys = slot(2 + hctr[0] % 2)
                hctr[0] += 1
                wh = wsl.rearrange("p (k j c) -> p k j c", k=8, j=4)
                load_w([(wsl.rearrange("p (k c) -> p k c", k=8), hg_w_in[l, h].rearrange("(k p) c -> p k c", p=128))], wkeys)
                hl[h] = (wh, wkeys)
            hissue(0)
            for h in range(8):
                hissue(h + 1)
                wh, wkeys = hl[h]
                s_hgrn_head(l, h, wh, wkeys)
            wsl, wokeys = slot(0, 2)
            wout = wsl.rearrange("p (k c) -> p k c", k=8)
            load_w([(wout, hg_w_out[l].rearrange("(k p) c -> p k c", p=128))], wokeys)
            s_proj_out(wout, wokeys, mixsT, "mixsT", modT[:, 16:24, 1:17], "modT")

        def s_mlp(l):
            s_norm_mod(1, modT[:, 24:32, 1:17], "aT1", "modT")
            mloaded = {}

            def missue(g):
                if g >= 8 or g in mloaded:
                    return
                wsl, wk = slot(2 * (mctr[0] % 2), 2)
                mctr[0] += 1
                wup = wsl[:, 0:4096].rearrange("p (k c) -> p k c", k=8)
                wdn = wsl[:, 4096:8192].rearrange("p (k c) -> p k c", k=4)
                load_w([(wup, w_up[l].rearrange("(k p) c -> p k c", p=128)[:, :, g * 512:(g + 1) * 512]),
                        (wdn, w_down[l][g * 512:(g + 1) * 512, :].rearrange("(k p) c -> p k c", p=128))], wk)
                mloaded[g] = (wup, wdn, wk)
            missue(0)
            for hg in range(8):
                missue(hg + 1)
                wup, wdn, wk = mloaded[hg]
                pb, pk = PS()
                for hc in range(4):
                    for k in range(8):
                        sop("pe", "matmul", pb[:, hc * 16:(hc + 1) * 16], wup[:, k, hc * 128:(hc + 1) * 128], hsT[:, k, :], start=(k == 0), stop=(k == 7),
                            reads=wk + ["hsT"], writes=[pk])
                r, rk = SF()
                sop("act", "activation", r[:, 0:64], pb[:, 0:64], AF.Relu, reads=[pk], writes=[rk])
                hd, hdk = SB()
                sop("dve", "tensor_tensor", hd[:, 0:64], r[:, 0:64], r[:, 0:64], ALU.mult, reads=[rk], writes=[hdk])
                pb2, pk2 = PS()
                for fc in range(8):
                    for hc in range(4):
                        sop("pe", "matmul", pb2[:, fc * 16:(fc + 1) * 16], wdn[:, hc, fc * 128:(fc + 1) * 128], hd[:, hc * 16:(hc + 1) * 16],
                            start=(hc == 0), stop=(hc == 3), reads=wk + [hdk], writes=[pk2])
                s_resid(pb2, pk2, modT[:, 40:48, 1:17], "modT")

        def s_kv_phase():
            ada(kv_w_ada, kv_b_ada, 4, kvmodT, "kvmodT")
            for k in range(8):
                S.op("dve", "tensor_scalar", aT[:, 2, k, :], kvmodT[:, 8 + k, :], 1.0, vcols[:, V_KVN + k:V_KVN + k + 1], ALU.add, ALU.mult,
                     reads=["kvmodT", "vcols"], writes=["aT2"])
            wsl, wkk = slot(0)
            wkv = wsl.rearrange("p (k c) -> p k c", k=8)
            load_w([(wkv, w_kv.rearrange("(k p) c -> p k c", p=128))], wkk)
            S.op("dve", "tensor_copy", kvmodP[:], kvmodT[:, :, 0], reads=["kvmodT"], writes=["kvmodP"])
            S.op("dve", "tensor_copy", aKVP[:], aT[:, 2, :, 0], reads=["aT2"], writes=["aKVP"])
            s_norm_mod(2, kvmodT[:, 0:8, 1:17], "aT2", "kvmodT")
            gk = vcols[:, V_KN:V_KN + 1]
            for m in range(2):
                pb, pk = PS()
                for k in range(8):
                    sop("pe", "matmul", pb[:, 0:16], wkv[:, k, m * 128:(m + 1) * 128], hsT[:, k, :], start=(k == 0), stop=(k == 7), reads=wkk + ["hsT"], writes=[pk])
                s_rope(pb[:, 0:16], pk, gk, ksT[:, m, :], "ksT")
                pb, pk = PS()
                for k in range(8):
                    sop("pe", "matmul", pb[:, 0:16], wkv[:, k, 256 + m * 128:256 + (m + 1) * 128], hsT[:, k, :], start=(k == 0), stop=(k == 7), reads=wkk + ["hsT"], writes=[pk])
                sop("act", "activation", vsT[:, m, :], pb[:, 0:16], AF.Copy, reads=[pk], writes=["vsT"])
            pt, ptk = PS()
            for m in range(2):
                sop("pe", "transpose", pt[0:16, m * 128:(m + 1) * 128], ksT[:, m, :], ident, reads=["ksT", "cst"], writes=[ptk])
                sop("pe", "transpose", pt[0:16, 256 + m * 128:256 + (m + 1) * 128], vsT[:, m, :], ident, reads=["vsT", "cst"], writes=[ptk])
            sop("dve", "tensor_copy", kvnew, pt[0:16, :], reads=[ptk], writes=["kvnew"])
            sdmas("sp", [(ks[:, 127, :], kvnew[:, 0:256]), (vs[:, 127, :], kvnew[:, 256:512])], reads=["kvnew"], sem_key="kvnew", final=True)
            S.dmas("sp", [(ks[:, 0:127, :], ck[:, 1:128, :]), (vs[:, 0:127, :], cv[:, 1:128, :])], sem_key="cachecp", final=True)

        def s_attn_layer(l):
            j = l - 2
            wsl, wqk = slot(0, 2)
            wq = wsl.rearrange("p (k c) -> p k c", k=8)
            load_w([(wq, w_q[j].rearrange("(k p) c -> p k c", p=128))], wqk)
            wsl2, wok = slot(2, 2)
            wo = wsl2.rearrange("p (c f) -> p c f", c=8)
            load_w([(wo, w_o[j].rearrange("(c p) f -> p c f", p=128))], wok)
            gq = vcols[:, V_QN + j:V_QN + j + 1]
            s_norm_mod(0, modT[:, 0:8, 1:17], "aT0", "modT")
            for c in range(8):
                pb, pk = PS()
                for k in range(8):
                    sop("pe", "matmul", pb[:, 0:16], wq[:, k, c * 128:(c + 1) * 128], hsT[:, k, :], start=(k == 0), stop=(k == 7), reads=wqk + ["hsT"], writes=[pk])
                s_rope(pb[:, 0:16], pk, gq, qsT[:, c, :], "qsT")
            sop("act", "activation", qsTb, qsT, AF.Copy, reads=["qsT"], writes=["qsTb"])
            for a in range(2):
                sop("dve", "memset", qpad[a], 0.0, writes=[("qpad", a)])
                rows = slice(a * 64, (a + 1) * 64)
                sop("dve", "tensor_copy", qpad[a][rows], qsT[rows], reads=["qsT"], writes=[("qpad", a)])
            pss, pssk = PL()
            po, pok = PL()
            for bi in range(4):
                sdmas("pool", [(cvt, cv[bi * 4:(bi + 1) * 4].rearrange("s j d -> j s d"))], writes=["cvt"], sem_key="cvt")
                for s4 in range(4):
                    s_ = bi * 4 + s4
                    ci, cik = ckin[s4 % 2], ("ckin", s4 % 2)
                    sdmas("sp", [(ci, ck[s_])], writes=[cik], sem_key=cik)
                    pt, ptk = PS()
                    for m in range(2):
                        sop("pe", "transpose", pt[:, m * 128:(m + 1) * 128], ci[:, m * 128:(m + 1) * 128], ident, reads=[cik, "cst"], writes=[ptk])
                    sop("act", "activation", ckT[:, :, s4, :], pt[:, 0:256].rearrange("p (m j) -> p m j", m=2), AF.Copy, reads=[ptk], writes=["ckT"])
                    for a in range(2):
                        for m in range(2):
                            c0_ = a * 128 + s_ * 8 + 4 * m
                            sop("pe", "matmul", pss[:, c0_:c0_ + 4], ckT[:, m, s4, :], qpad[a][:, 4 * m:4 * m + 4, s_],
                                start=True, stop=True, reads=["ckT", ("qpad", a)], writes=[pssk])
                for a in range(2):
                    cols = slice(bi * 32, (bi + 1) * 32)
                    pe_, pek = SF()
                    sop("act", "activation", pe_[:, 0:32], pss[:, a * 128 + bi * 32:a * 128 + (bi + 1) * 32], AF.Exp, scale=SCALE, reads=[pssk], writes=[pek])
                    sop("dve", "tensor_scalar", pTs[a][:, cols], pe_[:, 0:32], jmask, None, ALU.mult, reads=[pek, "cst"], writes=[("pTs", a)])
                for s4 in range(4):
                    s_ = bi * 4 + s4
                    for m in range(2):
                        for a in range(2):
                            cs_ = slice(s_ * 8 + 4 * m, s_ * 8 + 4 * m + 4)
                            sop("pe", "matmul", po[:, a * 128 + s_ * 8 + 4 * m:a * 128 + s_ * 8 + 4 * m + 4], cvt[:, s4, m * 128:(m + 1) * 128], pTs[a][:, cs_],
                                start=True, stop=True, reads=["cvt", ("pTs", a)], writes=[pok])
            for a in range(2):
                sop("pe", "matmul", po[:, 256 + a * 128:256 + (a + 1) * 128], onesb[:], pTs[a][:, 0:128], start=True, stop=True, reads=["onesb", ("pTs", a)], writes=[pok])
            pr, prk = SF()
            sop("dve", "tensor_tensor", pr.rearrange("p (m i s) -> p m i s", m=2, i=4), qsT.rearrange("p (m i) s -> p m i s", m=2),
                ksT.unsqueeze(2).to_broadcast([128, 2, 4, 16]), ALU.mult, reads=["qsT", "ksT"], writes=[prk])
            psn, psnk = PS()
            sop("pe", "matmul", psn[:, 0:128], onesblk, pr, start=True, stop=True, reads=[prk, "cst"], writes=[psnk])
            pn_, pnk_ = SF()
            sop("act", "activation", pn_, psn[:, 0:128], AF.Exp, scale=SCALE, reads=[psnk], writes=[pnk_])
            on_, onk = SF()
            sop("dve", "tensor_tensor", on_.rearrange("p (m i s) -> p m i s", m=2, i=4), pn_.rearrange("p (m i s) -> p m i s", m=2, i=4),
                vsT.unsqueeze(2).to_broadcast([128, 2, 4, 16]), ALU.mult, reads=[pnk_, "vsT"], writes=[onk])
            num, numk = SF()
            den, denk = SF()
            for a in range(2):
                rows = slice(a * 64, (a + 1) * 64)
                pov = po[rows, a * 128:(a + 1) * 128].rearrange("p (s c) -> p c s", c=8)
                dnv = po[rows, 256 + a * 128:256 + (a + 1) * 128].rearrange("p (s c) -> p c s", c=8)
                n3 = num[rows, :].rearrange("p (c s) -> p c s", c=8)
                d3 = den[rows, :].rearrange("p (c s) -> p c s", c=8)
                sop("dve", "tensor_tensor", n3, pov, on_[rows, :].rearrange("p (c s) -> p c s", c=8), ALU.add, reads=[pok, onk], writes=[numk])
                sop("dve", "tensor_tensor", d3, dnv, pn_[rows, :].rearrange("p (c s) -> p c s", c=8), ALU.add, reads=[pok, pnk_], writes=[denk])
                sop("dve", "tensor_tensor", d3, d3, esink[rows, j, :].unsqueeze(2).to_broadcast([64, 8, 16]), ALU.add, reads=[denk, "esink"], writes=[denk])
                sop("dve", "reciprocal", den[rows, :], den[rows, :], reads=[denk], writes=[denk])
                sop("dve", "tensor_tensor", mixsT[rows, :, :], n3, d3, ALU.mult, reads=[numk, denk], writes=["mixsT"])
            s_proj_out(wo, wok, mixsT, "mixsT", modT[:, 16:24, 1:17], "modT")

        def sample_phase():
            sop("dve", "memset", smp, 0.0, writes=["xsT", "hsT", "mixsT", "qsT", "qsTb", "ksT", "vsT", "km", "ktv", "ckT", "cvt", "kvnew",
                                                    ("pTs", 0), ("pTs", 1), "tab16"] + [("sf", i) for i in range(NSF)] + [("sb", i) for i in range(NSB)])
            sdmas("sp", [(tab16, cs16[:, :])], writes=["tab16"], sem_key="tab16")
            xa, xak = FT()
            xb_, xbk = FT()
            S.dmas("sp", [(xa[0:16, :], xs[:, 0:512]), (xb_[0:16, :], xs[:, 512:1024])], writes=[xak, xbk], sem_key=xak)
            pb, pk = PS()
            for k in range(8):
                src = (xa if k < 4 else xb_)[0:16, (k % 4) * 128:(k % 4 + 1) * 128]
                sop("pe", "transpose", pb[:, k * 16:(k + 1) * 16], src, ident[0:16, 0:16], reads=[xak, xbk, "cst"], writes=[pk])
            sop("dve", "tensor_copy", xsT.rearrange("p k s -> p (k s)"), pb[:, 0:128], reads=[pk], writes=["xsT"])
            for l in range(4):
                ada(w_ada[l], b_ada[l], 12, modT, "modT")
                mod_derive(l)
                S.op("dve", "tensor_copy", modP[:, l, :], modT[:, :, 0], reads=["modT"], writes=[("modP", l)])
                S.op("dve", "tensor_copy", aP[:, l, :, :], aT[:, 0:2, :, 0], reads=["aT0", "aT1"], writes=[("aP", l)])
                if l == 2:
                    s_kv_phase()
                if l < 2:
                    s_hgrn_layer(l)
                else:
                    s_attn_layer(l)
                s_mlp(l)
            pb, pk = PS()
            pb2, pk2 = PS()
            for k in range(8):
                dstp = pb if k < 4 else pb2
                dk_ = pk if k < 4 else pk2
                sop("pe", "transpose", dstp[0:16, (k % 4) * 128:(k % 4 + 1) * 128], xsT[:, k, :], ident, reads=["xsT", "cst"], writes=[dk_])
            ya, yak = FT()
            yb, ybk = FT()
            sop("dve", "tensor_copy", ya[0:16, :], pb[0:16, :], reads=[pk], writes=[yak])
            sop("dve", "tensor_copy", yb[0:16, :], pb2[0:16, :], reads=[pk2], writes=[ybk])
            S.dmas("sp", [(ys[:, 0:512], ya[0:16, :]), (ys[:, 512:1024], yb[0:16, :])], reads=[yak, ybk], sem_key=yak, final=True)

        def final_out():
            for b in range(NBLK):
                t = b // 4
                for g in range(2):
                    yo, yk = FT()
                    pb, pk = PS()
                    for kk in range(4):
                        k = g * 4 + kk
                        S.op("pe", "transpose", pb[:, kk * 128:(kk + 1) * 128], xT[:, k, b * 128:(b + 1) * 128], ident, reads=[("xT", k, t), "cst"], writes=[pk])
                    if g == 0:
                        S.op("dve", "tensor_copy", yo[:], pb[:], reads=[pk], writes=[yk])
                    else:
                        S.op("act", "activation", yo[:], pb[:], AF.Copy, reads=[pk], writes=[yk])
                    S.dmas("sp", [(yp[b * 128:(b + 1) * 128, g * 512:(g + 1) * 512], yo[:])], reads=[yk], sem_key=yk, final=True)

        import os
        STG = os.environ.get("KSTAGE", "full")
        if do_sample:
            sample_phase()
        load_x()
        for l in range(4):
            if STG == "io":
                break
            if do_sample:
                S.op("dve", "tensor_copy", modT[:, :, 0], modP[:, l, :], reads=[("modP", l)], writes=["modT"])
                S.op("dve", "tensor_copy", aT[:, 0:2, :, 0], aP[:, l, :, :], reads=[("aP", l)], writes=["aT0", "aT1"])
            else:
                ada(w_ada[l], b_ada[l], 12, modT, "modT")
                mod_derive(l)
            if STG == "ada":
                break
            if l == 2:
                kv_phase()
            if l < 2:
                if STG == "mlp0":
                    pass
                elif STG == "passA":
                    S.op("dve", "memset", Sst, 0.0, writes=SK)
                    hgrn_pass(l, True)
                    S.dmas("sp", [(sp_state[l].rearrange("h k v -> k h v"), Sst)], reads=SK, sem_key=("spst", l), final=True)
                    break
                else:
                    hgrn_layer(l)
            else:
                attn_layer(l)
            if STG == "hgrn0":
                break
            mlp(l)
            if STG in ("l0", "mlp0"):
                break
            if STG == "l1" and l == 1:
                break
            if STG == "l2" and l == 2:
                break
        final_out()

        S.emit()
        print("ops", len(S.ops), "sem counts", S.max_counts, "dma sems", len(S.dma_count), "sbuf left", nc.sbuf_bytes_remaining)
    return nc


_CACHE = {}


def make_in_maps(inputs, T, seq):
    f = lambda a: np.ascontiguousarray(np.asarray(a, dtype=np.float32))
    xp_all = f(inputs["x_prompt"])
    xs_all = f(inputs["x_sample"]).reshape(NSAMP, D)
    cp = f(inputs["c_prompt"])
    cs = f(inputs["c_sample"])
    st_all = f(inputs["state_hgrn"])
    ck_all = f(inputs["cache_k"]).reshape(NSAMP, 128, 256)
    cv_all = f(inputs["cache_v"]).reshape(NSAMP, 128, 256)
    shared = {k: f(inputs[k]) for k in ("w_ada", "b_ada", "norm1_g", "norm2_g", "hg_w_in", "hg_w_out", "hg_lower_bounds",
                                        "hg_gn_g", "kv_w_ada", "kv_b_ada", "kv_norm_g", "w_kv", "k_norm_g", "w_q",
                                        "q_norm_g", "sinks", "w_o", "w_up", "w_down")}
    shared["hg_w_in"] = np.ascontiguousarray(shared["hg_w_in"].reshape(2, D, 4, 8, 128).transpose(0, 3, 1, 2, 4).reshape(2, 8, D, 512))
    shared["w_q"] = np.ascontiguousarray(shared["w_q"].reshape(2, D, 2, 2, 4, 64).transpose(0, 1, 2, 4, 3, 5).reshape(2, D, D))
    shared["w_o"] = np.ascontiguousarray(shared["w_o"].reshape(2, 2, 2, 4, 64, D).transpose(0, 1, 3, 2, 4, 5).reshape(2, D, D))
    ct16, st16 = rope_tables(np.full((16,), PAST, np.int64))
    cs16 = np.ascontiguousarray(np.concatenate([ct16, st16], axis=1))
    maps = []
    for c in range(8):
        b, half = c // 2, c % 2
        m = dict(shared)
        m["xp"] = np.ascontiguousarray(xp_all[b, half * T:(half + 1) * T])
        sl = slice(c * NS, (c + 1) * NS)
        m["c17"] = np.ascontiguousarray(np.concatenate([cp[b:b + 1], cs[sl]], axis=0))
        m["xs"] = np.ascontiguousarray(xs_all[sl])
        m["st_in"] = np.ascontiguousarray(st_all[:, sl])
        m["ck"] = np.ascontiguousarray(ck_all[sl])
        m["cv"] = np.ascontiguousarray(cv_all[sl])
        m["consts"] = host_consts(float(half))
        ct, stb = rope_tables(np.arange(half * T, (half + 1) * T))
        m["costab"] = ct
        m["sintab"] = stb
        m["cs16"] = cs16
        maps.append(m)
    return maps


def run(inputs, T, dbg=False):
    if T not in _CACHE:
        _CACHE[T] = build(T, dbg=dbg)
    nc = _CACHE[T]
    maps = make_in_maps(inputs, T, 2 * T)
    res = run_bass_kernel_spmd(nc, maps, core_ids=list(range(8)))
    R = res.results
    global LAST_RESULTS
    LAST_RESULTS = R
    seq = 2 * T
    y_prompt = np.zeros((NB, seq, D), np.float32)
    hg_p = np.zeros((2, NB, 8, 128, 128), np.float32)
    k_p = np.zeros((NB, 128, 4, 64), np.float32)
    v_p = np.zeros((NB, 128, 4, 64), np.float32)
    y_sample = np.zeros((NSAMP, 1, D), np.float32)
    hg_s = np.zeros((2, NSAMP, 8, 128, 128), np.float32)
    k_s = np.zeros((NSAMP, 128, 4, 64), np.float32)
    v_s = np.zeros((NSAMP, 128, 4, 64), np.float32)
    for c in range(8):
        b, half = c // 2, c % 2
        r = R[c]
        y_prompt[b, half * T:(half + 1) * T] = r["yp"]
        if half == 1:
            hg_p[:, b] = r["sp_state"]
            k_p[b] = r["kp"].reshape(128, 4, 64)
            v_p[b] = r["vp"].reshape(128, 4, 64)
        sl = slice(c * NS, (c + 1) * NS)
        y_sample[sl, 0] = r["ys"]
        hg_s[:, sl] = r["ss"]
        k_s[sl] = r["ks"].reshape(NS, 128, 4, 64)
        v_s[sl] = r["vs"].reshape(NS, 128, 4, 64)
    return (y_prompt, y_sample, hg_p, k_p, v_p, hg_s, k_s, v_s)


def kernel(**inputs):
    return run(inputs, SEQ // 2)
```
